# Optimizing a Trainium2 kernel written in Bass

```python
import math
import jax, jax.numpy as jnp
from jax import lax
import numpy as np

D_MODEL = 2048
BATCH = 2
SEQ = 4096
DEPTH = 2

MEM_LEN = 256
N_HEADS = 12
MIX_WIDTH = 3 * D_MODEL // 4
HEAD_DIM = MIX_WIDTH // N_HEADS
DIFF_QK_DIM = HEAD_DIM // 2
N_KV_HEADS = 4
GQA_GROUP = N_HEADS // N_KV_HEADS
KV_WIDTH = N_KV_HEADS * HEAD_DIM
WINDOW = 128
BLOCK = 128
N_MEM_HEADS = 4
MEM_WIDTH = D_MODEL - MIX_WIDTH
MEM_HEAD_DIM = MEM_WIDTH // N_MEM_HEADS
D_FF = 4 * D_MODEL
N_BUCKETS = 32
MAX_DISTANCE = 128
RMS_EPS = 1e-6
NEG_INF = -1e30
N_LAYERS_A = (DEPTH + 1) // 2
N_LAYERS_B = DEPTH // 2
IN_A = 3 * MIX_WIDTH + MEM_WIDTH
IN_B = MIX_WIDTH + 2 * KV_WIDTH + MEM_WIDTH

kernel_name = "hybrid_diffattn_swa_sink_memxattn_encoder"


def rms_norm(x, gain):
    x32 = x.astype(jnp.float32)
    y = x32 * lax.rsqrt(jnp.mean(x32 * x32, axis=-1, keepdims=True) + RMS_EPS)
    return (y * gain.astype(jnp.float32)).astype(x.dtype)


def t5_bucket(rel):
    half = N_BUCKETS // 2
    max_exact = half // 2
    side = jnp.where(rel > 0, half, 0)
    n = jnp.abs(rel)
    n_f = jnp.maximum(n, 1).astype(jnp.float32)
    large = max_exact + (jnp.log(n_f / max_exact) / math.log(MAX_DISTANCE / max_exact)
                         * (half - max_exact)).astype(jnp.int32)
    large = jnp.minimum(large, half - 1)
    return side + jnp.where(n < max_exact, n, large)


def rel_bias_lookup(rel_bias, rel):
    return jnp.take(rel_bias, t5_bucket(rel), axis=0)


def diff_attention(q, k, v, positions, rel_bias, lam, scale):
    B, S = v.shape[:2]
    nb = S // BLOCK
    q_blocks = q.reshape(B, nb, BLOCK, N_HEADS, 2, DIFF_QK_DIM).transpose(1, 0, 2, 3, 4, 5)
    pos_blocks = positions.reshape(B, nb, BLOCK).transpose(1, 0, 2)

    def one_block(args):
        qb, pb = args
        s = jnp.einsum('bqhcd,bkhcd->bhcqk', qb, k).astype(jnp.float32) * scale
        bias = rel_bias_lookup(rel_bias, positions[:, None, :] - pb[:, :, None])
        s = s + jnp.transpose(bias, (0, 3, 1, 2))[:, :, None].astype(jnp.float32)
        p = jax.nn.softmax(s, axis=-1)
        a = p[:, :, 0] - lam * p[:, :, 1]
        return jnp.einsum('bhqk,bkhe->bqhe', a.astype(v.dtype), v)

    out = lax.map(one_block, (q_blocks, pos_blocks))
    return out.transpose(1, 0, 2, 3, 4).reshape(B, S, N_HEADS, HEAD_DIM)


def windowed_gqa_sink(q, k, v, positions, rel_bias, sink, scale):
    B, S = q.shape[:2]
    nb = S // BLOCK

    def neighbourhood(t):
        pad = [(0, 0), (BLOCK, BLOCK)] + [(0, 0)] * (t.ndim - 2)
        tp = jnp.pad(t, pad).reshape((B, nb + 2, BLOCK) + t.shape[2:])
        return jnp.concatenate([tp[:, :-2], tp[:, 1:-1], tp[:, 2:]], axis=2)

    kn, vn, pn = neighbourhood(k), neighbourhood(v), neighbourhood(positions)
    qb = q.reshape(B, nb, BLOCK, N_KV_HEADS, GQA_GROUP, HEAD_DIM)
    s = jnp.einsum('bnqkgd,bnckd->bnkgqc', qb, kn).astype(jnp.float32) * scale
    pq = positions.reshape(B, nb, BLOCK)
    bias = rel_bias_lookup(rel_bias, pn[:, :, None, :] - pq[:, :, :, None])
    bias = bias.reshape(B, nb, BLOCK, 3 * BLOCK, N_KV_HEADS, GQA_GROUP).transpose(0, 1, 4, 5, 2, 3)
    qi = jnp.arange(S).reshape(nb, BLOCK)
    ki = (jnp.arange(nb)[:, None] - 1) * BLOCK + jnp.arange(3 * BLOCK)[None, :]
    valid = ((jnp.abs(ki[:, None, :] - qi[:, :, None]) <= WINDOW)
             & (ki[:, None, :] >= 0) & (ki[:, None, :] < S))
    s = jnp.where(valid[None, :, None, None], s + bias.astype(jnp.float32), NEG_INF)
    sink_col = jnp.broadcast_to(
        sink.astype(jnp.float32).reshape(1, 1, N_KV_HEADS, GQA_GROUP, 1, 1), s.shape[:-1] + (1,))
    p = jax.nn.softmax(jnp.concatenate([s, sink_col], axis=-1), axis=-1)[..., :-1]
    out = jnp.einsum('bnkgqc,bnckd->bnqkgd', p.astype(v.dtype), vn)
    return out.reshape(B, S, N_HEADS, HEAD_DIM)


def memory_attention(q_m, k_m, v_m, scale):
    s = jnp.einsum('bshd,bmhd->bhsm', q_m, k_m).astype(jnp.float32) * scale
    p = jax.nn.softmax(s, axis=-1)
    return jnp.einsum('bhsm,bmhd->bshd', p.astype(v_m.dtype), v_m)


def setup_inputs(seed: int = 0) -> dict:
    key = jax.random.key(seed)
    ks = jax.random.split(key, 26)
    f32 = jnp.float32

    def w(k, shape, fan_in):
        return jax.random.normal(k, shape, f32) * fan_in ** -0.5

    def gain(k, shape):
        return 1.0 + 0.05 * jax.random.normal(k, shape, f32)

    x = jax.random.normal(ks[0], (BATCH, SEQ, D_MODEL), f32)
    mem = jax.random.normal(ks[1], (BATCH, MEM_LEN, D_MODEL), f32)
    offset = jax.random.randint(ks[2], (BATCH, 1), 0, 1024, dtype=jnp.int32)
    positions = offset + jnp.arange(SEQ, dtype=jnp.int32)[None, :]
    return {
        "x": x,
        "mem": mem,
        "positions": positions,
        "rel_bias": 0.5 * jax.random.normal(ks[3], (N_BUCKETS, N_HEADS), f32),
        "norm_attn": gain(ks[4], (DEPTH, D_MODEL)),
        "norm_mem": gain(ks[5], (DEPTH, D_MODEL)),
        "norm_mlp": gain(ks[6], (DEPTH, D_MODEL)),
        "w_in_a": w(ks[7], (N_LAYERS_A, D_MODEL, IN_A), D_MODEL),
        "a_q_norm": gain(ks[8], (N_LAYERS_A, DIFF_QK_DIM)),
        "a_k_norm": gain(ks[9], (N_LAYERS_A, DIFF_QK_DIM)),
        "a_lambda_q1": 0.1 * jax.random.normal(ks[10], (N_LAYERS_A, DIFF_QK_DIM), f32),
        "a_lambda_k1": 0.1 * jax.random.normal(ks[11], (N_LAYERS_A, DIFF_QK_DIM), f32),
        "a_lambda_q2": 0.1 * jax.random.normal(ks[12], (N_LAYERS_A, DIFF_QK_DIM), f32),
        "a_lambda_k2": 0.1 * jax.random.normal(ks[13], (N_LAYERS_A, DIFF_QK_DIM), f32),
        "a_subln": gain(ks[14], (N_LAYERS_A, HEAD_DIM)),
        "w_in_b": w(ks[15], (N_LAYERS_B, D_MODEL, IN_B), D_MODEL),
        "b_q_norm": gain(ks[16], (N_LAYERS_B, HEAD_DIM)),
        "b_k_norm": gain(ks[17], (N_LAYERS_B, HEAD_DIM)),
        "b_sink": 0.5 * jax.random.normal(ks[18], (N_LAYERS_B, N_HEADS), f32),
        "w_mem_kv": w(ks[19], (DEPTH, D_MODEL, 2 * MEM_WIDTH), D_MODEL),
        "m_q_norm": gain(ks[20], (DEPTH, MEM_HEAD_DIM)),
        "m_k_norm": gain(ks[21], (DEPTH, MEM_HEAD_DIM)),
        "w_out": w(ks[22], (DEPTH, MIX_WIDTH + MEM_WIDTH, D_MODEL), MIX_WIDTH + MEM_WIDTH),
        "w_up": w(ks[23], (DEPTH, D_MODEL, D_FF), D_MODEL),
        "w_down": w(ks[24], (DEPTH, D_FF, D_MODEL), D_FF),
    }


def reference(x, mem, positions, rel_bias, norm_attn, norm_mem, norm_mlp, w_in_a, a_q_norm, a_k_norm,
              a_lambda_q1, a_lambda_k1, a_lambda_q2, a_lambda_k2, a_subln, w_in_b, b_q_norm, b_k_norm,
              b_sink, w_mem_kv, m_q_norm, m_k_norm, w_out, w_up, w_down):
    B, S, _ = x.shape
    for i in range(DEPTH):
        h = rms_norm(x, norm_attn[i])
        mn = rms_norm(mem, norm_mem[i])
        mkv = (mn @ w_mem_kv[i]).reshape(B, MEM_LEN, 2, N_MEM_HEADS, MEM_HEAD_DIM)
        k_m = rms_norm(mkv[:, :, 0], m_k_norm[i])
        v_m = mkv[:, :, 1]
        if i % 2 == 0:
            j = i // 2
            proj = h @ w_in_a[j]
            q, k, v, q_m = jnp.split(proj, [MIX_WIDTH, 2 * MIX_WIDTH, 3 * MIX_WIDTH], axis=-1)
            q = rms_norm(q.reshape(B, S, N_HEADS, 2, DIFF_QK_DIM), a_q_norm[j])
            k = rms_norm(k.reshape(B, S, N_HEADS, 2, DIFF_QK_DIM), a_k_norm[j])
            v = v.reshape(B, S, N_HEADS, HEAD_DIM)
            lam_init = 0.8 - 0.6 * math.exp(-0.3 * i)
            lam = (jnp.exp(jnp.sum(a_lambda_q1[j].astype(jnp.float32) * a_lambda_k1[j].astype(jnp.float32)))
                   - jnp.exp(jnp.sum(a_lambda_q2[j].astype(jnp.float32) * a_lambda_k2[j].astype(jnp.float32)))
                   + lam_init)
            o = diff_attention(q, k, v, positions, rel_bias, lam, DIFF_QK_DIM ** -0.5)
            o = rms_norm(o, a_subln[j]) * (1.0 - lam_init)
        else:
            j = i // 2
            proj = h @ w_in_b[j]
            q, k, v, q_m = jnp.split(proj, [MIX_WIDTH, MIX_WIDTH + KV_WIDTH, MIX_WIDTH + 2 * KV_WIDTH], axis=-1)
            q = rms_norm(q.reshape(B, S, N_HEADS, HEAD_DIM), b_q_norm[j])
            k = rms_norm(k.reshape(B, S, N_KV_HEADS, HEAD_DIM), b_k_norm[j])
            v = v.reshape(B, S, N_KV_HEADS, HEAD_DIM)
            o = windowed_gqa_sink(q, k, v, positions, rel_bias, b_sink[j], HEAD_DIM ** -0.5)
        q_m = rms_norm(q_m.reshape(B, S, N_MEM_HEADS, MEM_HEAD_DIM), m_q_norm[i])
        o_m = memory_attention(q_m, k_m, v_m, MEM_HEAD_DIM ** -0.5)
        heads = jnp.concatenate([o.reshape(B, S, MIX_WIDTH), o_m.reshape(B, S, MEM_WIDTH)], axis=-1)
        x = x + heads @ w_out[i]
        h = rms_norm(x, norm_mlp[i])
        x = x + jnp.square(jax.nn.relu(h @ w_up[i])) @ w_down[i]
    return x
```

```python
import numpy as np
import concourse.bass as bass
import concourse.mybir as mybir
from concourse.bass_utils import run_bass_kernel_spmd

F32 = mybir.dt.float32
BF16 = mybir.dt.bfloat16
I32 = mybir.dt.int32
AF = mybir.ActivationFunctionType
ALU = mybir.AluOpType
AX = mybir.AxisListType


class Buf:
    __slots__ = ("name", "w", "r")

    def __init__(self, name):
        self.name = name
        self.w = None
        self.r = {}


class Chan:
    __slots__ = ("sem", "key", "total")

    def __init__(self, sem, key):
        self.sem = sem
        self.key = key
        self.total = 0


class Prog:
    ENGS = ("pe", "act", "dve", "pool", "sp")
    NCH = 12

    def __init__(self, nc):
        self.nc = nc
        self.e = {"pe": nc.tensor, "act": nc.scalar, "dve": nc.vector, "pool": nc.gpsimd, "sp": nc.sync}
        self.semobj = {}
        self.cnt = {}
        self.seen = {k: {} for k in self.ENGS}
        for k in self.ENGS:
            self.semobj[k] = nc.alloc_semaphore("s_" + k)
            self.cnt[k] = 0
        self.chans = {}
        self.rr = {}
        for q in ("sp", "pool", "act"):
            n = self.NCH if q != "act" else 4
            self.chans[q] = []
            for i in range(n):
                key = "c_%s%d" % (q, i)
                self.semobj[key] = nc.alloc_semaphore(key)
                self.chans[q].append(Chan(self.semobj[key], key))
            self.rr[q] = 0
        self.nwaits = 0
        self.nins = {k: 0 for k in self.ENGS}

    def buf(self, name):
        return Buf(name)

    def bufs(self, name, n):
        return [Buf("%s%d" % (name, i)) for i in range(n)]

    def _wait(self, eng, tok):
        key, val = tok
        if key == "pe" and eng == "pe":
            return
        if key == "pe":
            assert val <= self.cnt["pe"], "waiting on a pending (un-incremented) PE instruction"
        if self.seen[eng].get(key, 0) >= val:
            return
        self.e[eng].wait_ge(self.semobj[key], val)
        self.seen[eng][key] = val
        self.nwaits += 1

    def _deps(self, eng, reads, writes):
        toks = {}
        for b in reads:
            if b.w is not None:
                k, v = b.w
                toks[k] = max(toks.get(k, 0), v)
        for b in writes:
            if b.w is not None:
                k, v = b.w
                toks[k] = max(toks.get(k, 0), v)
            for k, v in b.r.items():
                toks[k] = max(toks.get(k, 0), v)
        for k, v in toks.items():
            self._wait(eng, (k, v))

    def _mark(self, tok, reads, writes):
        k, v = tok
        for b in reads:
            b.r[k] = max(b.r.get(k, 0), v)
        for b in writes:
            b.w = tok
            b.r = {}

    def op(self, eng, fn, reads=(), writes=(), inc=True, self_sync=True):
        if not self_sync:
            saved = self.seen[eng].get(eng, 0)
            self.seen[eng][eng] = 1 << 60
            self._deps(eng, reads, writes)
            self.seen[eng][eng] = saved
        else:
            self._deps(eng, reads, writes)
        ins = fn(self.e[eng])
        self.nins[eng] += 1
        if inc:
            self.cnt[eng] += 1
            ins.then_inc(self.semobj[eng], 1)
            tok = (eng, self.cnt[eng])
        else:
            assert eng == "pe"
            tok = (eng, self.cnt[eng] + 1)
        self._mark(tok, reads, writes)
        return ins

    def dma(self, q, out, in_, reads=(), writes=(), **kw):
        chs = self.chans[q]
        ch = chs[self.rr[q] % len(chs)]
        self.rr[q] += 1
        if ch.total > 0:
            self._wait(q, (ch.key, ch.total))
        self._deps(q, reads, writes)
        ins = self.e[q].dma_start(out=out, in_=in_, **kw)
        self.nins[q] += 1
        ch.total += 16
        ins.then_inc(ch.sem, 16)
        tok = (ch.key, ch.total)
        self._mark(tok, reads, writes)
        return tok

    def finish(self, out_bufs):
        for b in out_bufs:
            if b.w is not None:
                self._wait("sp", b.w)
        for k in self.ENGS:
            if k != "sp" and self.cnt[k] > 0:
                self._wait("sp", (k, self.cnt[k]))
        for q in self.chans:
            for ch in self.chans[q]:
                if ch.total > 0:
                    self._wait("sp", (ch.key, ch.total))


DM = 2048
KCH = 16
NT = 10
NTOK = NT * 128
NKC = 34
DFF = 8192
GFF = 256
EPS = 1e-6
NEG = -30000.0
S0 = 0.125
S1 = 128.0 ** -0.5
TOKBLKS = [(0, 512), (512, 512), (1024, 256)]


def build_program(stop_after=None, debug=()):
    nc = bass.Bass("TRN2", target_bir_lowering=False)
    p = Prog(nc)
    _uid = [0]

    def _un(name):
        _uid[0] += 1
        return "sb%d_%s" % (_uid[0], name)

    def A(name, shape, dt):
        return nc.alloc_sbuf_tensor(_un(name), shape, dt)

    def SB(name, shape, dt):
        return nc.sbuf_tensor(_un(name), shape, dt)

    def din(name, shape, dt=F32):
        return nc.dram_tensor(name, list(shape), dt, kind="ExternalInput").ap()

    xfull = din("xfull", [NKC * 128, DM])
    xown = din("xown", [NTOK, DM])
    memd = din("mem", [256, DM])
    w_in_a = din("w_in_a", [DM, 5120])
    w_in_b = din("w_in_b", [DM, 3072])
    w_mem_kv = din("w_mem_kv", [2, DM, 1024])
    w_out = din("w_out", [2, DM, DM])
    w_up = din("w_up", [2, DM, DFF])
    w_down = din("w_down", [2, DFF, DM])
    gcols_d = din("gcols", [128, 6, 16])
    hg_d = din("hg", [128, 9])
    lamv_d = din("lamv", [128, 4, 64])
    cst0_d = din("cst0", [128, 12, 3, NKC])
    edge_d = din("edge", [128, 2])
    sink_d = din("sinkb", [128, 12])
    relb_d = din("relb", [33, 12])
    ppos_d = din("ppos", [33, 4, 1280], I32)
    bkt_d = din("bkt", [33, 4])
    cmat_d = din("cmat", [128, 5, 128])
    identf_d = din("identf", [128, 128])
    y = nc.dram_tensor("y", [8 * 128, DM], F32, kind="ExternalOutput").ap()

    kvkind = "ExternalOutput" if "KV" in debug else "Internal"
    KT_d = nc.dram_tensor("KT_d", [12, 128, NKC * 128], BF16, kind=kvkind).ap()
    V_d = nc.dram_tensor("V_d", [12, 128, NKC, 128], BF16, kind=kvkind).ap()
    F0_d = nc.dram_tensor("F0_d", [12, 1280], BF16)
    F1_d = nc.dram_tensor("F1_d", [12, 512], BF16)
    KT_db = p.bufs("KT_d", 12)
    V_db = p.buf("V_d")
    F0_db = p.buf("F0_d")
    F1_db = p.buf("F1_d")
    yb = p.buf("y")
    dbg_out = {}

    def dbg(name, shape, dt=F32):
        t = nc.dram_tensor("dbg_" + name, list(shape), dt, kind="ExternalOutput").ap()
        dbg_out[name] = t
        return t

    gcols = A("gcols", [128, 6, 16], F32); gcolsb = p.buf("gcols")
    hg = A("hg", [128, 9], F32); hgb = p.buf("hg")
    cm = A("cm", [128, 5, 128], BF16); cmb = p.buf("cm")
    identf = A("identf", [128, 128], F32); identfb = p.buf("identf")
    sc = A("sc", [128, 16], F32); scb = p.buf("sc")
    edge = A("edge", [128, 2], F32); edgeb = p.buf("edge")
    esink = A("esink", [128, 12], F32); esinkb = p.buf("esink")
    ssq = A("ssq", [128, 8], F32); ssqb = p.bufs("ssq", 4)
    junk = A("junk", [128, DM], BF16); junkb = p.buf("junk")
    IDB, JM, OD64, O128, ONES = 0, 1, 2, 3, 4

    psd = [nc.alloc_psum_tensor("psd%d" % i, [128, 2, 512], F32) for i in range(4)]
    ps = [psd[i // 2][:, i % 2, :] for i in range(8)]
    psb = p.bufs("ps", 8)
    onesf = A("onesf", [128, 128], F32); onesfb = p.buf("onesf")
    p.op("dve", lambda e: e.memset(onesf[:], 1.0), writes=[onesfb])

    p.dma("sp", gcols[:], gcols_d, writes=[gcolsb])
    p.dma("sp", hg[:], hg_d, writes=[hgb])
    p.dma("sp", identf[:], identf_d, writes=[identfb])
    p.dma("sp", edge[:], edge_d, writes=[edgeb])
    p.dma("sp", esink[:], sink_d, writes=[esinkb])
    p.dma("pool", cm[:], cmat_d, writes=[cmb])
    p.op("act", lambda e: e.activation(out=esink[:], in_=esink[:], func=AF.Exp), reads=[esinkb], writes=[esinkb])

    state = {"ss": 0, "tp": 0}

    def barrier():
        for e in Prog.ENGS:
            for k in Prog.ENGS:
                if k != e and p.cnt[k] > 0:
                    p._wait(e, (k, p.cnt[k]))
            for q in p.chans:
                for ch in p.chans[q]:
                    if ch.total > 0:
                        p._wait(e, (ch.key, ch.total))

    def norm_tile(x_ap, x_buf, gi, dst, dst_buf, col0, xn, xnb, tpbanks):
        s = state["ss"] % 4
        state["ss"] += 1
        ssa = ssq[:, 2 * s:2 * s + 1]
        rsa = ssq[:, 2 * s + 1:2 * s + 2]
        p.op("act", lambda e: e.activation(out=junk[:], in_=x_ap, func=AF.Square, accum_out=ssa),
             reads=[x_buf], writes=[junkb, ssqb[s]])
        p.op("act", lambda e: e.activation(out=rsa, in_=ssa, func=AF.Ln, scale=1.0 / DM, bias=EPS),
             reads=[ssqb[s]], writes=[ssqb[s]])
        p.op("act", lambda e: e.activation(out=rsa, in_=rsa, func=AF.Exp, scale=-0.5),
             reads=[ssqb[s]], writes=[ssqb[s]])
        p.op("dve", lambda e: e.tensor_scalar(out=xn[:], in0=x_ap, scalar1=rsa, scalar2=None, op0=ALU.mult),
             reads=[x_buf, ssqb[s]], writes=[xnb])
        for kb in range(4):
            bi = tpbanks[state["tp"] % len(tpbanks)]
            state["tp"] += 1
            for j in range(4):
                k = kb * 4 + j
                p.op("pe", lambda e: e.transpose(ps[bi].bitcast(BF16)[:, j * 128:(j + 1) * 128], xn[:, k * 128:(k + 1) * 128], cm[:, IDB, :]),
                     reads=[xnb, cmb], writes=[psb[bi]], inc=(j == 3))
            p.op("dve", lambda e: e.tensor_tensor(
                out=dst[:, kb * 4:(kb + 1) * 4, col0:col0 + 128],
                in0=ps[bi].bitcast(BF16)[:, 0:512].rearrange("p (a b) -> p a b", a=4),
                in1=gcols[:, gi, kb * 4:(kb + 1) * 4].unsqueeze(2).to_broadcast([128, 4, 128]),
                op=ALU.mult), reads=[psb[bi], gcolsb], writes=[dst_buf])

    def proj_fm(w, wbuf, j, src, src_buf, c0, n, nm, gcol, dst_ap, dst_buf, tmp, banks):
        pj, msb = banks
        for k in range(KCH):
            p.op("pe", lambda e: e.matmul(ps[pj][:, :n], lhsT=w[:, k, j * 128:(j + 1) * 128], rhs=src[:, k, c0:c0 + n],
                                          start=(k == 0), stop=(k == KCH - 1)),
                 reads=[wbuf, src_buf], writes=[psb[pj]], inc=(k == KCH - 1))
        sq, sqb, ln, lnb = tmp
        p.op("act", lambda e: e.activation(out=sq[:, :n], in_=ps[pj][:, :n], func=AF.Square), reads=[psb[pj]], writes=[sqb])
        p.op("pe", lambda e: e.matmul(ps[msb][:, :n], lhsT=cm[:, nm, :], rhs=sq[:, :n], start=True, stop=True),
             reads=[sqb, cmb], writes=[psb[msb]])
        p.op("act", lambda e: e.activation(out=ln[:, :n], in_=ps[msb][:, :n], func=AF.Ln, bias=EPS), reads=[psb[msb]], writes=[lnb])
        p.op("act", lambda e: e.activation(out=ln[:, :n], in_=ln[:, :n], func=AF.Exp, scale=-0.5), reads=[lnb], writes=[lnb])
        p.op("dve", lambda e: e.scalar_tensor_tensor(out=dst_ap, in0=ps[pj][:, :n], scalar=gcol, in1=ln[:, :n],
                                                     op0=ALU.mult, op1=ALU.mult),
             reads=[psb[pj], lnb, hgb], writes=[dst_buf])

    def wpiece(dst, dst_buf, src2d):
        p.dma("pool", dst, src2d.rearrange("(k p) n -> p k n", p=128), writes=[dst_buf])

    with SB("lamv", [128, 4, 64], F32) as lamv, \
            SB("relb", [33, 12], F32) as relb, \
            SB("relb8", [33, 12], F32) as relb8, \
            SB("oh0", [32, 1280], F32) as oh0, \
            SB("oh1", [33, 512], F32) as oh1, \
            SB("fst", [12, 1280], BF16) as fst:
        lamvb, relbb, relb8b, oh0b, oh1b, fstb = [p.buf(n) for n in ("lamv", "relb", "relb8", "oh0", "oh1", "fst")]
        p.dma("sp", lamv[:], lamv_d, writes=[lamvb])
        p.dma("sp", relb[:], relb_d, writes=[relbb])
        with SB("ppos", [33, 4, 1280], I32) as ppos, SB("bkt", [33, 4], F32) as bkt, SB("urel", [33, 1280], F32) as urel:
            pposb, bktb, urelb = p.buf("ppos"), p.buf("bkt"), p.buf("urel")
            p.dma("sp", ppos[:], ppos_d, writes=[pposb])
            p.dma("sp", bkt[:], bkt_d, writes=[bktb])
            for (oh, ohb, kk, n, ia, c0) in ((oh0, oh0b, 32, 1280, 0, 0), (oh1, oh1b, 33, 512, 2, 2)):
                p.op("dve", lambda e: e.tensor_tensor(out=urel[:kk, :n], in0=ppos[:kk, ia, :n], in1=ppos[:kk, ia + 1, :n], op=ALU.subtract),
                     reads=[pposb], writes=[urelb])
                p.op("dve", lambda e: e.tensor_scalar(out=oh[:kk, :n], in0=urel[:kk, :n], scalar1=bkt[:kk, c0:c0 + 1], scalar2=None, op0=ALU.is_ge),
                     reads=[urelb, bktb], writes=[ohb])
                p.op("dve", lambda e: e.scalar_tensor_tensor(out=oh[:kk, :n], in0=urel[:kk, :n], scalar=bkt[:kk, c0 + 1:c0 + 2], in1=oh[:kk, :n],
                                                             op0=ALU.is_lt, op1=ALU.mult),
                     reads=[urelb, bktb, ohb], writes=[ohb])
            p.op("dve", lambda e: e.tensor_scalar(out=oh1[32:33, :], in0=oh1[32:33, :], scalar1=-1.0, scalar2=1.0, op0=ALU.mult, op1=ALU.add),
                 reads=[oh1b], writes=[oh1b])
            barrier()
        p.op("dve", lambda e: e.tensor_tensor(out=lamv[:, 0, :], in0=lamv[:, 0, :], in1=lamv[:, 1, :], op=ALU.mult),
             reads=[lamvb], writes=[lamvb])
        p.op("dve", lambda e: e.tensor_tensor(out=lamv[:, 2, :], in0=lamv[:, 2, :], in1=lamv[:, 3, :], op=ALU.mult),
             reads=[lamvb], writes=[lamvb])
        p.op("dve", lambda e: e.reduce_sum(out=sc[:, 2:3], in_=lamv[:, 0, :], axis=AX.X), reads=[lamvb], writes=[scb])
        p.op("dve", lambda e: e.reduce_sum(out=sc[:, 3:4], in_=lamv[:, 2, :], axis=AX.X), reads=[lamvb], writes=[scb])
        p.op("act", lambda e: e.activation(out=sc[:, 2:4], in_=sc[:, 2:4], func=AF.Exp), reads=[scb], writes=[scb])
        p.op("dve", lambda e: e.tensor_tensor(out=sc[:, 0:1], in0=sc[:, 3:4], in1=sc[:, 2:3], op=ALU.subtract),
             reads=[scb], writes=[scb])
        p.op("dve", lambda e: e.tensor_scalar(out=sc[:, 0:1], in0=sc[:, 0:1], scalar1=-0.2, scalar2=None, op0=ALU.add),
             reads=[scb], writes=[scb])
        p.op("dve", lambda e: e.tensor_scalar(out=sc[:, 1:2], in0=hg[:, 2:3], scalar1=0.8, scalar2=None, op0=ALU.mult),
             reads=[scb, hgb], writes=[scb])
        for (scale, oh, ohb, n, Fd, Fdb, kk) in ((1.0 / S0, oh0, oh0b, 1280, F0_d, F0_db, 32), (1.0 / S1, oh1, oh1b, 512, F1_d, F1_db, 33)):
            p.op("act", lambda e: e.mul(out=relb8[:], in_=relb[:], mul=scale), reads=[relbb], writes=[relb8b])
            for c0 in range(0, n, 512):
                nn = min(512, n - c0)
                p.op("pe", lambda e: e.matmul(ps[0][:12, :nn], lhsT=relb8[:kk, :], rhs=oh[:kk, c0:c0 + nn], start=True, stop=True),
                     reads=[relb8b, ohb], writes=[psb[0]])
                p.op("act", lambda e: e.copy(out=fst[:, c0:c0 + nn], in_=ps[0][:12, :nn]), reads=[psb[0]], writes=[fstb])
            p.dma("sp", Fd.ap()[:, :n], fst[:, :n], reads=[fstb], writes=[Fdb])
        barrier()
    if "F0" in debug:
        pass

    def phase_a():
        with SB("wk", [128, KCH, 1536], BF16) as wk, \
                SB("wv", [128, KCH, 1536], BF16) as wv, \
                SB("xt", [128, 2, DM], F32) as xt, \
                SB("xn", [128, 2, DM], BF16) as xn_, \
                SB("hTc", [128, 2, KCH, 512], BF16) as hTc, \
                SB("kst", [128, 2, 512], BF16) as kst, \
                SB("vst", [128, 2, 12, 4, 128], BF16) as vst, \
                SB("sq", [128, 2, 512], BF16) as sq, \
                SB("ln", [128, 2, 512], F32) as ln:
            wkb, wvb = p.bufs("wk", 3), p.bufs("wv", 3)
            xtb, xnb, hTcb, kstb, vstb, sqb, lnb = (p.bufs("xt", 2), p.bufs("xn", 2), p.bufs("hTc", 2), p.bufs("kst", 2),
                                                    p.bufs("vst", 2), p.bufs("sq", 2), p.bufs("ln", 2))
            for i in range(3):
                wpiece(wk[:, :, i * 512:(i + 1) * 512], wkb[i], w_in_a[:, 1536 + i * 512:1536 + (i + 1) * 512])
            for i in range(3):
                wpiece(wv[:, :, i * 512:(i + 1) * 512], wvb[i], w_in_a[:, 3072 + i * 512:3072 + (i + 1) * 512])
            blocks = [(t0, min(4, NKC - t0)) for t0 in range(0, NKC, 4)]
            ti = 0
            hk = 0
            for bi_, (t0, nt) in enumerate(blocks):
                hs = bi_ % 2
                n = nt * 128
                for i in range(nt):
                    s = ti % 2
                    p.dma("sp", xt[:, s, :], xfull[(t0 + i) * 128:(t0 + i + 1) * 128, :], writes=[xtb[s]])
                    norm_tile(xt[:, s, :], xtb[s], 0, hTc[:, hs], hTcb[hs], i * 128, xn_[:, ti % 2, :], xnb[ti % 2], (0, 1))
                    ti += 1
                for h in range(12):
                    s2 = hk % 2
                    hk += 1
                    proj_fm(wk, wkb[h // 4], h, hTc[:, hs], hTcb[hs], 0, n, OD64, hg[:, 1:2],
                            kst[:, s2, :n], kstb[s2], (sq[:, s2, :], sqb[s2], ln[:, s2, :], lnb[s2]), (2 + s2, 4 if s2 == 0 else 7))
                    p.dma("pool", KT_d[h, :, t0 * 128:t0 * 128 + n], kst[:, s2, :n], reads=[kstb[s2]], writes=[KT_db[h]])
                for i in range(nt):
                    for nb in range(3):
                        bk = 5 + (i * 3 + nb) % 2
                        for k in range(KCH):
                            p.op("pe", lambda e: e.matmul(ps[bk][:, :], lhsT=hTc[:, hs, k, i * 128:(i + 1) * 128],
                                                          rhs=wv[:, k, nb * 512:(nb + 1) * 512], start=(k == 0), stop=(k == KCH - 1)),
                                 reads=[hTcb[hs], wvb[nb]], writes=[psb[bk]], inc=(k == KCH - 1))
                        p.op("act", lambda e: e.copy(out=vst[:, hs, nb * 4:(nb + 1) * 4, i, :],
                                                     in_=ps[bk][:, :].rearrange("p (a b) -> p a b", a=4)),
                             reads=[psb[bk]], writes=[vstb[hs]])
                p.dma("pool", V_d.rearrange("h p c e -> p h c e")[:, :, t0:t0 + nt, :], vst[:, hs, :, :nt, :],
                      reads=[vstb[hs]], writes=[V_db])
            barrier()

    phase_a()
    if stop_after == "A":
        p.finish([])
        return nc, dbg_out

    hT = A("hT", [128, KCH, NTOK], BF16)
    hTb = p.buf("hT")
    X1_d = nc.dram_tensor("X1_d", [NTOK, DM], F32).ap()
    X1_db = p.buf("X1_d")

    def layer(li):
        xsrc = xown if li == 0 else X1_d
        xsrcb = p.buf("xsrc") if li == 0 else X1_db
        w_in = w_in_a if li == 0 else w_in_b
        with SB("QT%d" % li, [128, 16, NTOK], BF16) as QT, \
                SB("KmT%d" % li, [128, 4, 256], BF16) as KmT, \
                SB("Vm%d" % li, [128, 2, 512], BF16) as Vm, \
                SB("KT1%d" % li, [128, 4, NTOK if li == 1 else 2], BF16) as KT1, \
                SB("V1%d" % li, [128, NT, 512 if li == 1 else 2], BF16) as V1:
            QTb, KmTb, Vmb, KT1b, V1b = p.buf("QT"), p.buf("KmT"), p.buf("Vm"), p.buf("KT1"), p.buf("V1")
            with SB("xt", [128, 2, DM], F32) as xt, \
                    SB("xn", [128, 2, DM], BF16) as xn_, \
                    SB("wq", [128, 2, KCH, 512], BF16) as wq, \
                    SB("mnT", [128, KCH, 256], BF16) as mnT, \
                    SB("sq", [128, 2, 512], BF16) as sq, \
                    SB("ln", [128, 2, 512], F32) as ln:
                xtb, xnb, wqb, sqb, lnb = p.bufs("xt", 2), p.bufs("xn", 2), p.bufs("wq", 2), p.bufs("sq", 2), p.bufs("ln", 2)
                mnTb = p.buf("mnT")
                wi = [0]

                def next_w(src2d):
                    s = wi[0] % 2
                    wi[0] += 1
                    wpiece(wq[:, s], wqb[s], src2d)
                    return wq[:, s], wqb[s]

                pieces = []
                if li == 0:
                    for i in range(3):
                        pieces.append(("q", i, w_in[:, i * 512:(i + 1) * 512]))
                    pieces.append(("qm", 0, w_in[:, 4608:5120]))
                else:
                    for i in range(3):
                        pieces.append(("q", i, w_in[:, i * 512:(i + 1) * 512]))
                    pieces.append(("k", 0, w_in[:, 1536:2048]))
                    pieces.append(("v", 0, w_in[:, 2048:2560]))
                    pieces.append(("qm", 0, w_in[:, 2560:3072]))
                pieces.append(("mk", 0, w_mem_kv[li, :, 0:512]))
                pieces.append(("mv", 0, w_mem_kv[li, :, 512:1024]))
                loaded = [next_w(pieces[0][2]), None]
                for t in range(NT):
                    s = t % 2
                    p.dma("sp", xt[:, s, :], xsrc[t * 128:(t + 1) * 128, :], reads=[xsrcb], writes=[xtb[s]])
                    norm_tile(xt[:, s, :], xtb[s], 0 if li == 0 else 3, hT, hTb, t * 128, xn_[:, s, :], xnb[s], (0, 1))
                for t in range(2):
                    s = t % 2
                    p.dma("sp", xt[:, s, :], memd[t * 128:(t + 1) * 128, :], writes=[xtb[s]])
                    norm_tile(xt[:, s, :], xtb[s], 2 if li == 0 else 5, mnT, mnTb, t * 128, xn_[:, s, :], xnb[s], (0, 1))
                cnt = 0
                for pi, (kind, idx, src2d) in enumerate(pieces):
                    w, wb = loaded[pi % 2]
                    if pi + 1 < len(pieces):
                        loaded[(pi + 1) % 2] = next_w(pieces[pi + 1][2])
                    if kind in ("q", "qm", "k", "mk"):
                        for j in range(4):
                            if kind == "q":
                                fc = idx * 4 + j
                                nm, gc = (OD64, hg[:, 0:1]) if li == 0 else (O128, hg[:, 3:4])
                                dst, dstb, src, srcb, blks = QT, QTb, hT, hTb, TOKBLKS
                            elif kind == "qm":
                                fc = 12 + j
                                nm, gc = O128, (hg[:, 5:6] if li == 0 else hg[:, 7:8])
                                dst, dstb, src, srcb, blks = QT, QTb, hT, hTb, TOKBLKS
                            elif kind == "k":
                                fc = j
                                nm, gc = O128, hg[:, 4:5]
                                dst, dstb, src, srcb, blks = KT1, KT1b, hT, hTb, TOKBLKS
                            else:
                                fc = j
                                nm, gc = O128, (hg[:, 6:7] if li == 0 else hg[:, 8:9])
                                dst, dstb, src, srcb, blks = KmT, KmTb, mnT, mnTb, [(0, 256)]
                            for (c0, n) in blks:
                                s2 = cnt % 2
                                cnt += 1
                                proj_fm(w, wb, j, src, srcb, c0, n, nm, gc, dst[:, fc, c0:c0 + n], dstb,
                                        (sq[:, s2, :], sqb[s2], ln[:, s2, :], lnb[s2]), (2 + s2, 4 if s2 == 0 else 7))
                    else:
                        if kind == "v":
                            src, srcb, ntl, dst, dstb = hT, hTb, NT, V1, V1b
                        else:
                            src, srcb, ntl, dst, dstb = mnT, mnTb, 2, Vm, Vmb
                        for i in range(ntl):
                            bk = 5 + i % 2
                            for k in range(KCH):
                                p.op("pe", lambda e: e.matmul(ps[bk][:, :], lhsT=src[:, k, i * 128:(i + 1) * 128],
                                                              rhs=w[:, k, :], start=(k == 0), stop=(k == KCH - 1)),
                                     reads=[srcb, wb], writes=[psb[bk]], inc=(k == KCH - 1))
                            p.op("act", lambda e: e.copy(out=dst[:, i, :], in_=ps[bk][:, :]), reads=[psb[bk]], writes=[dstb])
                barrier()
            if stop_after == "B%d" % li:
                return ("B", QT, KmT, Vm, KT1, V1)
            with SB("eT", [128, 4, 512], BF16) as eT, \
                    SB("pp", [128, 6, 512], F32) as pp, \
                    SB("sqb", [128, 512], BF16) as sqh:
                eTb = p.bufs("eT", 4)
                ppb = p.bufs("pp", 6)
                sqhb = p.buf("sqh")
                ring = (0, 1, 2)
                OB0, OB1, ZB0, ZB1, MSB = 6, 7, 4, 5, 3
                ecnt = [0]

                def mem_attn():
                    blk = 0
                    for hm in range(4):
                        for (q0, n) in TOKBLKS:
                            ob, zb, pi = OB0 + blk % 2, ZB0 + blk % 2, 2 + blk % 2
                            blk += 1
                            for mc in range(2):
                                i = ecnt[0]
                                ecnt[0] += 1
                                bk = ring[i % 3]
                                es = i % 4
                                p.op("pe", lambda e: e.matmul(ps[bk][:, :n], lhsT=KmT[:, hm, mc * 128:(mc + 1) * 128],
                                                              rhs=QT[:, 12 + hm, q0:q0 + n], start=True, stop=True),
                                     reads=[KmTb, QTb], writes=[psb[bk]])
                                p.op("act", lambda e: e.activation(out=eT[:, es, :n], in_=ps[bk][:, :n], func=AF.Exp, scale=S1),
                                     reads=[psb[bk]], writes=[eTb[es]])
                                p.op("pe", lambda e: e.matmul(ps[ob][:, :n], lhsT=Vm[:, mc, hm * 128:(hm + 1) * 128], rhs=eT[:, es, :n],
                                                              start=(mc == 0), stop=(mc == 1)),
                                     reads=[Vmb, eTb[es]], writes=[psb[ob]], inc=False)
                                p.op("pe", lambda e: e.matmul(ps[zb][:, :n], lhsT=cm[:, ONES, :], rhs=eT[:, es, :n],
                                                              start=(mc == 0), stop=(mc == 1)),
                                     reads=[cmb, eTb[es]], writes=[psb[zb]])
                            p.op("act", lambda e: e.activation(out=pp[:, pi, :n], in_=ps[zb][:, :n], func=AF.Ln), reads=[psb[zb]], writes=[ppb[pi]])
                            p.op("act", lambda e: e.activation(out=pp[:, pi, :n], in_=pp[:, pi, :n], func=AF.Exp, scale=-1.0), reads=[ppb[pi]], writes=[ppb[pi]])
                            p.op("dve", lambda e: e.tensor_tensor(out=hT[:, 12 + hm, q0:q0 + n], in0=ps[ob][:, :n], in1=pp[:, pi, :n], op=ALU.mult),
                                 reads=[psb[ob], ppb[pi]], writes=[hTb])

                if li == 0:
                    with SB("KTh", [128, 2, NKC * 128], BF16) as KTh, \
                            SB("Vh", [128, 2, NKC, 128], BF16) as Vh, \
                            SB("Tb", [128, 2, 6, 512], BF16) as Tb, \
                            SB("cst0", [128, 12, 3, NKC], F32) as cst0, \
                            SB("eT2", [128, 4, 2, 512], BF16) as eT2, \
                            SB("zacc", [128, 2, 2, 512], F32) as zacc:
                        KThb, Vhb, Tbb = p.bufs("KTh", 2), p.bufs("Vh", 2), p.bufs("Tb", 2)
                        cst0b = p.buf("cst0")
                        eT2b = p.bufs("eT2", 4)
                        zaccb = p.bufs("zacc", 2)
                        p.dma("sp", cst0[:], cst0_d, writes=[cst0b])

                        def load_head(h):
                            s = h % 2
                            p.dma("sp", KTh[:, s, :], KT_d[h], reads=[KT_db[h]], writes=[KThb[s]])
                            p.dma("sp", Vh[:, s], V_d[h], reads=[V_db], writes=[Vhb[s]])
                            p.dma("sp", Tb[:, s], bass.AP(F0_d, h * 1280, [[1, 128], [128, 6], [1, 512]]),
                                  reads=[F0_db], writes=[Tbb[s]])

                        load_head(0)
                        kcnt = 0
                        for h in range(12):
                            s = h % 2
                            if h + 1 < 12:
                                load_head(h + 1)
                            for qb, (q0, n) in enumerate(TOKBLKS):
                                pend = []

                                def pv(kc, es):
                                    for c in (0, 1):
                                        p.op("pe", lambda e: e.matmul(ps[OB0 + c][:, :n], lhsT=Vh[:, s, kc, :], rhs=eT2[:, es, c, :n],
                                                                      start=(kc == 0), stop=(kc == NKC - 1)),
                                             reads=[Vhb[s], eT2b[es]], writes=[psb[OB0 + c]], inc=(c == 1))

                                first = {"dve": True, "pool": True}
                                for kc in range(NKC):
                                    r = kcnt % 3
                                    es = kcnt % 4
                                    kcnt += 1
                                    Dd = (kc - 1) * 128 - qb * 512
                                    near = (-128 <= Dd <= (512 if n == 512 else 256))
                                    di = (512 - Dd) // 128
                                    for c in (0, 1):
                                        bk = 2 * r + c
                                        p.op("pe", lambda e: e.matmul(ps[bk][:, :n], lhsT=KTh[64 * c:64 * c + 64, s, kc * 128:(kc + 1) * 128],
                                                                      rhs=QT[64 * c:64 * c + 64, h, q0:q0 + n], start=True, stop=(not near)),
                                             reads=[KThb[s], QTb], writes=[psb[bk]], inc=(c == 1 and not near))
                                    if near:
                                        for c in (0, 1):
                                            bk = 2 * r + c
                                            p.op("pe", lambda e: e.matmul(ps[bk][:, :n], lhsT=cm[:, JM, :], rhs=Tb[:, s, di, :n],
                                                                          start=False, stop=True),
                                                 reads=[cmb, Tbb[s]], writes=[psb[bk]], inc=(c == 1))
                                    p.op("act", lambda e: e.activation(out=eT2[:, es, :, :n], in_=psd[r][:, :, :n], func=AF.Exp,
                                                                       scale=S0, bias=cst0[:, h, qb, kc:kc + 1]),
                                         reads=[psb[2 * r], psb[2 * r + 1], cst0b], writes=[eT2b[es]])
                                    eng = "dve"
                                    zi = 1 if eng == "pool" else 0
                                    if first[eng]:
                                        first[eng] = False
                                        p.op(eng, lambda e: e.tensor_copy(out=zacc[:, zi, :, :n], in_=eT2[:, es, :, :n]),
                                             reads=[eT2b[es]], writes=[zaccb[zi]])
                                    else:
                                        p.op(eng, lambda e: e.tensor_tensor(out=zacc[:, zi, :, :n], in0=zacc[:, zi, :, :n], in1=eT2[:, es, :, :n], op=ALU.add),
                                             reads=[eT2b[es]], writes=[zaccb[zi]], self_sync=False)
                                    pend.append((kc, es))
                                    if len(pend) > 2:
                                        pv(*pend.pop(0))
                                for a in pend:
                                    pv(*a)
                                for c in (0, 1):
                                    p.op("pe", lambda e: e.matmul(ps[c][:, :n], lhsT=onesf[:], rhs=zacc[:, 0, c, :n], start=True, stop=True),
                                         reads=[onesfb, zaccb[0]], writes=[psb[c]])
                                    p.op("act", lambda e: e.activation(out=pp[:, 2 + c, :n], in_=ps[c][:, :n], func=AF.Ln), reads=[psb[c]], writes=[ppb[2 + c]])
                                    p.op("act", lambda e: e.activation(out=pp[:, 2 + c, :n], in_=pp[:, 2 + c, :n], func=AF.Exp, scale=-1.0),
                                         reads=[ppb[2 + c]], writes=[ppb[2 + c]])
                                    p.op("dve", lambda e: e.tensor_tensor(out=pp[:, c, :n], in0=ps[OB0 + c][:, :n], in1=pp[:, 2 + c, :n], op=ALU.mult),
                                         reads=[psb[OB0 + c], ppb[2 + c]], writes=[ppb[c]])
                                p.op("dve", lambda e: e.scalar_tensor_tensor(out=pp[:, 0, :n], in0=pp[:, 1, :n], scalar=sc[:, 0:1], in1=pp[:, 0, :n],
                                                                             op0=ALU.mult, op1=ALU.add),
                                     reads=[ppb[0], ppb[1], scb], writes=[ppb[0]])
                                p.op("dve", lambda e: e.tensor_tensor(out=sqh[:, :n], in0=pp[:, 0, :n], in1=pp[:, 0, :n], op=ALU.mult),
                                     reads=[ppb[0]], writes=[sqhb])
                                p.op("pe", lambda e: e.matmul(ps[2][:, :n], lhsT=cm[:, O128, :], rhs=sqh[:, :n], start=True, stop=True),
                                     reads=[cmb, sqhb], writes=[psb[2]])
                                p.op("act", lambda e: e.activation(out=pp[:, 4, :n], in_=ps[2][:, :n], func=AF.Ln, bias=EPS),
                                     reads=[psb[2]], writes=[ppb[4]])
                                p.op("act", lambda e: e.activation(out=pp[:, 4, :n], in_=pp[:, 4, :n], func=AF.Exp, scale=-0.5),
                                     reads=[ppb[4]], writes=[ppb[4]])
                                p.op("dve", lambda e: e.scalar_tensor_tensor(out=hT[:, h, q0:q0 + n], in0=pp[:, 0, :n], scalar=sc[:, 1:2], in1=pp[:, 4, :n],
                                                                             op0=ALU.mult, op1=ALU.mult),
                                     reads=[ppb[0], ppb[4], scb], writes=[hTb])
                        mem_attn()
                        barrier()
                else:
                    with SB("Tb1", [128, 12, 3, 128], BF16) as Tb1:
                        Tb1b = p.buf("Tb1")
                        for h in range(12):
                            p.dma("sp", Tb1[:, h], bass.AP(F1_d, h * 512, [[1, 128], [128, 3], [1, 128]]), reads=[F1_db], writes=[Tb1b])
                        blk1 = 0
                        for t in range(1, 9):
                            for g in range(4):
                                ob, zb, pi = OB0 + blk1 % 2, ZB0 + blk1 % 2, 2 + blk1 % 2
                                blk1 += 1
                                js = (t - 1, t, t + 1)
                                for jj, j in enumerate(js):
                                    i = ecnt[0]
                                    ecnt[0] += 1
                                    bk = ring[i % 3]
                                    es = i % 4
                                    dj = 1 - (j - t)
                                    o3 = ps[bk][:, :384].rearrange("p (a b) -> p a b", a=3)
                                    p.op("pe", lambda e: e.matmul(o3, lhsT=KT1[:, g, j * 128:(j + 1) * 128],
                                                                  rhs=QT[:, 3 * g:3 * g + 3, t * 128:(t + 1) * 128], start=True, stop=False),
                                         reads=[KT1b, QTb], writes=[psb[bk]], inc=False)
                                    p.op("pe", lambda e: e.matmul(o3, lhsT=cm[:, JM, :], rhs=Tb1[:, 3 * g:3 * g + 3, dj, :], start=False, stop=True),
                                         reads=[cmb, Tb1b], writes=[psb[bk]])
                                    if t == 1 and j == 0:
                                        bias = edge[:, 0:1]
                                    elif t == 8 and j == 9:
                                        bias = edge[:, 1:2]
                                    else:
                                        bias = 0.0
                                    p.op("act", lambda e: e.activation(out=eT[:, es, :384], in_=ps[bk][:, :384], func=AF.Exp, scale=S1, bias=bias),
                                         reads=[psb[bk], edgeb], writes=[eTb[es]])
                                    p.op("pe", lambda e: e.matmul(ps[ob][:, :384], lhsT=V1[:, j, g * 128:(g + 1) * 128], rhs=eT[:, es, :384],
                                                                  start=(jj == 0), stop=(jj == 2)),
                                         reads=[V1b, eTb[es]], writes=[psb[ob]], inc=False)
                                    p.op("pe", lambda e: e.matmul(ps[zb][:, :384], lhsT=cm[:, ONES, :], rhs=eT[:, es, :384],
                                                                  start=(jj == 0), stop=(jj == 2)),
                                         reads=[cmb, eTb[es]], writes=[psb[zb]])
                                p.op("dve", lambda e: e.tensor_tensor(out=pp[:, pi, :384].rearrange("p (a b) -> p a b", a=3),
                                                                      in0=ps[zb][:, :384].rearrange("p (a b) -> p a b", a=3),
                                                                      in1=esink[:, 3 * g:3 * g + 3].unsqueeze(2).to_broadcast([128, 3, 128]), op=ALU.add),
                                     reads=[psb[zb], esinkb], writes=[ppb[pi]])
                                p.op("act", lambda e: e.activation(out=pp[:, pi, :384], in_=pp[:, pi, :384], func=AF.Ln), reads=[ppb[pi]], writes=[ppb[pi]])
                                p.op("act", lambda e: e.activation(out=pp[:, pi, :384], in_=pp[:, pi, :384], func=AF.Exp, scale=-1.0), reads=[ppb[pi]], writes=[ppb[pi]])
                                p.op("dve", lambda e: e.tensor_tensor(out=hT[:, 3 * g:3 * g + 3, t * 128:(t + 1) * 128],
                                                                      in0=ps[ob][:, :384].rearrange("p (a b) -> p a b", a=3),
                                                                      in1=pp[:, pi, :384].rearrange("p (a b) -> p a b", a=3), op=ALU.mult),
                                     reads=[psb[ob], ppb[pi]], writes=[hTb])
                        mem_attn()
                        barrier()
        if stop_after == "D%d" % li:
            d = dbg("hT", [128, KCH, NTOK], BF16)
            p.dma("sp", d, hT[:], reads=[hTb], writes=[p.buf("d")])
            return ("D",)
        t_lo, t_hi = (0, NT) if li == 0 else (1, 9)
        mblks = TOKBLKS if li == 0 else [(128, 512), (640, 512)]
        with SB("x%d" % li, [128, NT, DM], F32) as x:
            xb = p.bufs("x", NT)
            xq = [p.bufs("xq%d_" % t, 4) for t in range(NT)]
            with SB("wo", [128, 2, KCH, 512], BF16) as wo:
                wob = p.bufs("wo", 2)
                wpiece(wo[:, 0], wob[0], w_out[li, :, 0:512])
                for t in range(t_lo, t_hi):
                    p.dma("sp", x[:, t, :], xsrc[t * 128:(t + 1) * 128, :], reads=[xsrcb], writes=[xb[t]] + xq[t])
                for nb in range(4):
                    s = nb % 2
                    if nb + 1 < 4:
                        wpiece(wo[:, (nb + 1) % 2], wob[(nb + 1) % 2], w_out[li, :, (nb + 1) * 512:(nb + 2) * 512])
                    for t in range(t_lo, t_hi):
                        bk = t % 4
                        for k in range(KCH):
                            p.op("pe", lambda e: e.matmul(ps[bk][:, :], lhsT=hT[:, k, t * 128:(t + 1) * 128], rhs=wo[:, s, k, :],
                                                          start=(k == 0), stop=(k == KCH - 1)),
                                 reads=[hTb, wob[s]], writes=[psb[bk]], inc=(k == KCH - 1))
                        p.op("dve", lambda e: e.tensor_tensor(out=x[:, t, nb * 512:(nb + 1) * 512], in0=x[:, t, nb * 512:(nb + 1) * 512],
                                                              in1=ps[bk][:, :], op=ALU.add),
                             reads=[psb[bk], xq[t][nb]], writes=[xq[t][nb]])
                barrier()
            if stop_after == "E%d" % li:
                d = dbg("x", [NTOK, DM])
                p.dma("sp", d.rearrange("(t p) d -> p t d", p=128), x[:], reads=xb, writes=[p.buf("d")])
                return ("E",)
            NP = DFF // GFF
            with SB("xn1", [128, DM], BF16) as xn1, \
                    SB("wu", [128, 3, KCH, GFF], BF16) as wu, \
                    SB("wd", [128, 3, GFF // 128, DM], BF16) as wd, \
                    SB("uT", [128, 4, NTOK], BF16) as uT, \
                    SB("rl", [128, 2, 512], F32) as rl:
                xn1b = p.buf("xn1")
                wub, wdb, rlb = p.bufs("wu", 3), p.bufs("wd", 3), p.bufs("rl", 2)
                uTb = p.buf("uT")

                def load_wu(g):
                    if g < NP:
                        p.dma("pool", wu[:, g % 3], w_up[li, :, g * GFF:(g + 1) * GFF].rearrange("(k p) n -> p k n", p=128), writes=[wub[g % 3]])

                def load_wd(g):
                    if g < NP:
                        p.dma("pool", wd[:, g % 3], w_down[li, g * GFF:(g + 1) * GFF, :].rearrange("(k p) n -> p k n", p=128), writes=[wdb[g % 3]])

                for g in range(3):
                    load_wu(g)
                    load_wd(g)
                for t in range(t_lo, t_hi):
                    norm_tile(x[:, t, :], xb[t], 1 if li == 0 else 4, hT, hTb, t * 128, xn1, xn1b, (0, 1))
                for t in range(t_lo, t_hi):
                    for nb in range(4):
                        xq[t][nb].r.update(xb[t].r)
                rc = 0
                for gp in range(NP // 2):
                    for half in range(2):
                        g = 2 * gp + half
                        s = g % 3
                        for fc in range(2):
                            for (c0, n) in mblks:
                                bk = 2 + rc % 2
                                rs = rc % 2
                                rc += 1
                                for k in range(KCH):
                                    p.op("pe", lambda e: e.matmul(ps[bk][:, :n], lhsT=wu[:, s, k, fc * 128:(fc + 1) * 128], rhs=hT[:, k, c0:c0 + n],
                                                                  start=(k == 0), stop=(k == KCH - 1)),
                                         reads=[wub[s], hTb], writes=[psb[bk]], inc=(k == KCH - 1))
                                p.op("act", lambda e: e.activation(out=rl[:, rs, :n], in_=ps[bk][:, :n], func=AF.Relu), reads=[psb[bk]], writes=[rlb[rs]])
                                p.op("act", lambda e: e.activation(out=uT[:, 2 * half + fc, c0:c0 + n], in_=rl[:, rs, :n], func=AF.Square),
                                     reads=[rlb[rs]], writes=[uTb])
                        load_wu(g + 3)
                    for t in range(t_lo, t_hi):
                        for nb in range(4):
                            bk = 4 + (t * 4 + nb) % 4
                            for j in range(4):
                                g = 2 * gp + j // 2
                                p.op("pe", lambda e: e.matmul(ps[bk][:, :], lhsT=uT[:, j, t * 128:(t + 1) * 128],
                                                              rhs=wd[:, g % 3, j % 2, nb * 512:(nb + 1) * 512],
                                                              start=(j == 0), stop=(j == 3)),
                                     reads=[uTb, wdb[g % 3]], writes=[psb[bk]], inc=(j == 3))
                            p.op("dve", lambda e: e.tensor_tensor(out=x[:, t, nb * 512:(nb + 1) * 512], in0=x[:, t, nb * 512:(nb + 1) * 512],
                                                                  in1=ps[bk][:, :], op=ALU.add),
                                 reads=[psb[bk], xq[t][nb]], writes=[xq[t][nb]])
                    load_wd(2 * gp + 3)
                    load_wd(2 * gp + 4)
                if li == 0:
                    p.dma("sp", X1_d.rearrange("(t p) d -> p t d", p=128), x[:], reads=[b for t in range(NT) for b in xq[t]], writes=[X1_db])
                else:
                    p.dma("sp", y.rearrange("(t p) d -> p t d", p=128), x[:, 1:9, :], reads=[b for t in range(1, 9) for b in xq[t]], writes=[yb])
                barrier()
        return None

    for li in range(2):
        r = layer(li)
        if r is not None:
            break
    p.finish([yb])
    return nc, dbg_out


def _t5_bucket_np(rel):
    import math
    rel = np.asarray(rel, dtype=np.int32)
    half, max_exact = 16, 8
    side = np.where(rel > 0, half, 0)
    n = np.abs(rel)
    n_f = np.maximum(n, 1).astype(np.float32)
    large = max_exact + (np.log(n_f / np.float32(max_exact)) / np.float32(math.log(128 / max_exact))
                         * np.float32(half - max_exact)).astype(np.int32)
    large = np.minimum(large, half - 1)
    return (side + np.where(n < max_exact, n, large)).astype(np.int32)


def _bucket_table(us):
    return _t5_bucket_np(us)


def core_order(r):
    g0t = 8 * r
    order = []
    for kc in range(12):
        gt = g0t - 2 + kc
        order.append(gt if 0 <= gt < 32 else None)
    have = set(t for t in order if t is not None)
    for i in range(32):
        t = (g0t + 10 + i) % 32
        if t not in have:
            order.append(t)
            have.add(t)
    while len(order) < NKC:
        order.append(None)
    assert len(order) == NKC and len(have) == 32
    return order


def prep_inputs(inp):
    f32 = np.float32
    x = np.asarray(inp["x"], f32)
    mem = np.asarray(inp["mem"], f32)
    rel_bias = np.asarray(inp["rel_bias"], f32)
    B = x.shape[0]
    uu = np.arange(-2000, 2001)
    bu = _bucket_table(uu)
    bkt = np.zeros((33, 4), f32)
    for bb in range(32):
        sel = uu[bu == bb]
        if len(sel) == 0:
            lo, hi = 1e9, 1e9
        else:
            lo, hi = float(sel.min()), float(sel.max()) + 1.0
            assert len(sel) == int(hi - lo)
            if sel.min() == uu[0]:
                lo = -1e9
            if sel.max() == uu[-1]:
                hi = 1e9
        bkt[bb, 0], bkt[bb, 1] = lo, hi
        bkt[bb, 2], bkt[bb, 3] = max(lo, -128.0), min(hi, 129.0)
        if bkt[bb, 3] < bkt[bb, 2]:
            bkt[bb, 3] = bkt[bb, 2]
    bkt[32] = (-128.0, 129.0, -128.0, 129.0)
    m0 = np.arange(1280)
    ia0, ib0 = np.maximum(639 - m0, 0), np.maximum(m0 - 639, 0)
    m1 = np.minimum(np.arange(1280), 511)
    ia1, ib1 = np.maximum(255 - m1, 0), np.maximum(m1 - 255, 0)
    positions = np.asarray(inp["positions"]).astype(np.int32)
    cmat = np.zeros((128, 5, 128), f32)
    cmat[:, 0, :] = np.eye(128)
    cmat[:, 1, :] = np.eye(128)[::-1]
    cmat[:64, 2, :64] = 1.0 / 64
    cmat[64:, 2, 64:] = 1.0 / 64
    cmat[:, 3, :] = 1.0 / 128
    cmat[:, 4, :] = 1.0
    identf = np.eye(128, dtype=f32)
    relb = np.concatenate([rel_bias, np.full((1, 12), NEG, f32)], axis=0)

    def col16(v):
        return np.ascontiguousarray(np.asarray(v, f32).reshape(16, 128).T)

    gcols = np.stack([col16(inp["norm_attn"][0]), col16(inp["norm_mlp"][0]), col16(inp["norm_mem"][0]),
                      col16(inp["norm_attn"][1]), col16(inp["norm_mlp"][1]), col16(inp["norm_mem"][1])], axis=1)
    hg = np.stack([np.tile(np.asarray(inp["a_q_norm"][0], f32), 2), np.tile(np.asarray(inp["a_k_norm"][0], f32), 2),
                   np.asarray(inp["a_subln"][0], f32), np.asarray(inp["b_q_norm"][0], f32), np.asarray(inp["b_k_norm"][0], f32),
                   np.asarray(inp["m_q_norm"][0], f32), np.asarray(inp["m_k_norm"][0], f32),
                   np.asarray(inp["m_q_norm"][1], f32), np.asarray(inp["m_k_norm"][1], f32)], axis=1)
    lamv = np.broadcast_to(np.stack([np.asarray(inp[k][0], f32) for k in
                                     ("a_lambda_q1", "a_lambda_k1", "a_lambda_q2", "a_lambda_k2")])[None], (128, 4, 64))
    sinkb = np.broadcast_to(np.asarray(inp["b_sink"][0], f32)[None], (128, 12))
    shared = {
        "w_in_a": np.ascontiguousarray(inp["w_in_a"][0], f32), "w_in_b": np.ascontiguousarray(inp["w_in_b"][0], f32),
        "w_mem_kv": np.ascontiguousarray(inp["w_mem_kv"], f32), "w_out": np.ascontiguousarray(inp["w_out"], f32),
        "w_up": np.ascontiguousarray(inp["w_up"], f32), "w_down": np.ascontiguousarray(inp["w_down"], f32),
        "gcols": np.ascontiguousarray(gcols), "hg": np.ascontiguousarray(hg), "lamv": np.ascontiguousarray(lamv),
        "sinkb": np.ascontiguousarray(sinkb), "relb": relb, "bkt": bkt, "cmat": cmat, "identf": identf,
    }
    maps = []
    zt = np.zeros((128, DM), f32)
    c15, c31 = rel_bias[15], rel_bias[31]
    for b in range(B):
        xt = x[b].reshape(32, 128, DM)
        for r in range(4):
            order = core_order(r)
            g0t = 8 * r
            xfull = np.concatenate([xt[t] if t is not None else zt for t in order], axis=0)
            win = [g0t - 1 + i for i in range(NT)]
            xown = np.concatenate([xt[t] if 0 <= t < 32 else zt for t in win], axis=0)
            cst = np.zeros((12, 3, NKC), f32)
            for qb in range(3):
                n = 512 if qb < 2 else 256
                qt = [win[i] for i in range(4 * qb, min(4 * qb + 4, NT)) if 0 <= win[i] < 32]
                for kc in range(NKC):
                    Dd = (kc - 1) * 128 - qb * 512
                    near = -128 <= Dd <= (512 if n == 512 else 256)
                    gt = order[kc]
                    if gt is None:
                        cst[:, qb, kc] = NEG
                    elif near:
                        cst[:, qb, kc] = 0.0
                    elif gt > max(qt):
                        cst[:, qb, kc] = c31
                    else:
                        assert gt < min(qt)
                        cst[:, qb, kc] = c15
            edge = np.zeros((128, 2), f32)
            if win[0] < 0:
                edge[:, 0] = NEG
            if win[-1] > 31:
                edge[:, 1] = NEG
            pown = positions[b, g0t * 128:(g0t + 8) * 128]
            ppos = np.stack([pown[ia0], pown[ib0], pown[ia1], pown[ib1]], axis=0)
            m = dict(shared)
            m["ppos"] = np.ascontiguousarray(np.broadcast_to(ppos[None], (33, 4, 1280))).astype(np.int32)
            m.update({"xfull": xfull, "xown": xown, "mem": np.ascontiguousarray(mem[b]),
                      "cst0": np.ascontiguousarray(np.broadcast_to(cst[None], (128, 12, 3, NKC))), "edge": edge})
            maps.append(m)
    return maps


_CACHE = {}


def kernel(**inputs):
    maps = prep_inputs(inputs)
    if "nc" not in _CACHE:
        _CACHE["nc"] = build_program()[0]
    nc = _CACHE["nc"]
    res = run_bass_kernel_spmd(nc, maps, core_ids=list(range(8)))
    B = 2
    out = np.zeros((B, 4096, DM), np.float32)
    for b in range(B):
        for r in range(4):
            out[b, r * 1024:(r + 1) * 1024, :] = res.results[b * 4 + r]["y"]
    return out
```

```python
import numpy as np
import concourse.bass as bass
import concourse.mybir as mybir
from concourse.bass_utils import run_bass_kernel_spmd

F32 = mybir.dt.float32
BF16 = mybir.dt.bfloat16
I32 = mybir.dt.int32
AF = mybir.ActivationFunctionType
ALU = mybir.AluOpType
AX = mybir.AxisListType


class Buf:
    __slots__ = ("name", "w", "r")

    def __init__(self, name):
        self.name = name
        self.w = None
        self.r = {}


class Chan:
    __slots__ = ("sem", "key", "total")

    def __init__(self, sem, key):
        self.sem = sem
        self.key = key
        self.total = 0


class Prog:
    ENGS = ("pe", "act", "dve", "pool", "sp")
    NCH = 12

    def __init__(self, nc):
        self.nc = nc
        self.e = {"pe": nc.tensor, "act": nc.scalar, "dve": nc.vector, "pool": nc.gpsimd, "sp": nc.sync}
        self.semobj = {}
        self.cnt = {}
        self.seen = {k: {} for k in self.ENGS}
        for k in self.ENGS:
            self.semobj[k] = nc.alloc_semaphore("s_" + k)
            self.cnt[k] = 0
        self.chans = {}
        self.rr = {}
        for q in ("sp", "pool", "act"):
            n = self.NCH if q != "act" else 4
            self.chans[q] = []
            for i in range(n):
                key = "c_%s%d" % (q, i)
                self.semobj[key] = nc.alloc_semaphore(key)
                self.chans[q].append(Chan(self.semobj[key], key))
            self.rr[q] = 0
        self.nwaits = 0
        self.nins = {k: 0 for k in self.ENGS}

    def buf(self, name):
        return Buf(name)

    def bufs(self, name, n):
        return [Buf("%s%d" % (name, i)) for i in range(n)]

    def _wait(self, eng, tok):
        key, val = tok
        if key == "pe" and eng == "pe":
            return
        if key == "pe":
            assert val <= self.cnt["pe"], "waiting on a pending (un-incremented) PE instruction"
        if self.seen[eng].get(key, 0) >= val:
            return
        self.e[eng].wait_ge(self.semobj[key], val)
        self.seen[eng][key] = val
        self.nwaits += 1

    def _deps(self, eng, reads, writes):
        toks = {}
        for b in reads:
            if b.w is not None:
                k, v = b.w
                toks[k] = max(toks.get(k, 0), v)
        for b in writes:
            if b.w is not None:
                k, v = b.w
                toks[k] = max(toks.get(k, 0), v)
            for k, v in b.r.items():
                toks[k] = max(toks.get(k, 0), v)
        for k, v in toks.items():
            self._wait(eng, (k, v))

    def _mark(self, tok, reads, writes):
        k, v = tok
        for b in reads:
            b.r[k] = max(b.r.get(k, 0), v)
        for b in writes:
            b.w = tok
            b.r = {}

    def op(self, eng, fn, reads=(), writes=(), inc=True, self_sync=True):
        if not self_sync:
            saved = self.seen[eng].get(eng, 0)
            self.seen[eng][eng] = 1 << 60
            self._deps(eng, reads, writes)
            self.seen[eng][eng] = saved
        else:
            self._deps(eng, reads, writes)
        ins = fn(self.e[eng])
        self.nins[eng] += 1
        if inc:
            self.cnt[eng] += 1
            ins.then_inc(self.semobj[eng], 1)
            tok = (eng, self.cnt[eng])
        else:
            assert eng == "pe"
            tok = (eng, self.cnt[eng] + 1)
        self._mark(tok, reads, writes)
        return ins

    def dma(self, q, out, in_, reads=(), writes=(), **kw):
        chs = self.chans[q]
        ch = chs[self.rr[q] % len(chs)]
        self.rr[q] += 1
        if ch.total > 0:
            self._wait(q, (ch.key, ch.total))
        self._deps(q, reads, writes)
        ins = self.e[q].dma_start(out=out, in_=in_, **kw)
        self.nins[q] += 1
        ch.total += 16
        ins.then_inc(ch.sem, 16)
        tok = (ch.key, ch.total)
        self._mark(tok, reads, writes)
        return tok

    def finish(self, out_bufs):
        for b in out_bufs:
            if b.w is not None:
                self._wait("sp", b.w)
        for k in self.ENGS:
            if k != "sp" and self.cnt[k] > 0:
                self._wait("sp", (k, self.cnt[k]))
        for q in self.chans:
            for ch in self.chans[q]:
                if ch.total > 0:
                    self._wait("sp", (ch.key, ch.total))


DM = 2048
KCH = 16
NT = 10
NTOK = NT * 128
NKC = 34
DFF = 8192
GFF = 256
EPS = 1e-6
NEG = -30000.0
S0 = 0.125
S1 = 128.0 ** -0.5
TOKBLKS = [(0, 512), (512, 512), (1024, 256)]


def build_program(stop_after=None, debug=()):
    nc = bass.Bass("TRN2", target_bir_lowering=False)
    p = Prog(nc)
    _uid = [0]

    def _un(name):
        _uid[0] += 1
        return "sb%d_%s" % (_uid[0], name)

    def A(name, shape, dt):
        return nc.alloc_sbuf_tensor(_un(name), shape, dt)

    def SB(name, shape, dt):
        return nc.sbuf_tensor(_un(name), shape, dt)

    def din(name, shape, dt=F32):
        return nc.dram_tensor(name, list(shape), dt, kind="ExternalInput").ap()

    xfull = din("xfull", [NKC * 128, DM])
    xown = din("xown", [NTOK, DM])
    memd = din("mem", [256, DM])
    w_in_a = din("w_in_a", [DM, 5120])
    w_in_b = din("w_in_b", [DM, 3072])
    w_mem_kv = din("w_mem_kv", [2, DM, 1024])
    w_out = din("w_out", [2, DM, DM])
    w_up = din("w_up", [2, DM, DFF])
    w_down = din("w_down", [2, DFF, DM])
    gcols_d = din("gcols", [128, 6, 16])
    hg_d = din("hg", [128, 9])
    lamv_d = din("lamv", [128, 4, 64])
    cst0_d = din("cst0", [128, 12, 3, NKC])
    edge_d = din("edge", [128, 2])
    sink_d = din("sinkb", [128, 12])
    relb_d = din("relb", [33, 12])
    ppos_d = din("ppos", [33, 4, 1280], I32)
    bkt_d = din("bkt", [33, 4])
    cmat_d = din("cmat", [128, 5, 128])
    identf_d = din("identf", [128, 128])
    y = nc.dram_tensor("y", [8 * 128, DM], F32, kind="ExternalOutput").ap()

    kvkind = "ExternalOutput" if "KV" in debug else "Internal"
    KT_d = nc.dram_tensor("KT_d", [12, 128, NKC * 128], BF16, kind=kvkind).ap()
    V_d = nc.dram_tensor("V_d", [12, 128, NKC, 128], BF16, kind=kvkind).ap()
    F0_d = nc.dram_tensor("F0_d", [12, 1280], BF16)
    F1_d = nc.dram_tensor("F1_d", [12, 512], BF16)
    KT_db = p.bufs("KT_d", 12)
    V_db = p.buf("V_d")
    F0_db = p.buf("F0_d")
    F1_db = p.buf("F1_d")
    yb = p.buf("y")
    dbg_out = {}

    def dbg(name, shape, dt=F32):
        t = nc.dram_tensor("dbg_" + name, list(shape), dt, kind="ExternalOutput").ap()
        dbg_out[name] = t
        return t

    gcols = A("gcols", [128, 6, 16], F32); gcolsb = p.buf("gcols")
    hg = A("hg", [128, 9], F32); hgb = p.buf("hg")
    cm = A("cm", [128, 5, 128], BF16); cmb = p.buf("cm")
    identf = A("identf", [128, 128], F32); identfb = p.buf("identf")
    sc = A("sc", [128, 16], F32); scb = p.buf("sc")
    edge = A("edge", [128, 2], F32); edgeb = p.buf("edge")
    esink = A("esink", [128, 12], F32); esinkb = p.buf("esink")
    ssq = A("ssq", [128, 8], F32); ssqb = p.bufs("ssq", 4)
    junk = A("junk", [128, DM], BF16); junkb = p.buf("junk")
    IDB, JM, OD64, O128, ONES = 0, 1, 2, 3, 4

    psd = [nc.alloc_psum_tensor("psd%d" % i, [128, 2, 512], F32) for i in range(4)]
    ps = [psd[i // 2][:, i % 2, :] for i in range(8)]
    psb = p.bufs("ps", 8)
    onesf = A("onesf", [128, 128], F32); onesfb = p.buf("onesf")
    p.op("dve", lambda e: e.memset(onesf[:], 1.0), writes=[onesfb])

    p.dma("sp", gcols[:], gcols_d, writes=[gcolsb])
    p.dma("sp", hg[:], hg_d, writes=[hgb])
    p.dma("sp", identf[:], identf_d, writes=[identfb])
    p.dma("sp", edge[:], edge_d, writes=[edgeb])
    p.dma("sp", esink[:], sink_d, writes=[esinkb])
    p.dma("pool", cm[:], cmat_d, writes=[cmb])
    p.op("act", lambda e: e.activation(out=esink[:], in_=esink[:], func=AF.Exp), reads=[esinkb], writes=[esinkb])

    state = {"ss": 0, "tp": 0}

    def barrier():
        for e in Prog.ENGS:
            for k in Prog.ENGS:
                if k != e and p.cnt[k] > 0:
                    p._wait(e, (k, p.cnt[k]))
            for q in p.chans:
                for ch in p.chans[q]:
                    if ch.total > 0:
                        p._wait(e, (ch.key, ch.total))

    def norm_tile(x_ap, x_buf, gi, dst, dst_buf, col0, xn, xnb, tpbanks):
        s = state["ss"] % 4
        state["ss"] += 1
        ssa = ssq[:, 2 * s:2 * s + 1]
        rsa = ssq[:, 2 * s + 1:2 * s + 2]
        p.op("act", lambda e: e.activation(out=junk[:], in_=x_ap, func=AF.Square, accum_out=ssa),
             reads=[x_buf], writes=[junkb, ssqb[s]])
        p.op("act", lambda e: e.activation(out=rsa, in_=ssa, func=AF.Ln, scale=1.0 / DM, bias=EPS),
             reads=[ssqb[s]], writes=[ssqb[s]])
        p.op("act", lambda e: e.activation(out=rsa, in_=rsa, func=AF.Exp, scale=-0.5),
             reads=[ssqb[s]], writes=[ssqb[s]])
        p.op("dve", lambda e: e.tensor_scalar(out=xn[:], in0=x_ap, scalar1=rsa, scalar2=None, op0=ALU.mult),
             reads=[x_buf, ssqb[s]], writes=[xnb])
        for kb in range(4):
            bi = tpbanks[state["tp"] % len(tpbanks)]
            state["tp"] += 1
            for j in range(4):
                k = kb * 4 + j
                p.op("pe", lambda e: e.transpose(ps[bi].bitcast(BF16)[:, j * 128:(j + 1) * 128], xn[:, k * 128:(k + 1) * 128], cm[:, IDB, :]),
                     reads=[xnb, cmb], writes=[psb[bi]], inc=(j == 3))
            p.op("dve", lambda e: e.tensor_tensor(
                out=dst[:, kb * 4:(kb + 1) * 4, col0:col0 + 128],
                in0=ps[bi].bitcast(BF16)[:, 0:512].rearrange("p (a b) -> p a b", a=4),
                in1=gcols[:, gi, kb * 4:(kb + 1) * 4].unsqueeze(2).to_broadcast([128, 4, 128]),
                op=ALU.mult), reads=[psb[bi], gcolsb], writes=[dst_buf])

    def proj_fm(w, wbuf, j, src, src_buf, c0, n, nm, gcol, dst_ap, dst_buf, tmp, banks, post=None):
        pj, msb = banks
        for k in range(KCH):
            p.op("pe", lambda e: e.matmul(ps[pj][:, :n], lhsT=w[:, k, j * 128:(j + 1) * 128], rhs=src[:, k, c0:c0 + n],
                                          start=(k == 0), stop=(k == KCH - 1)),
                 reads=[wbuf, src_buf], writes=[psb[pj]], inc=(k == KCH - 1))
        sq, sqb, ln, lnb = tmp

        def tail():
            p.op("act", lambda e: e.activation(out=sq[:, :n], in_=ps[pj][:, :n], func=AF.Square), reads=[psb[pj]], writes=[sqb])
            p.op("pe", lambda e: e.matmul(ps[msb][:, :n], lhsT=cm[:, nm, :], rhs=sq[:, :n], start=True, stop=True),
                 reads=[sqb, cmb], writes=[psb[msb]])
            p.op("act", lambda e: e.activation(out=ln[:, :n], in_=ps[msb][:, :n], func=AF.Ln, bias=EPS), reads=[psb[msb]], writes=[lnb])
            p.op("act", lambda e: e.activation(out=ln[:, :n], in_=ln[:, :n], func=AF.Exp, scale=-0.5), reads=[lnb], writes=[lnb])
            p.op("dve", lambda e: e.scalar_tensor_tensor(out=dst_ap, in0=ps[pj][:, :n], scalar=gcol, in1=ln[:, :n],
                                                         op0=ALU.mult, op1=ALU.mult),
                 reads=[psb[pj], lnb, hgb], writes=[dst_buf])
            if post is not None:
                post()

        proj_flush()
        state["tail"] = tail

    def proj_flush():
        t = state.pop("tail", None)
        if t is not None:
            t()

    def wpiece(dst, dst_buf, src2d):
        p.dma("pool", dst, src2d.rearrange("(k p) n -> p k n", p=128), writes=[dst_buf])

    with SB("lamv", [128, 4, 64], F32) as lamv, \
            SB("relb", [33, 12], F32) as relb, \
            SB("relb8", [33, 12], F32) as relb8, \
            SB("oh0", [32, 1280], F32) as oh0, \
            SB("oh1", [33, 512], F32) as oh1, \
            SB("fst", [12, 1280], BF16) as fst:
        lamvb, relbb, relb8b, oh0b, oh1b, fstb = [p.buf(n) for n in ("lamv", "relb", "relb8", "oh0", "oh1", "fst")]
        p.dma("sp", lamv[:], lamv_d, writes=[lamvb])
        p.dma("sp", relb[:], relb_d, writes=[relbb])
        with SB("ppos", [33, 4, 1280], I32) as ppos, SB("bkt", [33, 4], F32) as bkt, SB("urel", [33, 1280], F32) as urel:
            pposb, bktb, urelb = p.buf("ppos"), p.buf("bkt"), p.buf("urel")
            p.dma("sp", ppos[:], ppos_d, writes=[pposb])
            p.dma("sp", bkt[:], bkt_d, writes=[bktb])
            for (oh, ohb, kk, n, ia, c0) in ((oh0, oh0b, 32, 1280, 0, 0), (oh1, oh1b, 33, 512, 2, 2)):
                p.op("dve", lambda e: e.tensor_tensor(out=urel[:kk, :n], in0=ppos[:kk, ia, :n], in1=ppos[:kk, ia + 1, :n], op=ALU.subtract),
                     reads=[pposb], writes=[urelb])
                p.op("dve", lambda e: e.tensor_scalar(out=oh[:kk, :n], in0=urel[:kk, :n], scalar1=bkt[:kk, c0:c0 + 1], scalar2=None, op0=ALU.is_ge),
                     reads=[urelb, bktb], writes=[ohb])
                p.op("dve", lambda e: e.scalar_tensor_tensor(out=oh[:kk, :n], in0=urel[:kk, :n], scalar=bkt[:kk, c0 + 1:c0 + 2], in1=oh[:kk, :n],
                                                             op0=ALU.is_lt, op1=ALU.mult),
                     reads=[urelb, bktb, ohb], writes=[ohb])
            p.op("dve", lambda e: e.tensor_scalar(out=oh1[32:33, :], in0=oh1[32:33, :], scalar1=-1.0, scalar2=1.0, op0=ALU.mult, op1=ALU.add),
                 reads=[oh1b], writes=[oh1b])
            barrier()
        p.op("dve", lambda e: e.tensor_tensor(out=lamv[:, 0, :], in0=lamv[:, 0, :], in1=lamv[:, 1, :], op=ALU.mult),
             reads=[lamvb], writes=[lamvb])
        p.op("dve", lambda e: e.tensor_tensor(out=lamv[:, 2, :], in0=lamv[:, 2, :], in1=lamv[:, 3, :], op=ALU.mult),
             reads=[lamvb], writes=[lamvb])
        p.op("dve", lambda e: e.reduce_sum(out=sc[:, 2:3], in_=lamv[:, 0, :], axis=AX.X), reads=[lamvb], writes=[scb])
        p.op("dve", lambda e: e.reduce_sum(out=sc[:, 3:4], in_=lamv[:, 2, :], axis=AX.X), reads=[lamvb], writes=[scb])
        p.op("act", lambda e: e.activation(out=sc[:, 2:4], in_=sc[:, 2:4], func=AF.Exp), reads=[scb], writes=[scb])
        p.op("dve", lambda e: e.tensor_tensor(out=sc[:, 0:1], in0=sc[:, 3:4], in1=sc[:, 2:3], op=ALU.subtract),
             reads=[scb], writes=[scb])
        p.op("dve", lambda e: e.tensor_scalar(out=sc[:, 0:1], in0=sc[:, 0:1], scalar1=-0.2, scalar2=None, op0=ALU.add),
             reads=[scb], writes=[scb])
        p.op("dve", lambda e: e.tensor_scalar(out=sc[:, 1:2], in0=hg[:, 2:3], scalar1=0.8, scalar2=None, op0=ALU.mult),
             reads=[scb, hgb], writes=[scb])
        for (scale, oh, ohb, n, Fd, Fdb, kk) in ((1.0 / S0, oh0, oh0b, 1280, F0_d, F0_db, 32), (1.0 / S1, oh1, oh1b, 512, F1_d, F1_db, 33)):
            p.op("act", lambda e: e.mul(out=relb8[:], in_=relb[:], mul=scale), reads=[relbb], writes=[relb8b])
            for c0 in range(0, n, 512):
                nn = min(512, n - c0)
                p.op("pe", lambda e: e.matmul(ps[0][:12, :nn], lhsT=relb8[:kk, :], rhs=oh[:kk, c0:c0 + nn], start=True, stop=True),
                     reads=[relb8b, ohb], writes=[psb[0]])
                p.op("act", lambda e: e.copy(out=fst[:, c0:c0 + nn], in_=ps[0][:12, :nn]), reads=[psb[0]], writes=[fstb])
            p.dma("sp", Fd.ap()[:, :n], fst[:, :n], reads=[fstb], writes=[Fdb])
        barrier()
    if "F0" in debug:
        pass

    def phase_a():
        with SB("wk", [128, KCH, 1536], BF16) as wk, \
                SB("wv", [128, KCH, 1536], BF16) as wv, \
                SB("xt", [128, 2, DM], F32) as xt, \
                SB("xn", [128, 2, DM], BF16) as xn_, \
                SB("hTc", [128, 2, KCH, 512], BF16) as hTc, \
                SB("kst", [128, 2, 512], BF16) as kst, \
                SB("vst", [128, 2, 12, 4, 128], BF16) as vst, \
                SB("sq", [128, 2, 512], BF16) as sq, \
                SB("ln", [128, 2, 512], F32) as ln:
            wkb, wvb = p.bufs("wk", 3), p.bufs("wv", 3)
            xtb, xnb, hTcb, kstb, vstb, sqb, lnb = (p.bufs("xt", 2), p.bufs("xn", 2), p.bufs("hTc", 2), p.bufs("kst", 2),
                                                    p.bufs("vst", 2), p.bufs("sq", 2), p.bufs("ln", 2))
            for i in range(3):
                wpiece(wk[:, :, i * 512:(i + 1) * 512], wkb[i], w_in_a[:, 1536 + i * 512:1536 + (i + 1) * 512])
            for i in range(3):
                wpiece(wv[:, :, i * 512:(i + 1) * 512], wvb[i], w_in_a[:, 3072 + i * 512:3072 + (i + 1) * 512])
            blocks = [(t0, min(4, NKC - t0)) for t0 in range(0, NKC, 4)]
            ti = 0
            hk = 0
            for bi_, (t0, nt) in enumerate(blocks):
                hs = bi_ % 2
                n = nt * 128
                for i in range(nt):
                    s = ti % 2
                    p.dma("sp", xt[:, s, :], xfull[(t0 + i) * 128:(t0 + i + 1) * 128, :], writes=[xtb[s]])
                    norm_tile(xt[:, s, :], xtb[s], 0, hTc[:, hs], hTcb[hs], i * 128, xn_[:, ti % 2, :], xnb[ti % 2], (0, 1))
                    ti += 1
                for h in range(12):
                    s2 = hk % 2
                    hk += 1
                    def store(h=h, s2=s2, t0=t0, n=n):
                        p.dma("pool", KT_d[h, :, t0 * 128:t0 * 128 + n], kst[:, s2, :n], reads=[kstb[s2]], writes=[KT_db[h]])
                    proj_fm(wk, wkb[h // 4], h, hTc[:, hs], hTcb[hs], 0, n, OD64, hg[:, 1:2],
                            kst[:, s2, :n], kstb[s2], (sq[:, s2, :], sqb[s2], ln[:, s2, :], lnb[s2]), (2 + s2, 4 if s2 == 0 else 7), post=store)
                for i in range(nt):
                    for nb in range(3):
                        if i == 0 and nb == 1:
                            proj_flush()
                        bk = 5 + (i * 3 + nb) % 2
                        for k in range(KCH):
                            p.op("pe", lambda e: e.matmul(ps[bk][:, :], lhsT=hTc[:, hs, k, i * 128:(i + 1) * 128],
                                                          rhs=wv[:, k, nb * 512:(nb + 1) * 512], start=(k == 0), stop=(k == KCH - 1)),
                                 reads=[hTcb[hs], wvb[nb]], writes=[psb[bk]], inc=(k == KCH - 1))
                        p.op("act", lambda e: e.copy(out=vst[:, hs, nb * 4:(nb + 1) * 4, i, :],
                                                     in_=ps[bk][:, :].rearrange("p (a b) -> p a b", a=4)),
                             reads=[psb[bk]], writes=[vstb[hs]])
                p.dma("pool", V_d.rearrange("h p c e -> p h c e")[:, :, t0:t0 + nt, :], vst[:, hs, :, :nt, :],
                      reads=[vstb[hs]], writes=[V_db])
            barrier()

    phase_a()
    if stop_after == "A":
        p.finish([])
        return nc, dbg_out

    hT = A("hT", [128, KCH, NTOK], BF16)
    hTb = p.buf("hT")
    X1_d = nc.dram_tensor("X1_d", [NTOK, DM], F32).ap()
    X1_db = p.buf("X1_d")

    def layer(li):
        xsrc = xown if li == 0 else X1_d
        xsrcb = p.buf("xsrc") if li == 0 else X1_db
        w_in = w_in_a if li == 0 else w_in_b
        with SB("QT%d" % li, [128, 16, NTOK], BF16) as QT, \
                SB("KmT%d" % li, [128, 4, 256], BF16) as KmT, \
                SB("Vm%d" % li, [128, 2, 512], BF16) as Vm, \
                SB("KT1%d" % li, [128, 4, NTOK if li == 1 else 2], BF16) as KT1, \
                SB("V1%d" % li, [128, NT, 512 if li == 1 else 2], BF16) as V1:
            QTb, KmTb, Vmb, KT1b, V1b = p.buf("QT"), p.buf("KmT"), p.buf("Vm"), p.buf("KT1"), p.buf("V1")
            with SB("xt", [128, 2, DM], F32) as xt, \
                    SB("xn", [128, 2, DM], BF16) as xn_, \
                    SB("wq", [128, 2, KCH, 512], BF16) as wq, \
                    SB("mnT", [128, KCH, 256], BF16) as mnT, \
                    SB("sq", [128, 2, 512], BF16) as sq, \
                    SB("ln", [128, 2, 512], F32) as ln:
                xtb, xnb, wqb, sqb, lnb = p.bufs("xt", 2), p.bufs("xn", 2), p.bufs("wq", 2), p.bufs("sq", 2), p.bufs("ln", 2)
                mnTb = p.buf("mnT")
                wi = [0]

                def next_w(src2d):
                    s = wi[0] % 2
                    wi[0] += 1
                    wpiece(wq[:, s], wqb[s], src2d)
                    return wq[:, s], wqb[s]

                pieces = []
                if li == 0:
                    for i in range(3):
                        pieces.append(("q", i, w_in[:, i * 512:(i + 1) * 512]))
                    pieces.append(("qm", 0, w_in[:, 4608:5120]))
                else:
                    for i in range(3):
                        pieces.append(("q", i, w_in[:, i * 512:(i + 1) * 512]))
                    pieces.append(("k", 0, w_in[:, 1536:2048]))
                    pieces.append(("v", 0, w_in[:, 2048:2560]))
                    pieces.append(("qm", 0, w_in[:, 2560:3072]))
                pieces.append(("mk", 0, w_mem_kv[li, :, 0:512]))
                pieces.append(("mv", 0, w_mem_kv[li, :, 512:1024]))
                loaded = [next_w(pieces[0][2]), None]
                for t in range(NT):
                    s = t % 2
                    p.dma("sp", xt[:, s, :], xsrc[t * 128:(t + 1) * 128, :], reads=[xsrcb], writes=[xtb[s]])
                    norm_tile(xt[:, s, :], xtb[s], 0 if li == 0 else 3, hT, hTb, t * 128, xn_[:, s, :], xnb[s], (0, 1))
                for t in range(2):
                    s = t % 2
                    p.dma("sp", xt[:, s, :], memd[t * 128:(t + 1) * 128, :], writes=[xtb[s]])
                    norm_tile(xt[:, s, :], xtb[s], 2 if li == 0 else 5, mnT, mnTb, t * 128, xn_[:, s, :], xnb[s], (0, 1))
                cnt = 0
                for pi, (kind, idx, src2d) in enumerate(pieces):
                    w, wb = loaded[pi % 2]
                    if pi + 1 < len(pieces):
                        loaded[(pi + 1) % 2] = next_w(pieces[pi + 1][2])
                    if kind in ("q", "qm", "k", "mk"):
                        for j in range(4):
                            if kind == "q":
                                fc = idx * 4 + j
                                nm, gc = (OD64, hg[:, 0:1]) if li == 0 else (O128, hg[:, 3:4])
                                dst, dstb, src, srcb, blks = QT, QTb, hT, hTb, TOKBLKS
                            elif kind == "qm":
                                fc = 12 + j
                                nm, gc = O128, (hg[:, 5:6] if li == 0 else hg[:, 7:8])
                                dst, dstb, src, srcb, blks = QT, QTb, hT, hTb, TOKBLKS
                            elif kind == "k":
                                fc = j
                                nm, gc = O128, hg[:, 4:5]
                                dst, dstb, src, srcb, blks = KT1, KT1b, hT, hTb, TOKBLKS
                            else:
                                fc = j
                                nm, gc = O128, (hg[:, 6:7] if li == 0 else hg[:, 8:9])
                                dst, dstb, src, srcb, blks = KmT, KmTb, mnT, mnTb, [(0, 256)]
                            for (c0, n) in blks:
                                s2 = cnt % 2
                                cnt += 1
                                proj_fm(w, wb, j, src, srcb, c0, n, nm, gc, dst[:, fc, c0:c0 + n], dstb,
                                        (sq[:, s2, :], sqb[s2], ln[:, s2, :], lnb[s2]), (2 + s2, 4 if s2 == 0 else 7))
                    else:
                        if kind == "v":
                            src, srcb, ntl, dst, dstb = hT, hTb, NT, V1, V1b
                        else:
                            src, srcb, ntl, dst, dstb = mnT, mnTb, 2, Vm, Vmb
                        for i in range(ntl):
                            if i == 1:
                                proj_flush()
                            bk = 5 + i % 2
                            for k in range(KCH):
                                p.op("pe", lambda e: e.matmul(ps[bk][:, :], lhsT=src[:, k, i * 128:(i + 1) * 128],
                                                              rhs=w[:, k, :], start=(k == 0), stop=(k == KCH - 1)),
                                     reads=[srcb, wb], writes=[psb[bk]], inc=(k == KCH - 1))
                            p.op("act", lambda e: e.copy(out=dst[:, i, :], in_=ps[bk][:, :]), reads=[psb[bk]], writes=[dstb])
                proj_flush()
                barrier()
            if stop_after == "B%d" % li:
                return ("B", QT, KmT, Vm, KT1, V1)
            with SB("eT", [128, 4, 512], BF16) as eT, \
                    SB("pp", [128, 6, 512], F32) as pp, \
                    SB("sqb", [128, 512], BF16) as sqh:
                eTb = p.bufs("eT", 4)
                ppb = p.bufs("pp", 6)
                sqhb = p.buf("sqh")
                ring = (0, 1, 2)
                OB0, OB1, ZB0, ZB1, MSB = 6, 7, 4, 5, 3
                ecnt = [0]

                def mem_attn():
                    blk = 0
                    for hm in range(4):
                        for (q0, n) in TOKBLKS:
                            ob, zb, pi = OB0 + blk % 2, ZB0 + blk % 2, 2 + blk % 2
                            blk += 1
                            for mc in range(2):
                                i = ecnt[0]
                                ecnt[0] += 1
                                bk = ring[i % 3]
                                es = i % 4
                                p.op("pe", lambda e: e.matmul(ps[bk][:, :n], lhsT=KmT[:, hm, mc * 128:(mc + 1) * 128],
                                                              rhs=QT[:, 12 + hm, q0:q0 + n], start=True, stop=True),
                                     reads=[KmTb, QTb], writes=[psb[bk]])
                                p.op("act", lambda e: e.activation(out=eT[:, es, :n], in_=ps[bk][:, :n], func=AF.Exp, scale=S1),
                                     reads=[psb[bk]], writes=[eTb[es]])
                                p.op("pe", lambda e: e.matmul(ps[ob][:, :n], lhsT=Vm[:, mc, hm * 128:(hm + 1) * 128], rhs=eT[:, es, :n],
                                                              start=(mc == 0), stop=(mc == 1)),
                                     reads=[Vmb, eTb[es]], writes=[psb[ob]], inc=False)
                                p.op("pe", lambda e: e.matmul(ps[zb][:, :n], lhsT=cm[:, ONES, :], rhs=eT[:, es, :n],
                                                              start=(mc == 0), stop=(mc == 1)),
                                     reads=[cmb, eTb[es]], writes=[psb[zb]])
                            p.op("act", lambda e: e.activation(out=pp[:, pi, :n], in_=ps[zb][:, :n], func=AF.Ln), reads=[psb[zb]], writes=[ppb[pi]])
                            p.op("act", lambda e: e.activation(out=pp[:, pi, :n], in_=pp[:, pi, :n], func=AF.Exp, scale=-1.0), reads=[ppb[pi]], writes=[ppb[pi]])
                            p.op("dve", lambda e: e.tensor_tensor(out=hT[:, 12 + hm, q0:q0 + n], in0=ps[ob][:, :n], in1=pp[:, pi, :n], op=ALU.mult),
                                 reads=[psb[ob], ppb[pi]], writes=[hTb])

                if li == 0:
                    with SB("KTh", [128, 2, NKC * 128], BF16) as KTh, \
                            SB("Vh", [128, 2, NKC, 128], BF16) as Vh, \
                            SB("Tb", [128, 2, 6, 512], BF16) as Tb, \
                            SB("cst0", [128, 12, 3, NKC], F32) as cst0, \
                            SB("eT2", [128, 4, 2, 512], BF16) as eT2, \
                            SB("zacc", [128, 2, 2, 512], F32) as zacc:
                        KThb, Vhb, Tbb = p.bufs("KTh", 2), p.bufs("Vh", 2), p.bufs("Tb", 2)
                        cst0b = p.buf("cst0")
                        eT2b = p.bufs("eT2", 4)
                        zaccb = p.bufs("zacc", 2)
                        p.dma("sp", cst0[:], cst0_d, writes=[cst0b])

                        def load_head(h):
                            s = h % 2
                            p.dma("sp", KTh[:, s, :], KT_d[h], reads=[KT_db[h]], writes=[KThb[s]])
                            p.dma("sp", Vh[:, s], V_d[h], reads=[V_db], writes=[Vhb[s]])
                            p.dma("sp", Tb[:, s], bass.AP(F0_d, h * 1280, [[1, 128], [128, 6], [1, 512]]),
                                  reads=[F0_db], writes=[Tbb[s]])

                        load_head(0)
                        kcnt = 0
                        for h in range(12):
                            s = h % 2
                            if h + 1 < 12:
                                load_head(h + 1)
                            for qb, (q0, n) in enumerate(TOKBLKS):
                                pend = []

                                def pv(kc, es):
                                    for c in (0, 1):
                                        p.op("pe", lambda e: e.matmul(ps[OB0 + c][:, :n], lhsT=Vh[:, s, kc, :], rhs=eT2[:, es, c, :n],
                                                                      start=(kc == 0), stop=(kc == NKC - 1)),
                                             reads=[Vhb[s], eT2b[es]], writes=[psb[OB0 + c]], inc=(c == 1))

                                first = {"dve": True, "pool": True}
                                for kc in range(NKC):
                                    r = kcnt % 3
                                    es = kcnt % 4
                                    kcnt += 1
                                    Dd = (kc - 1) * 128 - qb * 512
                                    near = (-128 <= Dd <= (512 if n == 512 else 256))
                                    di = (512 - Dd) // 128
                                    for c in (0, 1):
                                        bk = 2 * r + c
                                        p.op("pe", lambda e: e.matmul(ps[bk][:, :n], lhsT=KTh[64 * c:64 * c + 64, s, kc * 128:(kc + 1) * 128],
                                                                      rhs=QT[64 * c:64 * c + 64, h, q0:q0 + n], start=True, stop=(not near)),
                                             reads=[KThb[s], QTb], writes=[psb[bk]], inc=(c == 1 and not near))
                                    if near:
                                        for c in (0, 1):
                                            bk = 2 * r + c
                                            p.op("pe", lambda e: e.matmul(ps[bk][:, :n], lhsT=cm[:, JM, :], rhs=Tb[:, s, di, :n],
                                                                          start=False, stop=True),
                                                 reads=[cmb, Tbb[s]], writes=[psb[bk]], inc=(c == 1))
                                    p.op("act", lambda e: e.activation(out=eT2[:, es, :, :n], in_=psd[r][:, :, :n], func=AF.Exp,
                                                                       scale=S0, bias=cst0[:, h, qb, kc:kc + 1]),
                                         reads=[psb[2 * r], psb[2 * r + 1], cst0b], writes=[eT2b[es]])
                                    eng = "dve"
                                    zi = 1 if eng == "pool" else 0
                                    if first[eng]:
                                        first[eng] = False
                                        p.op(eng, lambda e: e.tensor_copy(out=zacc[:, zi, :, :n], in_=eT2[:, es, :, :n]),
                                             reads=[eT2b[es]], writes=[zaccb[zi]])
                                    else:
                                        p.op(eng, lambda e: e.tensor_tensor(out=zacc[:, zi, :, :n], in0=zacc[:, zi, :, :n], in1=eT2[:, es, :, :n], op=ALU.add),
                                             reads=[eT2b[es]], writes=[zaccb[zi]], self_sync=False)
                                    pend.append((kc, es))
                                    if len(pend) > 2:
                                        pv(*pend.pop(0))
                                for a in pend:
                                    pv(*a)
                                for c in (0, 1):
                                    p.op("pe", lambda e: e.matmul(ps[c][:, :n], lhsT=onesf[:], rhs=zacc[:, 0, c, :n], start=True, stop=True),
                                         reads=[onesfb, zaccb[0]], writes=[psb[c]])
                                    p.op("act", lambda e: e.activation(out=pp[:, 2 + c, :n], in_=ps[c][:, :n], func=AF.Ln), reads=[psb[c]], writes=[ppb[2 + c]])
                                    p.op("act", lambda e: e.activation(out=pp[:, 2 + c, :n], in_=pp[:, 2 + c, :n], func=AF.Exp, scale=-1.0),
                                         reads=[ppb[2 + c]], writes=[ppb[2 + c]])
                                    p.op("dve", lambda e: e.tensor_tensor(out=pp[:, c, :n], in0=ps[OB0 + c][:, :n], in1=pp[:, 2 + c, :n], op=ALU.mult),
                                         reads=[psb[OB0 + c], ppb[2 + c]], writes=[ppb[c]])
                                p.op("dve", lambda e: e.scalar_tensor_tensor(out=pp[:, 0, :n], in0=pp[:, 1, :n], scalar=sc[:, 0:1], in1=pp[:, 0, :n],
                                                                             op0=ALU.mult, op1=ALU.add),
                                     reads=[ppb[0], ppb[1], scb], writes=[ppb[0]])
                                p.op("dve", lambda e: e.tensor_tensor(out=sqh[:, :n], in0=pp[:, 0, :n], in1=pp[:, 0, :n], op=ALU.mult),
                                     reads=[ppb[0]], writes=[sqhb])
                                p.op("pe", lambda e: e.matmul(ps[2][:, :n], lhsT=cm[:, O128, :], rhs=sqh[:, :n], start=True, stop=True),
                                     reads=[cmb, sqhb], writes=[psb[2]])
                                p.op("act", lambda e: e.activation(out=pp[:, 4, :n], in_=ps[2][:, :n], func=AF.Ln, bias=EPS),
                                     reads=[psb[2]], writes=[ppb[4]])
                                p.op("act", lambda e: e.activation(out=pp[:, 4, :n], in_=pp[:, 4, :n], func=AF.Exp, scale=-0.5),
                                     reads=[ppb[4]], writes=[ppb[4]])
                                p.op("dve", lambda e: e.scalar_tensor_tensor(out=hT[:, h, q0:q0 + n], in0=pp[:, 0, :n], scalar=sc[:, 1:2], in1=pp[:, 4, :n],
                                                                             op0=ALU.mult, op1=ALU.mult),
                                     reads=[ppb[0], ppb[4], scb], writes=[hTb])
                        mem_attn()
                        barrier()
                else:
                    with SB("Tb1", [128, 12, 3, 128], BF16) as Tb1:
                        Tb1b = p.buf("Tb1")
                        for h in range(12):
                            p.dma("sp", Tb1[:, h], bass.AP(F1_d, h * 512, [[1, 128], [128, 3], [1, 128]]), reads=[F1_db], writes=[Tb1b])
                        blk1 = 0
                        for t in range(1, 9):
                            for g in range(4):
                                ob, zb, pi = OB0 + blk1 % 2, ZB0 + blk1 % 2, 2 + blk1 % 2
                                blk1 += 1
                                js = (t - 1, t, t + 1)
                                for jj, j in enumerate(js):
                                    i = ecnt[0]
                                    ecnt[0] += 1
                                    bk = ring[i % 3]
                                    es = i % 4
                                    dj = 1 - (j - t)
                                    o3 = ps[bk][:, :384].rearrange("p (a b) -> p a b", a=3)
                                    p.op("pe", lambda e: e.matmul(o3, lhsT=KT1[:, g, j * 128:(j + 1) * 128],
                                                                  rhs=QT[:, 3 * g:3 * g + 3, t * 128:(t + 1) * 128], start=True, stop=False),
                                         reads=[KT1b, QTb], writes=[psb[bk]], inc=False)
                                    p.op("pe", lambda e: e.matmul(o3, lhsT=cm[:, JM, :], rhs=Tb1[:, 3 * g:3 * g + 3, dj, :], start=False, stop=True),
                                         reads=[cmb, Tb1b], writes=[psb[bk]])
                                    if t == 1 and j == 0:
                                        bias = edge[:, 0:1]
                                    elif t == 8 and j == 9:
                                        bias = edge[:, 1:2]
                                    else:
                                        bias = 0.0
                                    p.op("act", lambda e: e.activation(out=eT[:, es, :384], in_=ps[bk][:, :384], func=AF.Exp, scale=S1, bias=bias),
                                         reads=[psb[bk], edgeb], writes=[eTb[es]])
                                    p.op("pe", lambda e: e.matmul(ps[ob][:, :384], lhsT=V1[:, j, g * 128:(g + 1) * 128], rhs=eT[:, es, :384],
                                                                  start=(jj == 0), stop=(jj == 2)),
                                         reads=[V1b, eTb[es]], writes=[psb[ob]], inc=False)
                                    p.op("pe", lambda e: e.matmul(ps[zb][:, :384], lhsT=cm[:, ONES, :], rhs=eT[:, es, :384],
                                                                  start=(jj == 0), stop=(jj == 2)),
                                         reads=[cmb, eTb[es]], writes=[psb[zb]])
                                p.op("dve", lambda e: e.tensor_tensor(out=pp[:, pi, :384].rearrange("p (a b) -> p a b", a=3),
                                                                      in0=ps[zb][:, :384].rearrange("p (a b) -> p a b", a=3),
                                                                      in1=esink[:, 3 * g:3 * g + 3].unsqueeze(2).to_broadcast([128, 3, 128]), op=ALU.add),
                                     reads=[psb[zb], esinkb], writes=[ppb[pi]])
                                p.op("act", lambda e: e.activation(out=pp[:, pi, :384], in_=pp[:, pi, :384], func=AF.Ln), reads=[ppb[pi]], writes=[ppb[pi]])
                                p.op("act", lambda e: e.activation(out=pp[:, pi, :384], in_=pp[:, pi, :384], func=AF.Exp, scale=-1.0), reads=[ppb[pi]], writes=[ppb[pi]])
                                p.op("dve", lambda e: e.tensor_tensor(out=hT[:, 3 * g:3 * g + 3, t * 128:(t + 1) * 128],
                                                                      in0=ps[ob][:, :384].rearrange("p (a b) -> p a b", a=3),
                                                                      in1=pp[:, pi, :384].rearrange("p (a b) -> p a b", a=3), op=ALU.mult),
                                     reads=[psb[ob], ppb[pi]], writes=[hTb])
                        mem_attn()
                        barrier()
        if stop_after == "D%d" % li:
            d = dbg("hT", [128, KCH, NTOK], BF16)
            p.dma("sp", d, hT[:], reads=[hTb], writes=[p.buf("d")])
            return ("D",)
        t_lo, t_hi = (0, NT) if li == 0 else (1, 9)
        mblks = TOKBLKS if li == 0 else [(128, 512), (640, 512)]
        with SB("x%d" % li, [128, NT, DM], F32) as x:
            xb = p.bufs("x", NT)
            xq = [p.bufs("xq%d_" % t, 4) for t in range(NT)]
            with SB("wo", [128, 2, KCH, 512], BF16) as wo:
                wob = p.bufs("wo", 2)
                wpiece(wo[:, 0], wob[0], w_out[li, :, 0:512])
                for t in range(t_lo, t_hi):
                    p.dma("sp", x[:, t, :], xsrc[t * 128:(t + 1) * 128, :], reads=[xsrcb], writes=[xb[t]] + xq[t])
                for nb in range(4):
                    s = nb % 2
                    if nb + 1 < 4:
                        wpiece(wo[:, (nb + 1) % 2], wob[(nb + 1) % 2], w_out[li, :, (nb + 1) * 512:(nb + 2) * 512])
                    for t in range(t_lo, t_hi):
                        bk = t % 4
                        for k in range(KCH):
                            p.op("pe", lambda e: e.matmul(ps[bk][:, :], lhsT=hT[:, k, t * 128:(t + 1) * 128], rhs=wo[:, s, k, :],
                                                          start=(k == 0), stop=(k == KCH - 1)),
                                 reads=[hTb, wob[s]], writes=[psb[bk]], inc=(k == KCH - 1))
                        p.op("dve", lambda e: e.tensor_tensor(out=x[:, t, nb * 512:(nb + 1) * 512], in0=x[:, t, nb * 512:(nb + 1) * 512],
                                                              in1=ps[bk][:, :], op=ALU.add),
                             reads=[psb[bk], xq[t][nb]], writes=[xq[t][nb]])
                barrier()
            if stop_after == "E%d" % li:
                d = dbg("x", [NTOK, DM])
                p.dma("sp", d.rearrange("(t p) d -> p t d", p=128), x[:], reads=xb, writes=[p.buf("d")])
                return ("E",)
            NP = DFF // GFF
            with SB("xn1", [128, DM], BF16) as xn1, \
                    SB("wu", [128, 3, KCH, GFF], BF16) as wu, \
                    SB("wd", [128, 3, GFF // 128, DM], BF16) as wd, \
                    SB("uT", [128, 4, NTOK], BF16) as uT, \
                    SB("rl", [128, 2, 512], F32) as rl:
                xn1b = p.buf("xn1")
                wub, wdb, rlb = p.bufs("wu", 3), p.bufs("wd", 3), p.bufs("rl", 2)
                uTb = p.buf("uT")

                def load_wu(g):
                    if g < NP:
                        p.dma("pool", wu[:, g % 3], w_up[li, :, g * GFF:(g + 1) * GFF].rearrange("(k p) n -> p k n", p=128), writes=[wub[g % 3]])

                def load_wd(g):
                    if g < NP:
                        p.dma("pool", wd[:, g % 3], w_down[li, g * GFF:(g + 1) * GFF, :].rearrange("(k p) n -> p k n", p=128), writes=[wdb[g % 3]])

                for g in range(3):
                    load_wu(g)
                    load_wd(g)
                for t in range(t_lo, t_hi):
                    norm_tile(x[:, t, :], xb[t], 1 if li == 0 else 4, hT, hTb, t * 128, xn1, xn1b, (0, 1))
                for t in range(t_lo, t_hi):
                    for nb in range(4):
                        xq[t][nb].r.update(xb[t].r)
                rc = 0
                for gp in range(NP // 2):
                    for half in range(2):
                        g = 2 * gp + half
                        s = g % 3
                        for fc in range(2):
                            for (c0, n) in mblks:
                                bk = 2 + rc % 2
                                rs = rc % 2
                                rc += 1
                                for k in range(KCH):
                                    p.op("pe", lambda e: e.matmul(ps[bk][:, :n], lhsT=wu[:, s, k, fc * 128:(fc + 1) * 128], rhs=hT[:, k, c0:c0 + n],
                                                                  start=(k == 0), stop=(k == KCH - 1)),
                                         reads=[wub[s], hTb], writes=[psb[bk]], inc=(k == KCH - 1))
                                p.op("act", lambda e: e.activation(out=rl[:, rs, :n], in_=ps[bk][:, :n], func=AF.Relu), reads=[psb[bk]], writes=[rlb[rs]])
                                p.op("act", lambda e: e.activation(out=uT[:, 2 * half + fc, c0:c0 + n], in_=rl[:, rs, :n], func=AF.Square),
                                     reads=[rlb[rs]], writes=[uTb])
                        load_wu(g + 3)
                    for t in range(t_lo, t_hi):
                        for nb in range(4):
                            bk = 4 + (t * 4 + nb) % 4
                            for j in range(4):
                                g = 2 * gp + j // 2
                                p.op("pe", lambda e: e.matmul(ps[bk][:, :], lhsT=uT[:, j, t * 128:(t + 1) * 128],
                                                              rhs=wd[:, g % 3, j % 2, nb * 512:(nb + 1) * 512],
                                                              start=(j == 0), stop=(j == 3)),
                                     reads=[uTb, wdb[g % 3]], writes=[psb[bk]], inc=(j == 3))
                            p.op("dve", lambda e: e.tensor_tensor(out=x[:, t, nb * 512:(nb + 1) * 512], in0=x[:, t, nb * 512:(nb + 1) * 512],
                                                                  in1=ps[bk][:, :], op=ALU.add),
                                 reads=[psb[bk], xq[t][nb]], writes=[xq[t][nb]])
                    load_wd(2 * gp + 3)
                    load_wd(2 * gp + 4)
                if li == 0:
                    p.dma("sp", X1_d.rearrange("(t p) d -> p t d", p=128), x[:], reads=[b for t in range(NT) for b in xq[t]], writes=[X1_db])
                else:
                    p.dma("sp", y.rearrange("(t p) d -> p t d", p=128), x[:, 1:9, :], reads=[b for t in range(1, 9) for b in xq[t]], writes=[yb])
                barrier()
        return None

    for li in range(2):
        r = layer(li)
        if r is not None:
            break
    p.finish([yb])
    return nc, dbg_out


def _t5_bucket_np(rel):
    import math
    rel = np.asarray(rel, dtype=np.int32)
    half, max_exact = 16, 8
    side = np.where(rel > 0, half, 0)
    n = np.abs(rel)
    n_f = np.maximum(n, 1).astype(np.float32)
    large = max_exact + (np.log(n_f / np.float32(max_exact)) / np.float32(math.log(128 / max_exact))
                         * np.float32(half - max_exact)).astype(np.int32)
    large = np.minimum(large, half - 1)
    return (side + np.where(n < max_exact, n, large)).astype(np.int32)


def _bucket_table(us):
    return _t5_bucket_np(us)


def core_order(r):
    g0t = 8 * r
    order = []
    for kc in range(12):
        gt = g0t - 2 + kc
        order.append(gt if 0 <= gt < 32 else None)
    have = set(t for t in order if t is not None)
    for i in range(32):
        t = (g0t + 10 + i) % 32
        if t not in have:
            order.append(t)
            have.add(t)
    while len(order) < NKC:
        order.append(None)
    assert len(order) == NKC and len(have) == 32
    return order


def prep_inputs(inp):
    f32 = np.float32
    x = np.asarray(inp["x"], f32)
    mem = np.asarray(inp["mem"], f32)
    rel_bias = np.asarray(inp["rel_bias"], f32)
    B = x.shape[0]
    uu = np.arange(-2000, 2001)
    bu = _bucket_table(uu)
    bkt = np.zeros((33, 4), f32)
    for bb in range(32):
        sel = uu[bu == bb]
        if len(sel) == 0:
            lo, hi = 1e9, 1e9
        else:
            lo, hi = float(sel.min()), float(sel.max()) + 1.0
            assert len(sel) == int(hi - lo)
            if sel.min() == uu[0]:
                lo = -1e9
            if sel.max() == uu[-1]:
                hi = 1e9
        bkt[bb, 0], bkt[bb, 1] = lo, hi
        bkt[bb, 2], bkt[bb, 3] = max(lo, -128.0), min(hi, 129.0)
        if bkt[bb, 3] < bkt[bb, 2]:
            bkt[bb, 3] = bkt[bb, 2]
    bkt[32] = (-128.0, 129.0, -128.0, 129.0)
    m0 = np.arange(1280)
    ia0, ib0 = np.maximum(639 - m0, 0), np.maximum(m0 - 639, 0)
    m1 = np.minimum(np.arange(1280), 511)
    ia1, ib1 = np.maximum(255 - m1, 0), np.maximum(m1 - 255, 0)
    positions = np.asarray(inp["positions"]).astype(np.int32)
    cmat = np.zeros((128, 5, 128), f32)
    cmat[:, 0, :] = np.eye(128)
    cmat[:, 1, :] = np.eye(128)[::-1]
    cmat[:64, 2, :64] = 1.0 / 64
    cmat[64:, 2, 64:] = 1.0 / 64
    cmat[:, 3, :] = 1.0 / 128
    cmat[:, 4, :] = 1.0
    identf = np.eye(128, dtype=f32)
    relb = np.concatenate([rel_bias, np.full((1, 12), NEG, f32)], axis=0)

    def col16(v):
        return np.ascontiguousarray(np.asarray(v, f32).reshape(16, 128).T)

    gcols = np.stack([col16(inp["norm_attn"][0]), col16(inp["norm_mlp"][0]), col16(inp["norm_mem"][0]),
                      col16(inp["norm_attn"][1]), col16(inp["norm_mlp"][1]), col16(inp["norm_mem"][1])], axis=1)
    hg = np.stack([np.tile(np.asarray(inp["a_q_norm"][0], f32), 2), np.tile(np.asarray(inp["a_k_norm"][0], f32), 2),
                   np.asarray(inp["a_subln"][0], f32), np.asarray(inp["b_q_norm"][0], f32), np.asarray(inp["b_k_norm"][0], f32),
                   np.asarray(inp["m_q_norm"][0], f32), np.asarray(inp["m_k_norm"][0], f32),
                   np.asarray(inp["m_q_norm"][1], f32), np.asarray(inp["m_k_norm"][1], f32)], axis=1)
    lamv = np.broadcast_to(np.stack([np.asarray(inp[k][0], f32) for k in
                                     ("a_lambda_q1", "a_lambda_k1", "a_lambda_q2", "a_lambda_k2")])[None], (128, 4, 64))
    sinkb = np.broadcast_to(np.asarray(inp["b_sink"][0], f32)[None], (128, 12))
    shared = {
        "w_in_a": np.ascontiguousarray(inp["w_in_a"][0], f32), "w_in_b": np.ascontiguousarray(inp["w_in_b"][0], f32),
        "w_mem_kv": np.ascontiguousarray(inp["w_mem_kv"], f32), "w_out": np.ascontiguousarray(inp["w_out"], f32),
        "w_up": np.ascontiguousarray(inp["w_up"], f32), "w_down": np.ascontiguousarray(inp["w_down"], f32),
        "gcols": np.ascontiguousarray(gcols), "hg": np.ascontiguousarray(hg), "lamv": np.ascontiguousarray(lamv),
        "sinkb": np.ascontiguousarray(sinkb), "relb": relb, "bkt": bkt, "cmat": cmat, "identf": identf,
    }
    maps = []
    zt = np.zeros((128, DM), f32)
    c15, c31 = rel_bias[15], rel_bias[31]
    for b in range(B):
        xt = x[b].reshape(32, 128, DM)
        for r in range(4):
            order = core_order(r)
            g0t = 8 * r
            xfull = np.concatenate([xt[t] if t is not None else zt for t in order], axis=0)
            win = [g0t - 1 + i for i in range(NT)]
            xown = np.concatenate([xt[t] if 0 <= t < 32 else zt for t in win], axis=0)
            cst = np.zeros((12, 3, NKC), f32)
            for qb in range(3):
                n = 512 if qb < 2 else 256
                qt = [win[i] for i in range(4 * qb, min(4 * qb + 4, NT)) if 0 <= win[i] < 32]
                for kc in range(NKC):
                    Dd = (kc - 1) * 128 - qb * 512
                    near = -128 <= Dd <= (512 if n == 512 else 256)
                    gt = order[kc]
                    if gt is None:
                        cst[:, qb, kc] = NEG
                    elif near:
                        cst[:, qb, kc] = 0.0
                    elif gt > max(qt):
                        cst[:, qb, kc] = c31
                    else:
                        assert gt < min(qt)
                        cst[:, qb, kc] = c15
            edge = np.zeros((128, 2), f32)
            if win[0] < 0:
                edge[:, 0] = NEG
            if win[-1] > 31:
                edge[:, 1] = NEG
            pown = positions[b, g0t * 128:(g0t + 8) * 128]
            ppos = np.stack([pown[ia0], pown[ib0], pown[ia1], pown[ib1]], axis=0)
            m = dict(shared)
            m["ppos"] = np.ascontiguousarray(np.broadcast_to(ppos[None], (33, 4, 1280))).astype(np.int32)
            m.update({"xfull": xfull, "xown": xown, "mem": np.ascontiguousarray(mem[b]),
                      "cst0": np.ascontiguousarray(np.broadcast_to(cst[None], (128, 12, 3, NKC))), "edge": edge})
            maps.append(m)
    return maps


_CACHE = {}


def kernel(**inputs):
    maps = prep_inputs(inputs)
    if "nc" not in _CACHE:
        _CACHE["nc"] = build_program()[0]
    nc = _CACHE["nc"]
    res = run_bass_kernel_spmd(nc, maps, core_ids=list(range(8)))
    B = 2
    out = np.zeros((B, 4096, DM), np.float32)
    for b in range(B):
        for r in range(4):
            out[b, r * 1024:(r + 1) * 1024, :] = res.results[b * 4 + r]["y"]
    return out
```

```python
import numpy as np
import concourse.bass as bass
import concourse.mybir as mybir
from concourse.bass_utils import run_bass_kernel_spmd

F32 = mybir.dt.float32
BF16 = mybir.dt.bfloat16
I32 = mybir.dt.int32
AF = mybir.ActivationFunctionType
ALU = mybir.AluOpType
AX = mybir.AxisListType


class Buf:
    __slots__ = ("name", "w", "r")

    def __init__(self, name):
        self.name = name
        self.w = None
        self.r = {}


class Chan:
    __slots__ = ("sem", "key", "total")

    def __init__(self, sem, key):
        self.sem = sem
        self.key = key
        self.total = 0


class Prog:
    ENGS = ("pe", "act", "dve", "pool", "sp")
    NCH = 12

    def __init__(self, nc):
        self.nc = nc
        self.e = {"pe": nc.tensor, "act": nc.scalar, "dve": nc.vector, "pool": nc.gpsimd, "sp": nc.sync}
        self.semobj = {}
        self.cnt = {}
        self.seen = {k: {} for k in self.ENGS}
        for k in self.ENGS:
            self.semobj[k] = nc.alloc_semaphore("s_" + k)
            self.cnt[k] = 0
        self.chans = {}
        self.rr = {}
        for q in ("sp", "pool", "act"):
            n = self.NCH if q != "act" else 4
            self.chans[q] = []
            for i in range(n):
                key = "c_%s%d" % (q, i)
                self.semobj[key] = nc.alloc_semaphore(key)
                self.chans[q].append(Chan(self.semobj[key], key))
            self.rr[q] = 0
        self.nwaits = 0
        self.nins = {k: 0 for k in self.ENGS}

    def buf(self, name):
        return Buf(name)

    def bufs(self, name, n):
        return [Buf("%s%d" % (name, i)) for i in range(n)]

    def _wait(self, eng, tok):
        key, val = tok
        if key == "pe" and eng == "pe":
            return
        if key == "pe":
            assert val <= self.cnt["pe"], "waiting on a pending (un-incremented) PE instruction"
        if self.seen[eng].get(key, 0) >= val:
            return
        self.e[eng].wait_ge(self.semobj[key], val)
        self.seen[eng][key] = val
        self.nwaits += 1

    def _deps(self, eng, reads, writes):
        toks = {}
        for b in reads:
            if b.w is not None:
                k, v = b.w
                toks[k] = max(toks.get(k, 0), v)
        for b in writes:
            if b.w is not None:
                k, v = b.w
                toks[k] = max(toks.get(k, 0), v)
            for k, v in b.r.items():
                toks[k] = max(toks.get(k, 0), v)
        for k, v in toks.items():
            self._wait(eng, (k, v))

    def _mark(self, tok, reads, writes):
        k, v = tok
        for b in reads:
            b.r[k] = max(b.r.get(k, 0), v)
        for b in writes:
            b.w = tok
            b.r = {}

    def op(self, eng, fn, reads=(), writes=(), inc=True, self_sync=True):
        if not self_sync:
            saved = self.seen[eng].get(eng, 0)
            self.seen[eng][eng] = 1 << 60
            self._deps(eng, reads, writes)
            self.seen[eng][eng] = saved
        else:
            self._deps(eng, reads, writes)
        ins = fn(self.e[eng])
        self.nins[eng] += 1
        if inc:
            self.cnt[eng] += 1
            ins.then_inc(self.semobj[eng], 1)
            tok = (eng, self.cnt[eng])
        else:
            assert eng == "pe"
            tok = (eng, self.cnt[eng] + 1)
        self._mark(tok, reads, writes)
        return ins

    def dma(self, q, out, in_, reads=(), writes=(), **kw):
        chs = self.chans[q]
        ch = chs[self.rr[q] % len(chs)]
        self.rr[q] += 1
        if ch.total > 0:
            self._wait(q, (ch.key, ch.total))
        self._deps(q, reads, writes)
        ins = self.e[q].dma_start(out=out, in_=in_, **kw)
        self.nins[q] += 1
        ch.total += 16
        ins.then_inc(ch.sem, 16)
        tok = (ch.key, ch.total)
        self._mark(tok, reads, writes)
        return tok

    def finish(self, out_bufs):
        for b in out_bufs:
            if b.w is not None:
                self._wait("sp", b.w)
        for k in self.ENGS:
            if k != "sp" and self.cnt[k] > 0:
                self._wait("sp", (k, self.cnt[k]))
        for q in self.chans:
            for ch in self.chans[q]:
                if ch.total > 0:
                    self._wait("sp", (ch.key, ch.total))


DM = 2048
KCH = 16
NT = 10
NTOK = NT * 128
NKC = 34
DFF = 8192
GFF = 256
EPS = 1e-6
NEG = -30000.0
S0 = 0.125
S1 = 128.0 ** -0.5
TOKBLKS = [(0, 512), (512, 512), (1024, 256)]


def build_program(stop_after=None, debug=()):
    nc = bass.Bass("TRN2", target_bir_lowering=False)
    p = Prog(nc)
    _uid = [0]

    def _un(name):
        _uid[0] += 1
        return "sb%d_%s" % (_uid[0], name)

    def A(name, shape, dt):
        return nc.alloc_sbuf_tensor(_un(name), shape, dt)

    def SB(name, shape, dt):
        return nc.sbuf_tensor(_un(name), shape, dt)

    def din(name, shape, dt=F32):
        return nc.dram_tensor(name, list(shape), dt, kind="ExternalInput").ap()

    xfull = din("xfull", [NKC * 128, DM])
    xown = din("xown", [NTOK, DM])
    memd = din("mem", [256, DM])
    w_in_a = din("w_in_a", [DM, 5120])
    w_in_b = din("w_in_b", [DM, 3072])
    w_mem_kv = din("w_mem_kv", [2, DM, 1024])
    w_out = din("w_out", [2, DM, DM])
    w_up = din("w_up", [2, DM, DFF])
    w_down = din("w_down", [2, DFF, DM])
    gcols_d = din("gcols", [128, 6, 16])
    hg_d = din("hg", [128, 9])
    lamv_d = din("lamv", [128, 4, 64])
    cst0_d = din("cst0", [128, 12, 3, NKC])
    edge_d = din("edge", [128, 2])
    sink_d = din("sinkb", [128, 12])
    relb_d = din("relb", [33, 12])
    ppos_d = din("ppos", [33, 4, 1280], I32)
    bkt_d = din("bkt", [33, 4])
    cmat_d = din("cmat", [128, 5, 128])
    identf_d = din("identf", [128, 128])
    y = nc.dram_tensor("y", [8 * 128, DM], F32, kind="ExternalOutput").ap()

    kvkind = "ExternalOutput" if "KV" in debug else "Internal"
    KT_d = nc.dram_tensor("KT_d", [12, 128, NKC * 128], BF16, kind=kvkind).ap()
    V_d = nc.dram_tensor("V_d", [12, 128, NKC, 128], BF16, kind=kvkind).ap()
    F0_d = nc.dram_tensor("F0_d", [12, 1280], BF16)
    F1_d = nc.dram_tensor("F1_d", [12, 512], BF16)
    KT_db = p.bufs("KT_d", 12)
    V_db = p.buf("V_d")
    F0_db = p.buf("F0_d")
    F1_db = p.buf("F1_d")
    yb = p.buf("y")
    dbg_out = {}

    def dbg(name, shape, dt=F32):
        t = nc.dram_tensor("dbg_" + name, list(shape), dt, kind="ExternalOutput").ap()
        dbg_out[name] = t
        return t

    gcols = A("gcols", [128, 6, 16], F32); gcolsb = p.buf("gcols")
    hg = A("hg", [128, 9], F32); hgb = p.buf("hg")
    cm = A("cm", [128, 5, 128], BF16); cmb = p.buf("cm")
    identf = A("identf", [128, 128], F32); identfb = p.buf("identf")
    sc = A("sc", [128, 16], F32); scb = p.buf("sc")
    edge = A("edge", [128, 2], F32); edgeb = p.buf("edge")
    esink = A("esink", [128, 12], F32); esinkb = p.buf("esink")
    ssq = A("ssq", [128, 8], F32); ssqb = p.bufs("ssq", 4)
    junk = A("junk", [128, DM], BF16); junkb = p.buf("junk")
    IDB, JM, OD64, O128, ONES = 0, 1, 2, 3, 4

    psd = [nc.alloc_psum_tensor("psd%d" % i, [128, 2, 512], F32) for i in range(4)]
    ps = [psd[i // 2][:, i % 2, :] for i in range(8)]
    psb = p.bufs("ps", 8)
    onesf = A("onesf", [128, 128], F32); onesfb = p.buf("onesf")
    p.op("dve", lambda e: e.memset(onesf[:], 1.0), writes=[onesfb])

    p.dma("sp", gcols[:], gcols_d, writes=[gcolsb])
    p.dma("sp", hg[:], hg_d, writes=[hgb])
    p.dma("sp", identf[:], identf_d, writes=[identfb])
    p.dma("sp", edge[:], edge_d, writes=[edgeb])
    p.dma("sp", esink[:], sink_d, writes=[esinkb])
    p.dma("pool", cm[:], cmat_d, writes=[cmb])
    p.op("act", lambda e: e.activation(out=esink[:], in_=esink[:], func=AF.Exp), reads=[esinkb], writes=[esinkb])

    state = {"ss": 0, "tp": 0}

    def barrier():
        for e in Prog.ENGS:
            for k in Prog.ENGS:
                if k != e and p.cnt[k] > 0:
                    p._wait(e, (k, p.cnt[k]))
            for q in p.chans:
                for ch in p.chans[q]:
                    if ch.total > 0:
                        p._wait(e, (ch.key, ch.total))

    def norm_tile(x_ap, x_buf, gi, dst, dst_buf, col0, xn, xnb, tpbanks):
        s = state["ss"] % 4
        state["ss"] += 1
        ssa = ssq[:, 2 * s:2 * s + 1]
        rsa = ssq[:, 2 * s + 1:2 * s + 2]
        p.op("act", lambda e: e.activation(out=junk[:], in_=x_ap, func=AF.Square, accum_out=ssa),
             reads=[x_buf], writes=[junkb, ssqb[s]])
        p.op("act", lambda e: e.activation(out=rsa, in_=ssa, func=AF.Ln, scale=1.0 / DM, bias=EPS),
             reads=[ssqb[s]], writes=[ssqb[s]])
        p.op("act", lambda e: e.activation(out=rsa, in_=rsa, func=AF.Exp, scale=-0.5),
             reads=[ssqb[s]], writes=[ssqb[s]])
        p.op("dve", lambda e: e.tensor_scalar(out=xn[:], in0=x_ap, scalar1=rsa, scalar2=None, op0=ALU.mult),
             reads=[x_buf, ssqb[s]], writes=[xnb])
        for kb in range(4):
            bi = tpbanks[state["tp"] % len(tpbanks)]
            state["tp"] += 1
            for j in range(4):
                k = kb * 4 + j
                p.op("pe", lambda e: e.transpose(ps[bi].bitcast(BF16)[:, j * 128:(j + 1) * 128], xn[:, k * 128:(k + 1) * 128], cm[:, IDB, :]),
                     reads=[xnb, cmb], writes=[psb[bi]], inc=(j == 3))
            p.op("dve", lambda e: e.tensor_tensor(
                out=dst[:, kb * 4:(kb + 1) * 4, col0:col0 + 128],
                in0=ps[bi].bitcast(BF16)[:, 0:512].rearrange("p (a b) -> p a b", a=4),
                in1=gcols[:, gi, kb * 4:(kb + 1) * 4].unsqueeze(2).to_broadcast([128, 4, 128]),
                op=ALU.mult), reads=[psb[bi], gcolsb], writes=[dst_buf])

    def proj_fm(w, wbuf, j, src, src_buf, c0, n, nm, gcol, dst_ap, dst_buf, tmp, banks, post=None):
        pj = (2, 3, 7)[state.get("pj", 0) % 3]
        state["pj"] = state.get("pj", 0) + 1
        msb = 4
        for k in range(KCH):
            p.op("pe", lambda e: e.matmul(ps[pj][:, :n], lhsT=w[:, k, j * 128:(j + 1) * 128], rhs=src[:, k, c0:c0 + n],
                                          start=(k == 0), stop=(k == KCH - 1)),
                 reads=[wbuf, src_buf], writes=[psb[pj]], inc=(k == KCH - 1))
        sq, sqb, ln, lnb = tmp
        p.op("act", lambda e: e.activation(out=sq[:, :n], in_=ps[pj][:, :n], func=AF.Square), reads=[psb[pj]], writes=[sqb])

        def tail():
            p.op("pe", lambda e: e.matmul(ps[msb][:, :n], lhsT=cm[:, nm, :], rhs=sq[:, :n], start=True, stop=True),
                 reads=[sqb, cmb], writes=[psb[msb]])
            p.op("act", lambda e: e.activation(out=ln[:, :n], in_=ps[msb][:, :n], func=AF.Ln, bias=EPS), reads=[psb[msb]], writes=[lnb])
            p.op("act", lambda e: e.activation(out=ln[:, :n], in_=ln[:, :n], func=AF.Exp, scale=-0.5), reads=[lnb], writes=[lnb])
            p.op("dve", lambda e: e.scalar_tensor_tensor(out=dst_ap, in0=ps[pj][:, :n], scalar=gcol, in1=ln[:, :n],
                                                         op0=ALU.mult, op1=ALU.mult),
                 reads=[psb[pj], lnb, hgb], writes=[dst_buf])
            if post is not None:
                post()

        proj_flush()
        state["tail"] = tail

    def proj_flush():
        t = state.pop("tail", None)
        if t is not None:
            t()

    def wpiece(dst, dst_buf, src2d):
        p.dma("pool", dst, src2d.rearrange("(k p) n -> p k n", p=128), writes=[dst_buf])

    with SB("lamv", [128, 4, 64], F32) as lamv, \
            SB("relb", [33, 12], F32) as relb, \
            SB("relb8", [33, 12], F32) as relb8, \
            SB("oh0", [32, 1280], F32) as oh0, \
            SB("oh1", [33, 512], F32) as oh1, \
            SB("fst", [12, 1280], BF16) as fst:
        lamvb, relbb, relb8b, oh0b, oh1b, fstb = [p.buf(n) for n in ("lamv", "relb", "relb8", "oh0", "oh1", "fst")]
        p.dma("sp", lamv[:], lamv_d, writes=[lamvb])
        p.dma("sp", relb[:], relb_d, writes=[relbb])
        with SB("ppos", [33, 4, 1280], I32) as ppos, SB("bkt", [33, 4], F32) as bkt, SB("urel", [33, 1280], F32) as urel:
            pposb, bktb, urelb = p.buf("ppos"), p.buf("bkt"), p.buf("urel")
            p.dma("sp", ppos[:], ppos_d, writes=[pposb])
            p.dma("sp", bkt[:], bkt_d, writes=[bktb])
            for (oh, ohb, kk, n, ia, c0) in ((oh0, oh0b, 32, 1280, 0, 0), (oh1, oh1b, 33, 512, 2, 2)):
                p.op("dve", lambda e: e.tensor_tensor(out=urel[:kk, :n], in0=ppos[:kk, ia, :n], in1=ppos[:kk, ia + 1, :n], op=ALU.subtract),
                     reads=[pposb], writes=[urelb])
                p.op("dve", lambda e: e.tensor_scalar(out=oh[:kk, :n], in0=urel[:kk, :n], scalar1=bkt[:kk, c0:c0 + 1], scalar2=None, op0=ALU.is_ge),
                     reads=[urelb, bktb], writes=[ohb])
                p.op("dve", lambda e: e.scalar_tensor_tensor(out=oh[:kk, :n], in0=urel[:kk, :n], scalar=bkt[:kk, c0 + 1:c0 + 2], in1=oh[:kk, :n],
                                                             op0=ALU.is_lt, op1=ALU.mult),
                     reads=[urelb, bktb, ohb], writes=[ohb])
            p.op("dve", lambda e: e.tensor_scalar(out=oh1[32:33, :], in0=oh1[32:33, :], scalar1=-1.0, scalar2=1.0, op0=ALU.mult, op1=ALU.add),
                 reads=[oh1b], writes=[oh1b])
            barrier()
        p.op("dve", lambda e: e.tensor_tensor(out=lamv[:, 0, :], in0=lamv[:, 0, :], in1=lamv[:, 1, :], op=ALU.mult),
             reads=[lamvb], writes=[lamvb])
        p.op("dve", lambda e: e.tensor_tensor(out=lamv[:, 2, :], in0=lamv[:, 2, :], in1=lamv[:, 3, :], op=ALU.mult),
             reads=[lamvb], writes=[lamvb])
        p.op("dve", lambda e: e.reduce_sum(out=sc[:, 2:3], in_=lamv[:, 0, :], axis=AX.X), reads=[lamvb], writes=[scb])
        p.op("dve", lambda e: e.reduce_sum(out=sc[:, 3:4], in_=lamv[:, 2, :], axis=AX.X), reads=[lamvb], writes=[scb])
        p.op("act", lambda e: e.activation(out=sc[:, 2:4], in_=sc[:, 2:4], func=AF.Exp), reads=[scb], writes=[scb])
        p.op("dve", lambda e: e.tensor_tensor(out=sc[:, 0:1], in0=sc[:, 3:4], in1=sc[:, 2:3], op=ALU.subtract),
             reads=[scb], writes=[scb])
        p.op("dve", lambda e: e.tensor_scalar(out=sc[:, 0:1], in0=sc[:, 0:1], scalar1=-0.2, scalar2=None, op0=ALU.add),
             reads=[scb], writes=[scb])
        p.op("dve", lambda e: e.tensor_scalar(out=sc[:, 1:2], in0=hg[:, 2:3], scalar1=0.8, scalar2=None, op0=ALU.mult),
             reads=[scb, hgb], writes=[scb])
        for (scale, oh, ohb, n, Fd, Fdb, kk) in ((1.0 / S0, oh0, oh0b, 1280, F0_d, F0_db, 32), (1.0 / S1, oh1, oh1b, 512, F1_d, F1_db, 33)):
            p.op("act", lambda e: e.mul(out=relb8[:], in_=relb[:], mul=scale), reads=[relbb], writes=[relb8b])
            for c0 in range(0, n, 512):
                nn = min(512, n - c0)
                p.op("pe", lambda e: e.matmul(ps[0][:12, :nn], lhsT=relb8[:kk, :], rhs=oh[:kk, c0:c0 + nn], start=True, stop=True),
                     reads=[relb8b, ohb], writes=[psb[0]])
                p.op("act", lambda e: e.copy(out=fst[:, c0:c0 + nn], in_=ps[0][:12, :nn]), reads=[psb[0]], writes=[fstb])
            p.dma("sp", Fd.ap()[:, :n], fst[:, :n], reads=[fstb], writes=[Fdb])
        barrier()
    if "F0" in debug:
        pass

    def phase_a():
        with SB("wk", [128, KCH, 1536], BF16) as wk, \
                SB("wv", [128, KCH, 1536], BF16) as wv, \
                SB("xt", [128, 2, DM], F32) as xt, \
                SB("xn", [128, 2, DM], BF16) as xn_, \
                SB("hTc", [128, 2, KCH, 512], BF16) as hTc, \
                SB("kst", [128, 2, 512], BF16) as kst, \
                SB("vst", [128, 2, 12, 4, 128], BF16) as vst, \
                SB("sq", [128, 2, 512], BF16) as sq, \
                SB("ln", [128, 2, 512], F32) as ln:
            wkb, wvb = p.bufs("wk", 3), p.bufs("wv", 3)
            xtb, xnb, hTcb, kstb, vstb, sqb, lnb = (p.bufs("xt", 2), p.bufs("xn", 2), p.bufs("hTc", 2), p.bufs("kst", 2),
                                                    p.bufs("vst", 2), p.bufs("sq", 2), p.bufs("ln", 2))
            for i in range(3):
                wpiece(wk[:, :, i * 512:(i + 1) * 512], wkb[i], w_in_a[:, 1536 + i * 512:1536 + (i + 1) * 512])
            for i in range(3):
                wpiece(wv[:, :, i * 512:(i + 1) * 512], wvb[i], w_in_a[:, 3072 + i * 512:3072 + (i + 1) * 512])
            blocks = [(t0, min(4, NKC - t0)) for t0 in range(0, NKC, 4)]
            ti = 0
            hk = 0
            for bi_, (t0, nt) in enumerate(blocks):
                hs = bi_ % 2
                n = nt * 128
                for i in range(nt):
                    s = ti % 2
                    p.dma("sp", xt[:, s, :], xfull[(t0 + i) * 128:(t0 + i + 1) * 128, :], writes=[xtb[s]])
                    norm_tile(xt[:, s, :], xtb[s], 0, hTc[:, hs], hTcb[hs], i * 128, xn_[:, ti % 2, :], xnb[ti % 2], (0, 1))
                    ti += 1
                for h in range(12):
                    s2 = hk % 2
                    hk += 1
                    def store(h=h, s2=s2, t0=t0, n=n):
                        p.dma("pool", KT_d[h, :, t0 * 128:t0 * 128 + n], kst[:, s2, :n], reads=[kstb[s2]], writes=[KT_db[h]])
                    proj_fm(wk, wkb[h // 4], h, hTc[:, hs], hTcb[hs], 0, n, OD64, hg[:, 1:2],
                            kst[:, s2, :n], kstb[s2], (sq[:, s2, :], sqb[s2], ln[:, s2, :], lnb[s2]), (2 + s2, 4 if s2 == 0 else 7), post=store)
                for i in range(nt):
                    for nb in range(3):
                        if i == 0 and nb == 1:
                            proj_flush()
                        bk = 5 + (i * 3 + nb) % 2
                        for k in range(KCH):
                            p.op("pe", lambda e: e.matmul(ps[bk][:, :], lhsT=hTc[:, hs, k, i * 128:(i + 1) * 128],
                                                          rhs=wv[:, k, nb * 512:(nb + 1) * 512], start=(k == 0), stop=(k == KCH - 1)),
                                 reads=[hTcb[hs], wvb[nb]], writes=[psb[bk]], inc=(k == KCH - 1))
                        p.op("act", lambda e: e.copy(out=vst[:, hs, nb * 4:(nb + 1) * 4, i, :],
                                                     in_=ps[bk][:, :].rearrange("p (a b) -> p a b", a=4)),
                             reads=[psb[bk]], writes=[vstb[hs]])
                p.dma("pool", V_d.rearrange("h p c e -> p h c e")[:, :, t0:t0 + nt, :], vst[:, hs, :, :nt, :],
                      reads=[vstb[hs]], writes=[V_db])
            barrier()

    phase_a()
    if stop_after == "A":
        p.finish([])
        return nc, dbg_out

    hT = A("hT", [128, KCH, NTOK], BF16)
    hTb = p.buf("hT")
    X1_d = nc.dram_tensor("X1_d", [NTOK, DM], F32).ap()
    X1_db = p.buf("X1_d")

    def layer(li):
        xsrc = xown if li == 0 else X1_d
        xsrcb = p.buf("xsrc") if li == 0 else X1_db
        w_in = w_in_a if li == 0 else w_in_b
        with SB("QT%d" % li, [128, 16, NTOK], BF16) as QT, \
                SB("KmT%d" % li, [128, 4, 256], BF16) as KmT, \
                SB("Vm%d" % li, [128, 2, 512], BF16) as Vm, \
                SB("KT1%d" % li, [128, 4, NTOK if li == 1 else 2], BF16) as KT1, \
                SB("V1%d" % li, [128, NT, 512 if li == 1 else 2], BF16) as V1:
            QTb, KmTb, Vmb, KT1b, V1b = p.buf("QT"), p.buf("KmT"), p.buf("Vm"), p.buf("KT1"), p.buf("V1")
            with SB("xt", [128, 2, DM], F32) as xt, \
                    SB("xn", [128, 2, DM], BF16) as xn_, \
                    SB("wq", [128, 2, KCH, 512], BF16) as wq, \
                    SB("mnT", [128, KCH, 256], BF16) as mnT, \
                    SB("sq", [128, 2, 512], BF16) as sq, \
                    SB("ln", [128, 2, 512], F32) as ln:
                xtb, xnb, wqb, sqb, lnb = p.bufs("xt", 2), p.bufs("xn", 2), p.bufs("wq", 2), p.bufs("sq", 2), p.bufs("ln", 2)
                mnTb = p.buf("mnT")
                wi = [0]

                def next_w(src2d):
                    s = wi[0] % 2
                    wi[0] += 1
                    wpiece(wq[:, s], wqb[s], src2d)
                    return wq[:, s], wqb[s]

                pieces = []
                if li == 0:
                    for i in range(3):
                        pieces.append(("q", i, w_in[:, i * 512:(i + 1) * 512]))
                    pieces.append(("qm", 0, w_in[:, 4608:5120]))
                else:
                    for i in range(3):
                        pieces.append(("q", i, w_in[:, i * 512:(i + 1) * 512]))
                    pieces.append(("k", 0, w_in[:, 1536:2048]))
                    pieces.append(("v", 0, w_in[:, 2048:2560]))
                    pieces.append(("qm", 0, w_in[:, 2560:3072]))
                pieces.append(("mk", 0, w_mem_kv[li, :, 0:512]))
                pieces.append(("mv", 0, w_mem_kv[li, :, 512:1024]))
                loaded = [next_w(pieces[0][2]), None]
                for t in range(NT):
                    s = t % 2
                    p.dma("sp", xt[:, s, :], xsrc[t * 128:(t + 1) * 128, :], reads=[xsrcb], writes=[xtb[s]])
                    norm_tile(xt[:, s, :], xtb[s], 0 if li == 0 else 3, hT, hTb, t * 128, xn_[:, s, :], xnb[s], (0, 1))
                for t in range(2):
                    s = t % 2
                    p.dma("sp", xt[:, s, :], memd[t * 128:(t + 1) * 128, :], writes=[xtb[s]])
                    norm_tile(xt[:, s, :], xtb[s], 2 if li == 0 else 5, mnT, mnTb, t * 128, xn_[:, s, :], xnb[s], (0, 1))
                cnt = 0
                for pi, (kind, idx, src2d) in enumerate(pieces):
                    w, wb = loaded[pi % 2]
                    if pi + 1 < len(pieces):
                        loaded[(pi + 1) % 2] = next_w(pieces[pi + 1][2])
                    if kind in ("q", "qm", "k", "mk"):
                        for j in range(4):
                            if kind == "q":
                                fc = idx * 4 + j
                                nm, gc = (OD64, hg[:, 0:1]) if li == 0 else (O128, hg[:, 3:4])
                                dst, dstb, src, srcb, blks = QT, QTb, hT, hTb, TOKBLKS
                            elif kind == "qm":
                                fc = 12 + j
                                nm, gc = O128, (hg[:, 5:6] if li == 0 else hg[:, 7:8])
                                dst, dstb, src, srcb, blks = QT, QTb, hT, hTb, TOKBLKS
                            elif kind == "k":
                                fc = j
                                nm, gc = O128, hg[:, 4:5]
                                dst, dstb, src, srcb, blks = KT1, KT1b, hT, hTb, TOKBLKS
                            else:
                                fc = j
                                nm, gc = O128, (hg[:, 6:7] if li == 0 else hg[:, 8:9])
                                dst, dstb, src, srcb, blks = KmT, KmTb, mnT, mnTb, [(0, 256)]
                            for (c0, n) in blks:
                                s2 = cnt % 2
                                cnt += 1
                                proj_fm(w, wb, j, src, srcb, c0, n, nm, gc, dst[:, fc, c0:c0 + n], dstb,
                                        (sq[:, s2, :], sqb[s2], ln[:, s2, :], lnb[s2]), (2 + s2, 4 if s2 == 0 else 7))
                    else:
                        if kind == "v":
                            src, srcb, ntl, dst, dstb = hT, hTb, NT, V1, V1b
                        else:
                            src, srcb, ntl, dst, dstb = mnT, mnTb, 2, Vm, Vmb
                        for i in range(ntl):
                            if i == 1:
                                proj_flush()
                            bk = 5 + i % 2
                            for k in range(KCH):
                                p.op("pe", lambda e: e.matmul(ps[bk][:, :], lhsT=src[:, k, i * 128:(i + 1) * 128],
                                                              rhs=w[:, k, :], start=(k == 0), stop=(k == KCH - 1)),
                                     reads=[srcb, wb], writes=[psb[bk]], inc=(k == KCH - 1))
                            p.op("act", lambda e: e.copy(out=dst[:, i, :], in_=ps[bk][:, :]), reads=[psb[bk]], writes=[dstb])
                proj_flush()
                barrier()
            if stop_after == "B%d" % li:
                return ("B", QT, KmT, Vm, KT1, V1)
            with SB("eT", [128, 4, 512], BF16) as eT, \
                    SB("pp", [128, 6, 512], F32) as pp, \
                    SB("sqb", [128, 512], BF16) as sqh:
                eTb = p.bufs("eT", 4)
                ppb = p.bufs("pp", 6)
                sqhb = p.buf("sqh")
                ring = (0, 1, 2)
                OB0, OB1, ZB0, ZB1, MSB = 6, 7, 4, 5, 3
                ecnt = [0]

                def mem_attn():
                    blk = 0
                    for hm in range(4):
                        for (q0, n) in TOKBLKS:
                            ob, zb, pi = OB0 + blk % 2, ZB0 + blk % 2, 2 + blk % 2
                            blk += 1
                            for mc in range(2):
                                i = ecnt[0]
                                ecnt[0] += 1
                                bk = ring[i % 3]
                                es = i % 4
                                p.op("pe", lambda e: e.matmul(ps[bk][:, :n], lhsT=KmT[:, hm, mc * 128:(mc + 1) * 128],
                                                              rhs=QT[:, 12 + hm, q0:q0 + n], start=True, stop=True),
                                     reads=[KmTb, QTb], writes=[psb[bk]])
                                p.op("act", lambda e: e.activation(out=eT[:, es, :n], in_=ps[bk][:, :n], func=AF.Exp, scale=S1),
                                     reads=[psb[bk]], writes=[eTb[es]])
                                p.op("pe", lambda e: e.matmul(ps[ob][:, :n], lhsT=Vm[:, mc, hm * 128:(hm + 1) * 128], rhs=eT[:, es, :n],
                                                              start=(mc == 0), stop=(mc == 1)),
                                     reads=[Vmb, eTb[es]], writes=[psb[ob]], inc=False)
                                p.op("pe", lambda e: e.matmul(ps[zb][:, :n], lhsT=cm[:, ONES, :], rhs=eT[:, es, :n],
                                                              start=(mc == 0), stop=(mc == 1)),
                                     reads=[cmb, eTb[es]], writes=[psb[zb]])
                            p.op("act", lambda e: e.activation(out=pp[:, pi, :n], in_=ps[zb][:, :n], func=AF.Ln), reads=[psb[zb]], writes=[ppb[pi]])
                            p.op("act", lambda e: e.activation(out=pp[:, pi, :n], in_=pp[:, pi, :n], func=AF.Exp, scale=-1.0), reads=[ppb[pi]], writes=[ppb[pi]])
                            p.op("dve", lambda e: e.tensor_tensor(out=hT[:, 12 + hm, q0:q0 + n], in0=ps[ob][:, :n], in1=pp[:, pi, :n], op=ALU.mult),
                                 reads=[psb[ob], ppb[pi]], writes=[hTb])

                if li == 0:
                    with SB("KTh", [128, 2, NKC * 128], BF16) as KTh, \
                            SB("Vh", [128, 2, NKC, 128], BF16) as Vh, \
                            SB("Tb", [128, 2, 6, 512], BF16) as Tb, \
                            SB("cst0", [128, 12, 3, NKC], F32) as cst0, \
                            SB("eT2", [128, 4, 2, 512], BF16) as eT2, \
                            SB("zacc", [128, 2, 2, 512], F32) as zacc:
                        KThb, Vhb, Tbb = p.bufs("KTh", 2), p.bufs("Vh", 2), p.bufs("Tb", 2)
                        cst0b = p.buf("cst0")
                        eT2b = p.bufs("eT2", 4)
                        zaccb = p.bufs("zacc", 2)
                        p.dma("sp", cst0[:], cst0_d, writes=[cst0b])

                        def load_head(h):
                            s = h % 2
                            p.dma("sp", KTh[:, s, :], KT_d[h], reads=[KT_db[h]], writes=[KThb[s]])
                            p.dma("sp", Vh[:, s], V_d[h], reads=[V_db], writes=[Vhb[s]])
                            p.dma("sp", Tb[:, s], bass.AP(F0_d, h * 1280, [[1, 128], [128, 6], [1, 512]]),
                                  reads=[F0_db], writes=[Tbb[s]])

                        load_head(0)
                        kcnt = 0
                        for h in range(12):
                            s = h % 2
                            if h + 1 < 12:
                                load_head(h + 1)
                            for qb, (q0, n) in enumerate(TOKBLKS):
                                pend = []

                                def pv(kc, es):
                                    for c in (0, 1):
                                        p.op("pe", lambda e: e.matmul(ps[OB0 + c][:, :n], lhsT=Vh[:, s, kc, :], rhs=eT2[:, es, c, :n],
                                                                      start=(kc == 0), stop=(kc == NKC - 1)),
                                             reads=[Vhb[s], eT2b[es]], writes=[psb[OB0 + c]], inc=(c == 1))

                                first = {"dve": True, "pool": True}
                                for kc in range(NKC):
                                    r = kcnt % 3
                                    es = kcnt % 4
                                    kcnt += 1
                                    Dd = (kc - 1) * 128 - qb * 512
                                    near = (-128 <= Dd <= (512 if n == 512 else 256))
                                    di = (512 - Dd) // 128
                                    for c in (0, 1):
                                        bk = 2 * r + c
                                        p.op("pe", lambda e: e.matmul(ps[bk][:, :n], lhsT=KTh[64 * c:64 * c + 64, s, kc * 128:(kc + 1) * 128],
                                                                      rhs=QT[64 * c:64 * c + 64, h, q0:q0 + n], start=True, stop=(not near)),
                                             reads=[KThb[s], QTb], writes=[psb[bk]], inc=(c == 1 and not near))
                                    if near:
                                        for c in (0, 1):
                                            bk = 2 * r + c
                                            p.op("pe", lambda e: e.matmul(ps[bk][:, :n], lhsT=cm[:, JM, :], rhs=Tb[:, s, di, :n],
                                                                          start=False, stop=True),
                                                 reads=[cmb, Tbb[s]], writes=[psb[bk]], inc=(c == 1))
                                    p.op("act", lambda e: e.activation(out=eT2[:, es, :, :n], in_=psd[r][:, :, :n], func=AF.Exp,
                                                                       scale=S0, bias=cst0[:, h, qb, kc:kc + 1]),
                                         reads=[psb[2 * r], psb[2 * r + 1], cst0b], writes=[eT2b[es]])
                                    eng = "dve"
                                    zi = 1 if eng == "pool" else 0
                                    if first[eng]:
                                        first[eng] = False
                                        p.op(eng, lambda e: e.tensor_copy(out=zacc[:, zi, :, :n], in_=eT2[:, es, :, :n]),
                                             reads=[eT2b[es]], writes=[zaccb[zi]])
                                    else:
                                        p.op(eng, lambda e: e.tensor_tensor(out=zacc[:, zi, :, :n], in0=zacc[:, zi, :, :n], in1=eT2[:, es, :, :n], op=ALU.add),
                                             reads=[eT2b[es]], writes=[zaccb[zi]], self_sync=False)
                                    pend.append((kc, es))
                                    if len(pend) > 2:
                                        pv(*pend.pop(0))
                                for a in pend:
                                    pv(*a)
                                for c in (0, 1):
                                    p.op("pe", lambda e: e.matmul(ps[c][:, :n], lhsT=onesf[:], rhs=zacc[:, 0, c, :n], start=True, stop=True),
                                         reads=[onesfb, zaccb[0]], writes=[psb[c]])
                                    p.op("act", lambda e: e.activation(out=pp[:, 2 + c, :n], in_=ps[c][:, :n], func=AF.Ln), reads=[psb[c]], writes=[ppb[2 + c]])
                                    p.op("act", lambda e: e.activation(out=pp[:, 2 + c, :n], in_=pp[:, 2 + c, :n], func=AF.Exp, scale=-1.0),
                                         reads=[ppb[2 + c]], writes=[ppb[2 + c]])
                                    p.op("dve", lambda e: e.tensor_tensor(out=pp[:, c, :n], in0=ps[OB0 + c][:, :n], in1=pp[:, 2 + c, :n], op=ALU.mult),
                                         reads=[psb[OB0 + c], ppb[2 + c]], writes=[ppb[c]])
                                p.op("dve", lambda e: e.scalar_tensor_tensor(out=pp[:, 0, :n], in0=pp[:, 1, :n], scalar=sc[:, 0:1], in1=pp[:, 0, :n],
                                                                             op0=ALU.mult, op1=ALU.add),
                                     reads=[ppb[0], ppb[1], scb], writes=[ppb[0]])
                                p.op("dve", lambda e: e.tensor_tensor(out=sqh[:, :n], in0=pp[:, 0, :n], in1=pp[:, 0, :n], op=ALU.mult),
                                     reads=[ppb[0]], writes=[sqhb])
                                p.op("pe", lambda e: e.matmul(ps[2][:, :n], lhsT=cm[:, O128, :], rhs=sqh[:, :n], start=True, stop=True),
                                     reads=[cmb, sqhb], writes=[psb[2]])
                                p.op("act", lambda e: e.activation(out=pp[:, 4, :n], in_=ps[2][:, :n], func=AF.Ln, bias=EPS),
                                     reads=[psb[2]], writes=[ppb[4]])
                                p.op("act", lambda e: e.activation(out=pp[:, 4, :n], in_=pp[:, 4, :n], func=AF.Exp, scale=-0.5),
                                     reads=[ppb[4]], writes=[ppb[4]])
                                p.op("dve", lambda e: e.scalar_tensor_tensor(out=hT[:, h, q0:q0 + n], in0=pp[:, 0, :n], scalar=sc[:, 1:2], in1=pp[:, 4, :n],
                                                                             op0=ALU.mult, op1=ALU.mult),
                                     reads=[ppb[0], ppb[4], scb], writes=[hTb])
                        mem_attn()
                        barrier()
                else:
                    with SB("Tb1", [128, 12, 3, 128], BF16) as Tb1:
                        Tb1b = p.buf("Tb1")
                        for h in range(12):
                            p.dma("sp", Tb1[:, h], bass.AP(F1_d, h * 512, [[1, 128], [128, 3], [1, 128]]), reads=[F1_db], writes=[Tb1b])
                        blk1 = 0
                        for t in range(1, 9):
                            for g in range(4):
                                ob, zb, pi = OB0 + blk1 % 2, ZB0 + blk1 % 2, 2 + blk1 % 2
                                blk1 += 1
                                js = (t - 1, t, t + 1)
                                for jj, j in enumerate(js):
                                    i = ecnt[0]
                                    ecnt[0] += 1
                                    bk = ring[i % 3]
                                    es = i % 4
                                    dj = 1 - (j - t)
                                    o3 = ps[bk][:, :384].rearrange("p (a b) -> p a b", a=3)
                                    p.op("pe", lambda e: e.matmul(o3, lhsT=KT1[:, g, j * 128:(j + 1) * 128],
                                                                  rhs=QT[:, 3 * g:3 * g + 3, t * 128:(t + 1) * 128], start=True, stop=False),
                                         reads=[KT1b, QTb], writes=[psb[bk]], inc=False)
                                    p.op("pe", lambda e: e.matmul(o3, lhsT=cm[:, JM, :], rhs=Tb1[:, 3 * g:3 * g + 3, dj, :], start=False, stop=True),
                                         reads=[cmb, Tb1b], writes=[psb[bk]])
                                    if t == 1 and j == 0:
                                        bias = edge[:, 0:1]
                                    elif t == 8 and j == 9:
                                        bias = edge[:, 1:2]
                                    else:
                                        bias = 0.0
                                    p.op("act", lambda e: e.activation(out=eT[:, es, :384], in_=ps[bk][:, :384], func=AF.Exp, scale=S1, bias=bias),
                                         reads=[psb[bk], edgeb], writes=[eTb[es]])
                                    p.op("pe", lambda e: e.matmul(ps[ob][:, :384], lhsT=V1[:, j, g * 128:(g + 1) * 128], rhs=eT[:, es, :384],
                                                                  start=(jj == 0), stop=(jj == 2)),
                                         reads=[V1b, eTb[es]], writes=[psb[ob]], inc=False)
                                    p.op("pe", lambda e: e.matmul(ps[zb][:, :384], lhsT=cm[:, ONES, :], rhs=eT[:, es, :384],
                                                                  start=(jj == 0), stop=(jj == 2)),
                                         reads=[cmb, eTb[es]], writes=[psb[zb]])
                                p.op("dve", lambda e: e.tensor_tensor(out=pp[:, pi, :384].rearrange("p (a b) -> p a b", a=3),
                                                                      in0=ps[zb][:, :384].rearrange("p (a b) -> p a b", a=3),
                                                                      in1=esink[:, 3 * g:3 * g + 3].unsqueeze(2).to_broadcast([128, 3, 128]), op=ALU.add),
                                     reads=[psb[zb], esinkb], writes=[ppb[pi]])
                                p.op("act", lambda e: e.activation(out=pp[:, pi, :384], in_=pp[:, pi, :384], func=AF.Ln), reads=[ppb[pi]], writes=[ppb[pi]])
                                p.op("act", lambda e: e.activation(out=pp[:, pi, :384], in_=pp[:, pi, :384], func=AF.Exp, scale=-1.0), reads=[ppb[pi]], writes=[ppb[pi]])
                                p.op("dve", lambda e: e.tensor_tensor(out=hT[:, 3 * g:3 * g + 3, t * 128:(t + 1) * 128],
                                                                      in0=ps[ob][:, :384].rearrange("p (a b) -> p a b", a=3),
                                                                      in1=pp[:, pi, :384].rearrange("p (a b) -> p a b", a=3), op=ALU.mult),
                                     reads=[psb[ob], ppb[pi]], writes=[hTb])
                        mem_attn()
                        barrier()
        if stop_after == "D%d" % li:
            d = dbg("hT", [128, KCH, NTOK], BF16)
            p.dma("sp", d, hT[:], reads=[hTb], writes=[p.buf("d")])
            return ("D",)
        t_lo, t_hi = (0, NT) if li == 0 else (1, 9)
        mblks = TOKBLKS if li == 0 else [(128, 512), (640, 512)]
        with SB("x%d" % li, [128, NT, DM], F32) as x:
            xb = p.bufs("x", NT)
            xq = [p.bufs("xq%d_" % t, 4) for t in range(NT)]
            with SB("wo", [128, 2, KCH, 512], BF16) as wo:
                wob = p.bufs("wo", 2)
                wpiece(wo[:, 0], wob[0], w_out[li, :, 0:512])
                for t in range(t_lo, t_hi):
                    p.dma("sp", x[:, t, :], xsrc[t * 128:(t + 1) * 128, :], reads=[xsrcb], writes=[xb[t]] + xq[t])
                for nb in range(4):
                    s = nb % 2
                    if nb + 1 < 4:
                        wpiece(wo[:, (nb + 1) % 2], wob[(nb + 1) % 2], w_out[li, :, (nb + 1) * 512:(nb + 2) * 512])
                    for t in range(t_lo, t_hi):
                        bk = t % 4
                        for k in range(KCH):
                            p.op("pe", lambda e: e.matmul(ps[bk][:, :], lhsT=hT[:, k, t * 128:(t + 1) * 128], rhs=wo[:, s, k, :],
                                                          start=(k == 0), stop=(k == KCH - 1)),
                                 reads=[hTb, wob[s]], writes=[psb[bk]], inc=(k == KCH - 1))
                        p.op("dve", lambda e: e.tensor_tensor(out=x[:, t, nb * 512:(nb + 1) * 512], in0=x[:, t, nb * 512:(nb + 1) * 512],
                                                              in1=ps[bk][:, :], op=ALU.add),
                             reads=[psb[bk], xq[t][nb]], writes=[xq[t][nb]])
                barrier()
            if stop_after == "E%d" % li:
                d = dbg("x", [NTOK, DM])
                p.dma("sp", d.rearrange("(t p) d -> p t d", p=128), x[:], reads=xb, writes=[p.buf("d")])
                return ("E",)
            NP = DFF // GFF
            with SB("xn1", [128, DM], BF16) as xn1, \
                    SB("wu", [128, 3, KCH, GFF], BF16) as wu, \
                    SB("wd", [128, 3, GFF // 128, DM], BF16) as wd, \
                    SB("uT", [128, 4, NTOK], BF16) as uT, \
                    SB("rl", [128, 2, 512], F32) as rl:
                xn1b = p.buf("xn1")
                wub, wdb, rlb = p.bufs("wu", 3), p.bufs("wd", 3), p.bufs("rl", 2)
                uTb = p.buf("uT")

                def load_wu(g):
                    if g < NP:
                        p.dma("pool", wu[:, g % 3], w_up[li, :, g * GFF:(g + 1) * GFF].rearrange("(k p) n -> p k n", p=128), writes=[wub[g % 3]])

                def load_wd(g):
                    if g < NP:
                        p.dma("pool", wd[:, g % 3], w_down[li, g * GFF:(g + 1) * GFF, :].rearrange("(k p) n -> p k n", p=128), writes=[wdb[g % 3]])

                for g in range(3):
                    load_wu(g)
                    load_wd(g)
                for t in range(t_lo, t_hi):
                    norm_tile(x[:, t, :], xb[t], 1 if li == 0 else 4, hT, hTb, t * 128, xn1, xn1b, (0, 1))
                for t in range(t_lo, t_hi):
                    for nb in range(4):
                        xq[t][nb].r.update(xb[t].r)
                rc = 0
                for gp in range(NP // 2):
                    for half in range(2):
                        g = 2 * gp + half
                        s = g % 3
                        for fc in range(2):
                            for (c0, n) in mblks:
                                bk = 2 + rc % 2
                                rs = rc % 2
                                rc += 1
                                for k in range(KCH):
                                    p.op("pe", lambda e: e.matmul(ps[bk][:, :n], lhsT=wu[:, s, k, fc * 128:(fc + 1) * 128], rhs=hT[:, k, c0:c0 + n],
                                                                  start=(k == 0), stop=(k == KCH - 1)),
                                         reads=[wub[s], hTb], writes=[psb[bk]], inc=(k == KCH - 1))
                                p.op("act", lambda e: e.activation(out=rl[:, rs, :n], in_=ps[bk][:, :n], func=AF.Relu), reads=[psb[bk]], writes=[rlb[rs]])
                                p.op("act", lambda e: e.activation(out=uT[:, 2 * half + fc, c0:c0 + n], in_=rl[:, rs, :n], func=AF.Square),
                                     reads=[rlb[rs]], writes=[uTb])
                        load_wu(g + 3)
                    for t in range(t_lo, t_hi):
                        for nb in range(4):
                            bk = 4 + (t * 4 + nb) % 4
                            for j in range(4):
                                g = 2 * gp + j // 2
                                p.op("pe", lambda e: e.matmul(ps[bk][:, :], lhsT=uT[:, j, t * 128:(t + 1) * 128],
                                                              rhs=wd[:, g % 3, j % 2, nb * 512:(nb + 1) * 512],
                                                              start=(j == 0), stop=(j == 3)),
                                     reads=[uTb, wdb[g % 3]], writes=[psb[bk]], inc=(j == 3))
                            p.op("dve", lambda e: e.tensor_tensor(out=x[:, t, nb * 512:(nb + 1) * 512], in0=x[:, t, nb * 512:(nb + 1) * 512],
                                                                  in1=ps[bk][:, :], op=ALU.add),
                                 reads=[psb[bk], xq[t][nb]], writes=[xq[t][nb]])
                    load_wd(2 * gp + 3)
                    load_wd(2 * gp + 4)
                if li == 0:
                    p.dma("sp", X1_d.rearrange("(t p) d -> p t d", p=128), x[:], reads=[b for t in range(NT) for b in xq[t]], writes=[X1_db])
                else:
                    p.dma("sp", y.rearrange("(t p) d -> p t d", p=128), x[:, 1:9, :], reads=[b for t in range(1, 9) for b in xq[t]], writes=[yb])
                barrier()
        return None

    for li in range(2):
        r = layer(li)
        if r is not None:
            break
    p.finish([yb])
    return nc, dbg_out


def _t5_bucket_np(rel):
    import math
    rel = np.asarray(rel, dtype=np.int32)
    half, max_exact = 16, 8
    side = np.where(rel > 0, half, 0)
    n = np.abs(rel)
    n_f = np.maximum(n, 1).astype(np.float32)
    large = max_exact + (np.log(n_f / np.float32(max_exact)) / np.float32(math.log(128 / max_exact))
                         * np.float32(half - max_exact)).astype(np.int32)
    large = np.minimum(large, half - 1)
    return (side + np.where(n < max_exact, n, large)).astype(np.int32)


def _bucket_table(us):
    return _t5_bucket_np(us)


def core_order(r):
    g0t = 8 * r
    order = []
    for kc in range(12):
        gt = g0t - 2 + kc
        order.append(gt if 0 <= gt < 32 else None)
    have = set(t for t in order if t is not None)
    for i in range(32):
        t = (g0t + 10 + i) % 32
        if t not in have:
            order.append(t)
            have.add(t)
    while len(order) < NKC:
        order.append(None)
    assert len(order) == NKC and len(have) == 32
    return order


def prep_inputs(inp):
    f32 = np.float32
    x = np.asarray(inp["x"], f32)
    mem = np.asarray(inp["mem"], f32)
    rel_bias = np.asarray(inp["rel_bias"], f32)
    B = x.shape[0]
    uu = np.arange(-2000, 2001)
    bu = _bucket_table(uu)
    bkt = np.zeros((33, 4), f32)
    for bb in range(32):
        sel = uu[bu == bb]
        if len(sel) == 0:
            lo, hi = 1e9, 1e9
        else:
            lo, hi = float(sel.min()), float(sel.max()) + 1.0
            assert len(sel) == int(hi - lo)
            if sel.min() == uu[0]:
                lo = -1e9
            if sel.max() == uu[-1]:
                hi = 1e9
        bkt[bb, 0], bkt[bb, 1] = lo, hi
        bkt[bb, 2], bkt[bb, 3] = max(lo, -128.0), min(hi, 129.0)
        if bkt[bb, 3] < bkt[bb, 2]:
            bkt[bb, 3] = bkt[bb, 2]
    bkt[32] = (-128.0, 129.0, -128.0, 129.0)
    m0 = np.arange(1280)
    ia0, ib0 = np.maximum(639 - m0, 0), np.maximum(m0 - 639, 0)
    m1 = np.minimum(np.arange(1280), 511)
    ia1, ib1 = np.maximum(255 - m1, 0), np.maximum(m1 - 255, 0)
    positions = np.asarray(inp["positions"]).astype(np.int32)
    cmat = np.zeros((128, 5, 128), f32)
    cmat[:, 0, :] = np.eye(128)
    cmat[:, 1, :] = np.eye(128)[::-1]
    cmat[:64, 2, :64] = 1.0 / 64
    cmat[64:, 2, 64:] = 1.0 / 64
    cmat[:, 3, :] = 1.0 / 128
    cmat[:, 4, :] = 1.0
    identf = np.eye(128, dtype=f32)
    relb = np.concatenate([rel_bias, np.full((1, 12), NEG, f32)], axis=0)

    def col16(v):
        return np.ascontiguousarray(np.asarray(v, f32).reshape(16, 128).T)

    gcols = np.stack([col16(inp["norm_attn"][0]), col16(inp["norm_mlp"][0]), col16(inp["norm_mem"][0]),
                      col16(inp["norm_attn"][1]), col16(inp["norm_mlp"][1]), col16(inp["norm_mem"][1])], axis=1)
    hg = np.stack([np.tile(np.asarray(inp["a_q_norm"][0], f32), 2), np.tile(np.asarray(inp["a_k_norm"][0], f32), 2),
                   np.asarray(inp["a_subln"][0], f32), np.asarray(inp["b_q_norm"][0], f32), np.asarray(inp["b_k_norm"][0], f32),
                   np.asarray(inp["m_q_norm"][0], f32), np.asarray(inp["m_k_norm"][0], f32),
                   np.asarray(inp["m_q_norm"][1], f32), np.asarray(inp["m_k_norm"][1], f32)], axis=1)
    lamv = np.broadcast_to(np.stack([np.asarray(inp[k][0], f32) for k in
                                     ("a_lambda_q1", "a_lambda_k1", "a_lambda_q2", "a_lambda_k2")])[None], (128, 4, 64))
    sinkb = np.broadcast_to(np.asarray(inp["b_sink"][0], f32)[None], (128, 12))
    shared = {
        "w_in_a": np.ascontiguousarray(inp["w_in_a"][0], f32), "w_in_b": np.ascontiguousarray(inp["w_in_b"][0], f32),
        "w_mem_kv": np.ascontiguousarray(inp["w_mem_kv"], f32), "w_out": np.ascontiguousarray(inp["w_out"], f32),
        "w_up": np.ascontiguousarray(inp["w_up"], f32), "w_down": np.ascontiguousarray(inp["w_down"], f32),
        "gcols": np.ascontiguousarray(gcols), "hg": np.ascontiguousarray(hg), "lamv": np.ascontiguousarray(lamv),
        "sinkb": np.ascontiguousarray(sinkb), "relb": relb, "bkt": bkt, "cmat": cmat, "identf": identf,
    }
    maps = []
    zt = np.zeros((128, DM), f32)
    c15, c31 = rel_bias[15], rel_bias[31]
    for b in range(B):
        xt = x[b].reshape(32, 128, DM)
        for r in range(4):
            order = core_order(r)
            g0t = 8 * r
            xfull = np.concatenate([xt[t] if t is not None else zt for t in order], axis=0)
            win = [g0t - 1 + i for i in range(NT)]
            xown = np.concatenate([xt[t] if 0 <= t < 32 else zt for t in win], axis=0)
            cst = np.zeros((12, 3, NKC), f32)
            for qb in range(3):
                n = 512 if qb < 2 else 256
                qt = [win[i] for i in range(4 * qb, min(4 * qb + 4, NT)) if 0 <= win[i] < 32]
                for kc in range(NKC):
                    Dd = (kc - 1) * 128 - qb * 512
                    near = -128 <= Dd <= (512 if n == 512 else 256)
                    gt = order[kc]
                    if gt is None:
                        cst[:, qb, kc] = NEG
                    elif near:
                        cst[:, qb, kc] = 0.0
                    elif gt > max(qt):
                        cst[:, qb, kc] = c31
                    else:
                        assert gt < min(qt)
                        cst[:, qb, kc] = c15
            edge = np.zeros((128, 2), f32)
            if win[0] < 0:
                edge[:, 0] = NEG
            if win[-1] > 31:
                edge[:, 1] = NEG
            pown = positions[b, g0t * 128:(g0t + 8) * 128]
            ppos = np.stack([pown[ia0], pown[ib0], pown[ia1], pown[ib1]], axis=0)
            m = dict(shared)
            m["ppos"] = np.ascontiguousarray(np.broadcast_to(ppos[None], (33, 4, 1280))).astype(np.int32)
            m.update({"xfull": xfull, "xown": xown, "mem": np.ascontiguousarray(mem[b]),
                      "cst0": np.ascontiguousarray(np.broadcast_to(cst[None], (128, 12, 3, NKC))), "edge": edge})
            maps.append(m)
    return maps


_CACHE = {}


def kernel(**inputs):
    maps = prep_inputs(inputs)
    if "nc" not in _CACHE:
        _CACHE["nc"] = build_program()[0]
    nc = _CACHE["nc"]
    res = run_bass_kernel_spmd(nc, maps, core_ids=list(range(8)))
    B = 2
    out = np.zeros((B, 4096, DM), np.float32)
    for b in range(B):
        for r in range(4):
            out[b, r * 1024:(r + 1) * 1024, :] = res.results[b * 4 + r]["y"]
    return out
```

```python
import numpy as np
import concourse.bass as bass
import concourse.mybir as mybir
from concourse.bass_utils import run_bass_kernel_spmd

F32 = mybir.dt.float32
BF16 = mybir.dt.bfloat16
I32 = mybir.dt.int32
AF = mybir.ActivationFunctionType
ALU = mybir.AluOpType
AX = mybir.AxisListType


class Buf:
    __slots__ = ("name", "w", "r")

    def __init__(self, name):
        self.name = name
        self.w = None
        self.r = {}


class Chan:
    __slots__ = ("sem", "key", "total")

    def __init__(self, sem, key):
        self.sem = sem
        self.key = key
        self.total = 0


class Prog:
    ENGS = ("pe", "act", "dve", "pool", "sp")
    NCH = 12

    def __init__(self, nc):
        self.nc = nc
        self.e = {"pe": nc.tensor, "act": nc.scalar, "dve": nc.vector, "pool": nc.gpsimd, "sp": nc.sync}
        self.semobj = {}
        self.cnt = {}
        self.seen = {k: {} for k in self.ENGS}
        for k in self.ENGS:
            self.semobj[k] = nc.alloc_semaphore("s_" + k)
            self.cnt[k] = 0
        self.chans = {}
        self.rr = {}
        for q in ("sp", "pool", "act"):
            n = self.NCH if q != "act" else 4
            self.chans[q] = []
            for i in range(n):
                key = "c_%s%d" % (q, i)
                self.semobj[key] = nc.alloc_semaphore(key)
                self.chans[q].append(Chan(self.semobj[key], key))
            self.rr[q] = 0
        self.nwaits = 0
        self.nins = {k: 0 for k in self.ENGS}

    def buf(self, name):
        return Buf(name)

    def bufs(self, name, n):
        return [Buf("%s%d" % (name, i)) for i in range(n)]

    def _wait(self, eng, tok):
        key, val = tok
        if key == "pe" and eng == "pe":
            return
        if key == "pe":
            assert val <= self.cnt["pe"], "waiting on a pending (un-incremented) PE instruction"
        if self.seen[eng].get(key, 0) >= val:
            return
        self.e[eng].wait_ge(self.semobj[key], val)
        self.seen[eng][key] = val
        self.nwaits += 1

    def _deps(self, eng, reads, writes):
        toks = {}
        for b in reads:
            if b.w is not None:
                k, v = b.w
                toks[k] = max(toks.get(k, 0), v)
        for b in writes:
            if b.w is not None:
                k, v = b.w
                toks[k] = max(toks.get(k, 0), v)
            for k, v in b.r.items():
                toks[k] = max(toks.get(k, 0), v)
        for k, v in toks.items():
            self._wait(eng, (k, v))

    def _mark(self, tok, reads, writes):
        k, v = tok
        for b in reads:
            b.r[k] = max(b.r.get(k, 0), v)
        for b in writes:
            b.w = tok
            b.r = {}

    def op(self, eng, fn, reads=(), writes=(), inc=True, self_sync=True):
        if not self_sync:
            saved = self.seen[eng].get(eng, 0)
            self.seen[eng][eng] = 1 << 60
            self._deps(eng, reads, writes)
            self.seen[eng][eng] = saved
        else:
            self._deps(eng, reads, writes)
        ins = fn(self.e[eng])
        self.nins[eng] += 1
        if inc:
            self.cnt[eng] += 1
            ins.then_inc(self.semobj[eng], 1)
            tok = (eng, self.cnt[eng])
        else:
            assert eng == "pe"
            tok = (eng, self.cnt[eng] + 1)
        self._mark(tok, reads, writes)
        return ins

    def dma(self, q, out, in_, reads=(), writes=(), **kw):
        chs = self.chans[q]
        ch = chs[self.rr[q] % len(chs)]
        self.rr[q] += 1
        if ch.total > 0:
            self._wait(q, (ch.key, ch.total))
        self._deps(q, reads, writes)
        ins = self.e[q].dma_start(out=out, in_=in_, **kw)
        self.nins[q] += 1
        ch.total += 16
        ins.then_inc(ch.sem, 16)
        tok = (ch.key, ch.total)
        self._mark(tok, reads, writes)
        return tok

    def finish(self, out_bufs):
        for b in out_bufs:
            if b.w is not None:
                self._wait("sp", b.w)
        for k in self.ENGS:
            if k != "sp" and self.cnt[k] > 0:
                self._wait("sp", (k, self.cnt[k]))
        for q in self.chans:
            for ch in self.chans[q]:
                if ch.total > 0:
                    self._wait("sp", (ch.key, ch.total))


DM = 2048
KCH = 16
NT = 10
NTOK = NT * 128
NKC = 34
DFF = 8192
GFF = 256
EPS = 1e-6
NEG = -30000.0
S0 = 0.125
S1 = 128.0 ** -0.5
TOKBLKS = [(0, 512), (512, 512), (1024, 256)]


def build_program(stop_after=None, debug=()):
    nc = bass.Bass("TRN2", target_bir_lowering=False)
    p = Prog(nc)
    _uid = [0]

    def _un(name):
        _uid[0] += 1
        return "sb%d_%s" % (_uid[0], name)

    def A(name, shape, dt):
        return nc.alloc_sbuf_tensor(_un(name), shape, dt)

    def SB(name, shape, dt):
        return nc.sbuf_tensor(_un(name), shape, dt)

    def din(name, shape, dt=F32):
        return nc.dram_tensor(name, list(shape), dt, kind="ExternalInput").ap()

    xfull = din("xfull", [NKC * 128, DM])
    xown = din("xown", [NTOK, DM])
    memd = din("mem", [256, DM])
    w_in_a = din("w_in_a", [DM, 5120])
    w_in_b = din("w_in_b", [DM, 3072])
    w_mem_kv = din("w_mem_kv", [2, DM, 1024])
    w_out = din("w_out", [2, DM, DM])
    w_up = din("w_up", [2, DM, DFF])
    w_down = din("w_down", [2, DFF, DM])
    gcols_d = din("gcols", [128, 6, 16])
    hg_d = din("hg", [128, 9])
    lamv_d = din("lamv", [128, 4, 64])
    cst0_d = din("cst0", [128, 12, 3, NKC])
    edge_d = din("edge", [128, 2])
    sink_d = din("sinkb", [128, 12])
    relb_d = din("relb", [33, 12])
    ppos_d = din("ppos", [33, 4, 1280], I32)
    bkt_d = din("bkt", [33, 4])
    cmat_d = din("cmat", [128, 5, 128])
    identf_d = din("identf", [128, 128])
    y = nc.dram_tensor("y", [8 * 128, DM], F32, kind="ExternalOutput").ap()

    kvkind = "ExternalOutput" if "KV" in debug else "Internal"
    KT_d = nc.dram_tensor("KT_d", [12, 128, NKC * 128], BF16, kind=kvkind).ap()
    V_d = nc.dram_tensor("V_d", [12, 128, NKC, 128], BF16, kind=kvkind).ap()
    F0_d = nc.dram_tensor("F0_d", [12, 1280], BF16)
    F1_d = nc.dram_tensor("F1_d", [12, 512], BF16)
    KT_db = p.bufs("KT_d", 12)
    V_db = p.buf("V_d")
    F0_db = p.buf("F0_d")
    F1_db = p.buf("F1_d")
    yb = p.buf("y")
    dbg_out = {}

    def dbg(name, shape, dt=F32):
        t = nc.dram_tensor("dbg_" + name, list(shape), dt, kind="ExternalOutput").ap()
        dbg_out[name] = t
        return t

    gcols = A("gcols", [128, 6, 16], F32); gcolsb = p.buf("gcols")
    hg = A("hg", [128, 9], F32); hgb = p.buf("hg")
    cm = A("cm", [128, 5, 128], BF16); cmb = p.buf("cm")
    identf = A("identf", [128, 128], F32); identfb = p.buf("identf")
    sc = A("sc", [128, 16], F32); scb = p.buf("sc")
    edge = A("edge", [128, 2], F32); edgeb = p.buf("edge")
    esink = A("esink", [128, 12], F32); esinkb = p.buf("esink")
    ssq = A("ssq", [128, 8], F32); ssqb = p.bufs("ssq", 4)
    junk = A("junk", [128, DM], BF16); junkb = p.buf("junk")
    IDB, JM, OD64, O128, ONES = 0, 1, 2, 3, 4

    psd = [nc.alloc_psum_tensor("psd%d" % i, [128, 2, 512], F32) for i in range(4)]
    ps = [psd[i // 2][:, i % 2, :] for i in range(8)]
    psb = p.bufs("ps", 8)
    onesf = A("onesf", [128, 128], F32); onesfb = p.buf("onesf")
    p.op("dve", lambda e: e.memset(onesf[:], 1.0), writes=[onesfb])

    p.dma("sp", gcols[:], gcols_d, writes=[gcolsb])
    p.dma("sp", hg[:], hg_d, writes=[hgb])
    p.dma("sp", identf[:], identf_d, writes=[identfb])
    p.dma("sp", edge[:], edge_d, writes=[edgeb])
    p.dma("sp", esink[:], sink_d, writes=[esinkb])
    p.dma("pool", cm[:], cmat_d, writes=[cmb])
    p.op("act", lambda e: e.activation(out=esink[:], in_=esink[:], func=AF.Exp), reads=[esinkb], writes=[esinkb])

    state = {"ss": 0, "tp": 0}

    def barrier():
        for e in Prog.ENGS:
            for k in Prog.ENGS:
                if k != e and p.cnt[k] > 0:
                    p._wait(e, (k, p.cnt[k]))
            for q in p.chans:
                for ch in p.chans[q]:
                    if ch.total > 0:
                        p._wait(e, (ch.key, ch.total))

    def norm_tile(x_ap, x_buf, gi, dst, dst_buf, col0, xn, xnb, tpbanks):
        s = state["ss"] % 4
        state["ss"] += 1
        ssa = ssq[:, 2 * s:2 * s + 1]
        rsa = ssq[:, 2 * s + 1:2 * s + 2]
        p.op("act", lambda e: e.activation(out=junk[:], in_=x_ap, func=AF.Square, accum_out=ssa),
             reads=[x_buf], writes=[junkb, ssqb[s]])
        p.op("act", lambda e: e.activation(out=rsa, in_=ssa, func=AF.Ln, scale=1.0 / DM, bias=EPS),
             reads=[ssqb[s]], writes=[ssqb[s]])
        p.op("act", lambda e: e.activation(out=rsa, in_=rsa, func=AF.Exp, scale=-0.5),
             reads=[ssqb[s]], writes=[ssqb[s]])
        p.op("dve", lambda e: e.tensor_scalar(out=xn[:], in0=x_ap, scalar1=rsa, scalar2=None, op0=ALU.mult),
             reads=[x_buf, ssqb[s]], writes=[xnb])
        for kb in range(4):
            bi = tpbanks[state["tp"] % len(tpbanks)]
            state["tp"] += 1
            for j in range(4):
                k = kb * 4 + j
                p.op("pe", lambda e: e.transpose(ps[bi].bitcast(BF16)[:, j * 128:(j + 1) * 128], xn[:, k * 128:(k + 1) * 128], cm[:, IDB, :]),
                     reads=[xnb, cmb], writes=[psb[bi]], inc=(j == 3))
            p.op("dve", lambda e: e.tensor_tensor(
                out=dst[:, kb * 4:(kb + 1) * 4, col0:col0 + 128],
                in0=ps[bi].bitcast(BF16)[:, 0:512].rearrange("p (a b) -> p a b", a=4),
                in1=gcols[:, gi, kb * 4:(kb + 1) * 4].unsqueeze(2).to_broadcast([128, 4, 128]),
                op=ALU.mult), reads=[psb[bi], gcolsb], writes=[dst_buf])

    def proj_fm(w, wbuf, j, src, src_buf, c0, n, nm, gcol, dst_ap, dst_buf, tmp, banks, post=None):
        pj = (2, 3, 7)[state.get("pj", 0) % 3]
        state["pj"] = state.get("pj", 0) + 1
        msb = 4
        for k in range(KCH):
            p.op("pe", lambda e: e.matmul(ps[pj][:, :n], lhsT=w[:, k, j * 128:(j + 1) * 128], rhs=src[:, k, c0:c0 + n],
                                          start=(k == 0), stop=(k == KCH - 1)),
                 reads=[wbuf, src_buf], writes=[psb[pj]], inc=(k == KCH - 1))
        sq, sqb, ln, lnb = tmp
        p.op("act", lambda e: e.activation(out=sq[:, :n], in_=ps[pj][:, :n], func=AF.Square), reads=[psb[pj]], writes=[sqb])

        def tail():
            p.op("pe", lambda e: e.matmul(ps[msb][:, :n], lhsT=cm[:, nm, :], rhs=sq[:, :n], start=True, stop=True),
                 reads=[sqb, cmb], writes=[psb[msb]])
            p.op("act", lambda e: e.activation(out=ln[:, :n], in_=ps[msb][:, :n], func=AF.Ln, bias=EPS), reads=[psb[msb]], writes=[lnb])
            p.op("act", lambda e: e.activation(out=ln[:, :n], in_=ln[:, :n], func=AF.Exp, scale=-0.5), reads=[lnb], writes=[lnb])
            p.op("dve", lambda e: e.scalar_tensor_tensor(out=dst_ap, in0=ps[pj][:, :n], scalar=gcol, in1=ln[:, :n],
                                                         op0=ALU.mult, op1=ALU.mult),
                 reads=[psb[pj], lnb, hgb], writes=[dst_buf])
            if post is not None:
                post()

        proj_flush()
        state["tail"] = tail

    def proj_flush():
        t = state.pop("tail", None)
        if t is not None:
            t()

    def wpiece(dst, dst_buf, src2d):
        p.dma("pool", dst, src2d.rearrange("(k p) n -> p k n", p=128), writes=[dst_buf])

    with SB("lamv", [128, 4, 64], F32) as lamv, \
            SB("relb", [33, 12], F32) as relb, \
            SB("relb8", [33, 12], F32) as relb8, \
            SB("oh0", [32, 1280], F32) as oh0, \
            SB("oh1", [33, 512], F32) as oh1, \
            SB("fst", [12, 1280], BF16) as fst:
        lamvb, relbb, relb8b, oh0b, oh1b, fstb = [p.buf(n) for n in ("lamv", "relb", "relb8", "oh0", "oh1", "fst")]
        p.dma("sp", lamv[:], lamv_d, writes=[lamvb])
        p.dma("sp", relb[:], relb_d, writes=[relbb])
        with SB("ppos", [33, 4, 1280], I32) as ppos, SB("bkt", [33, 4], F32) as bkt, SB("urel", [33, 1280], F32) as urel:
            pposb, bktb, urelb = p.buf("ppos"), p.buf("bkt"), p.buf("urel")
            p.dma("sp", ppos[:], ppos_d, writes=[pposb])
            p.dma("sp", bkt[:], bkt_d, writes=[bktb])
            for (oh, ohb, kk, n, ia, c0) in ((oh0, oh0b, 32, 1280, 0, 0), (oh1, oh1b, 33, 512, 2, 2)):
                p.op("dve", lambda e: e.tensor_tensor(out=urel[:kk, :n], in0=ppos[:kk, ia, :n], in1=ppos[:kk, ia + 1, :n], op=ALU.subtract),
                     reads=[pposb], writes=[urelb])
                p.op("dve", lambda e: e.tensor_scalar(out=oh[:kk, :n], in0=urel[:kk, :n], scalar1=bkt[:kk, c0:c0 + 1], scalar2=None, op0=ALU.is_ge),
                     reads=[urelb, bktb], writes=[ohb])
                p.op("dve", lambda e: e.scalar_tensor_tensor(out=oh[:kk, :n], in0=urel[:kk, :n], scalar=bkt[:kk, c0 + 1:c0 + 2], in1=oh[:kk, :n],
                                                             op0=ALU.is_lt, op1=ALU.mult),
                     reads=[urelb, bktb, ohb], writes=[ohb])
            p.op("dve", lambda e: e.tensor_scalar(out=oh1[32:33, :], in0=oh1[32:33, :], scalar1=-1.0, scalar2=1.0, op0=ALU.mult, op1=ALU.add),
                 reads=[oh1b], writes=[oh1b])
            barrier()
        p.op("dve", lambda e: e.tensor_tensor(out=lamv[:, 0, :], in0=lamv[:, 0, :], in1=lamv[:, 1, :], op=ALU.mult),
             reads=[lamvb], writes=[lamvb])
        p.op("dve", lambda e: e.tensor_tensor(out=lamv[:, 2, :], in0=lamv[:, 2, :], in1=lamv[:, 3, :], op=ALU.mult),
             reads=[lamvb], writes=[lamvb])
        p.op("dve", lambda e: e.reduce_sum(out=sc[:, 2:3], in_=lamv[:, 0, :], axis=AX.X), reads=[lamvb], writes=[scb])
        p.op("dve", lambda e: e.reduce_sum(out=sc[:, 3:4], in_=lamv[:, 2, :], axis=AX.X), reads=[lamvb], writes=[scb])
        p.op("act", lambda e: e.activation(out=sc[:, 2:4], in_=sc[:, 2:4], func=AF.Exp), reads=[scb], writes=[scb])
        p.op("dve", lambda e: e.tensor_tensor(out=sc[:, 0:1], in0=sc[:, 3:4], in1=sc[:, 2:3], op=ALU.subtract),
             reads=[scb], writes=[scb])
        p.op("dve", lambda e: e.tensor_scalar(out=sc[:, 0:1], in0=sc[:, 0:1], scalar1=-0.2, scalar2=None, op0=ALU.add),
             reads=[scb], writes=[scb])
        p.op("dve", lambda e: e.tensor_scalar(out=sc[:, 1:2], in0=hg[:, 2:3], scalar1=0.8, scalar2=None, op0=ALU.mult),
             reads=[scb, hgb], writes=[scb])
        for (scale, oh, ohb, n, Fd, Fdb, kk) in ((1.0 / S0, oh0, oh0b, 1280, F0_d, F0_db, 32), (1.0 / S1, oh1, oh1b, 512, F1_d, F1_db, 33)):
            p.op("act", lambda e: e.mul(out=relb8[:], in_=relb[:], mul=scale), reads=[relbb], writes=[relb8b])
            for c0 in range(0, n, 512):
                nn = min(512, n - c0)
                p.op("pe", lambda e: e.matmul(ps[0][:12, :nn], lhsT=relb8[:kk, :], rhs=oh[:kk, c0:c0 + nn], start=True, stop=True),
                     reads=[relb8b, ohb], writes=[psb[0]])
                p.op("act", lambda e: e.copy(out=fst[:, c0:c0 + nn], in_=ps[0][:12, :nn]), reads=[psb[0]], writes=[fstb])
            p.dma("sp", Fd.ap()[:, :n], fst[:, :n], reads=[fstb], writes=[Fdb])
        barrier()
    if "F0" in debug:
        pass

    def phase_a():
        with SB("wk", [128, KCH, 1536], BF16) as wk, \
                SB("wv", [128, KCH, 1536], BF16) as wv, \
                SB("xt", [128, 2, DM], F32) as xt, \
                SB("xn", [128, 2, DM], BF16) as xn_, \
                SB("hTc", [128, 2, KCH, 512], BF16) as hTc, \
                SB("kst", [128, 2, 512], BF16) as kst, \
                SB("vst", [128, 2, 12, 4, 128], BF16) as vst, \
                SB("sq", [128, 2, 512], BF16) as sq, \
                SB("ln", [128, 2, 512], F32) as ln:
            wkb, wvb = p.bufs("wk", 3), p.bufs("wv", 3)
            xtb, xnb, hTcb, kstb, vstb, sqb, lnb = (p.bufs("xt", 2), p.bufs("xn", 2), p.bufs("hTc", 2), p.bufs("kst", 2),
                                                    p.bufs("vst", 2), p.bufs("sq", 2), p.bufs("ln", 2))
            for i in range(3):
                wpiece(wk[:, :, i * 512:(i + 1) * 512], wkb[i], w_in_a[:, 1536 + i * 512:1536 + (i + 1) * 512])
            for i in range(3):
                wpiece(wv[:, :, i * 512:(i + 1) * 512], wvb[i], w_in_a[:, 3072 + i * 512:3072 + (i + 1) * 512])
            blocks = [(t0, min(4, NKC - t0)) for t0 in range(0, NKC, 4)]
            ti = 0
            hk = 0
            for bi_, (t0, nt) in enumerate(blocks):
                hs = bi_ % 2
                n = nt * 128
                for i in range(nt):
                    s = ti % 2
                    p.dma("sp", xt[:, s, :], xfull[(t0 + i) * 128:(t0 + i + 1) * 128, :], writes=[xtb[s]])
                    norm_tile(xt[:, s, :], xtb[s], 0, hTc[:, hs], hTcb[hs], i * 128, xn_[:, ti % 2, :], xnb[ti % 2], (0, 1))
                    ti += 1
                for h in range(12):
                    s2 = hk % 2
                    hk += 1
                    def store(h=h, s2=s2, t0=t0, n=n):
                        p.dma("pool", KT_d[h, :, t0 * 128:t0 * 128 + n], kst[:, s2, :n], reads=[kstb[s2]], writes=[KT_db[h]])
                    proj_fm(wk, wkb[h // 4], h, hTc[:, hs], hTcb[hs], 0, n, OD64, hg[:, 1:2],
                            kst[:, s2, :n], kstb[s2], (sq[:, s2, :], sqb[s2], ln[:, s2, :], lnb[s2]), (2 + s2, 4 if s2 == 0 else 7), post=store)
                for i in range(nt):
                    for nb in range(3):
                        if i == 0 and nb == 1:
                            proj_flush()
                        bk = 5 + (i * 3 + nb) % 2
                        for k in range(KCH):
                            p.op("pe", lambda e: e.matmul(ps[bk][:, :], lhsT=hTc[:, hs, k, i * 128:(i + 1) * 128],
                                                          rhs=wv[:, k, nb * 512:(nb + 1) * 512], start=(k == 0), stop=(k == KCH - 1)),
                                 reads=[hTcb[hs], wvb[nb]], writes=[psb[bk]], inc=(k == KCH - 1))
                        p.op("act", lambda e: e.copy(out=vst[:, hs, nb * 4:(nb + 1) * 4, i, :],
                                                     in_=ps[bk][:, :].rearrange("p (a b) -> p a b", a=4)),
                             reads=[psb[bk]], writes=[vstb[hs]])
                p.dma("pool", V_d.rearrange("h p c e -> p h c e")[:, :, t0:t0 + nt, :], vst[:, hs, :, :nt, :],
                      reads=[vstb[hs]], writes=[V_db])
            barrier()

    phase_a()
    if stop_after == "A":
        p.finish([])
        return nc, dbg_out

    hT = A("hT", [128, KCH, NTOK], BF16)
    hTb = p.buf("hT")
    X1_d = nc.dram_tensor("X1_d", [NTOK, DM], F32).ap()
    X1_db = p.buf("X1_d")

    def layer(li):
        xsrc = xown if li == 0 else X1_d
        xsrcb = p.buf("xsrc") if li == 0 else X1_db
        w_in = w_in_a if li == 0 else w_in_b
        with SB("QT%d" % li, [128, 16, NTOK], BF16) as QT, \
                SB("KmT%d" % li, [128, 4, 256], BF16) as KmT, \
                SB("Vm%d" % li, [128, 2, 512], BF16) as Vm, \
                SB("KT1%d" % li, [128, 4, NTOK if li == 1 else 2], BF16) as KT1, \
                SB("V1%d" % li, [128, NT, 512 if li == 1 else 2], BF16) as V1:
            QTb, KmTb, Vmb, KT1b, V1b = p.buf("QT"), p.buf("KmT"), p.buf("Vm"), p.buf("KT1"), p.buf("V1")
            with SB("xt", [128, 2, DM], F32) as xt, \
                    SB("xn", [128, 2, DM], BF16) as xn_, \
                    SB("wq", [128, 2, KCH, 512], BF16) as wq, \
                    SB("mnT", [128, KCH, 256], BF16) as mnT, \
                    SB("sq", [128, 2, 512], BF16) as sq, \
                    SB("ln", [128, 2, 512], F32) as ln:
                xtb, xnb, wqb, sqb, lnb = p.bufs("xt", 2), p.bufs("xn", 2), p.bufs("wq", 2), p.bufs("sq", 2), p.bufs("ln", 2)
                mnTb = p.buf("mnT")
                wi = [0]

                def next_w(src2d):
                    s = wi[0] % 2
                    wi[0] += 1
                    wpiece(wq[:, s], wqb[s], src2d)
                    return wq[:, s], wqb[s]

                pieces = []
                if li == 0:
                    for i in range(3):
                        pieces.append(("q", i, w_in[:, i * 512:(i + 1) * 512]))
                    pieces.append(("qm", 0, w_in[:, 4608:5120]))
                else:
                    for i in range(3):
                        pieces.append(("q", i, w_in[:, i * 512:(i + 1) * 512]))
                    pieces.append(("k", 0, w_in[:, 1536:2048]))
                    pieces.append(("v", 0, w_in[:, 2048:2560]))
                    pieces.append(("qm", 0, w_in[:, 2560:3072]))
                pieces.append(("mk", 0, w_mem_kv[li, :, 0:512]))
                pieces.append(("mv", 0, w_mem_kv[li, :, 512:1024]))
                loaded = [next_w(pieces[0][2]), None]
                for t in range(NT):
                    s = t % 2
                    p.dma("sp", xt[:, s, :], xsrc[t * 128:(t + 1) * 128, :], reads=[xsrcb], writes=[xtb[s]])
                    norm_tile(xt[:, s, :], xtb[s], 0 if li == 0 else 3, hT, hTb, t * 128, xn_[:, s, :], xnb[s], (0, 1))
                for t in range(2):
                    s = t % 2
                    p.dma("sp", xt[:, s, :], memd[t * 128:(t + 1) * 128, :], writes=[xtb[s]])
                    norm_tile(xt[:, s, :], xtb[s], 2 if li == 0 else 5, mnT, mnTb, t * 128, xn_[:, s, :], xnb[s], (0, 1))
                cnt = 0
                for pi, (kind, idx, src2d) in enumerate(pieces):
                    w, wb = loaded[pi % 2]
                    if pi + 1 < len(pieces):
                        loaded[(pi + 1) % 2] = next_w(pieces[pi + 1][2])
                    if kind in ("q", "qm", "k", "mk"):
                        for j in range(4):
                            if kind == "q":
                                fc = idx * 4 + j
                                nm, gc = (OD64, hg[:, 0:1]) if li == 0 else (O128, hg[:, 3:4])
                                dst, dstb, src, srcb, blks = QT, QTb, hT, hTb, TOKBLKS
                            elif kind == "qm":
                                fc = 12 + j
                                nm, gc = O128, (hg[:, 5:6] if li == 0 else hg[:, 7:8])
                                dst, dstb, src, srcb, blks = QT, QTb, hT, hTb, TOKBLKS
                            elif kind == "k":
                                fc = j
                                nm, gc = O128, hg[:, 4:5]
                                dst, dstb, src, srcb, blks = KT1, KT1b, hT, hTb, TOKBLKS
                            else:
                                fc = j
                                nm, gc = O128, (hg[:, 6:7] if li == 0 else hg[:, 8:9])
                                dst, dstb, src, srcb, blks = KmT, KmTb, mnT, mnTb, [(0, 256)]
                            for (c0, n) in blks:
                                s2 = cnt % 2
                                cnt += 1
                                proj_fm(w, wb, j, src, srcb, c0, n, nm, gc, dst[:, fc, c0:c0 + n], dstb,
                                        (sq[:, s2, :], sqb[s2], ln[:, s2, :], lnb[s2]), (2 + s2, 4 if s2 == 0 else 7))
                    else:
                        if kind == "v":
                            src, srcb, ntl, dst, dstb = hT, hTb, NT, V1, V1b
                        else:
                            src, srcb, ntl, dst, dstb = mnT, mnTb, 2, Vm, Vmb
                        for i in range(ntl):
                            if i == 1:
                                proj_flush()
                            bk = 5 + i % 2
                            for k in range(KCH):
                                p.op("pe", lambda e: e.matmul(ps[bk][:, :], lhsT=src[:, k, i * 128:(i + 1) * 128],
                                                              rhs=w[:, k, :], start=(k == 0), stop=(k == KCH - 1)),
                                     reads=[srcb, wb], writes=[psb[bk]], inc=(k == KCH - 1))
                            p.op("act", lambda e: e.copy(out=dst[:, i, :], in_=ps[bk][:, :]), reads=[psb[bk]], writes=[dstb])
                proj_flush()
                barrier()
            if stop_after == "B%d" % li:
                return ("B", QT, KmT, Vm, KT1, V1)
            with SB("eT", [128, 4, 512], BF16) as eT, \
                    SB("pp", [128, 6, 512], F32) as pp, \
                    SB("sqb", [128, 512], BF16) as sqh:
                eTb = p.bufs("eT", 4)
                ppb = p.bufs("pp", 6)
                sqhb = p.buf("sqh")
                ring = (0, 1, 2)
                OB0, OB1, ZB0, ZB1, MSB = 6, 7, 4, 5, 3
                ecnt = [0]

                def mem_attn():
                    blk = 0
                    for hm in range(4):
                        for (q0, n) in TOKBLKS:
                            ob, zb, pi = OB0 + blk % 2, ZB0 + blk % 2, 2 + blk % 2
                            blk += 1
                            for mc in range(2):
                                i = ecnt[0]
                                ecnt[0] += 1
                                bk = ring[i % 3]
                                es = i % 4
                                p.op("pe", lambda e: e.matmul(ps[bk][:, :n], lhsT=KmT[:, hm, mc * 128:(mc + 1) * 128],
                                                              rhs=QT[:, 12 + hm, q0:q0 + n], start=True, stop=True),
                                     reads=[KmTb, QTb], writes=[psb[bk]])
                                p.op("act", lambda e: e.activation(out=eT[:, es, :n], in_=ps[bk][:, :n], func=AF.Exp, scale=S1),
                                     reads=[psb[bk]], writes=[eTb[es]])
                                p.op("pe", lambda e: e.matmul(ps[ob][:, :n], lhsT=Vm[:, mc, hm * 128:(hm + 1) * 128], rhs=eT[:, es, :n],
                                                              start=(mc == 0), stop=(mc == 1)),
                                     reads=[Vmb, eTb[es]], writes=[psb[ob]], inc=False)
                                p.op("pe", lambda e: e.matmul(ps[zb][:, :n], lhsT=cm[:, ONES, :], rhs=eT[:, es, :n],
                                                              start=(mc == 0), stop=(mc == 1)),
                                     reads=[cmb, eTb[es]], writes=[psb[zb]])
                            p.op("act", lambda e: e.activation(out=pp[:, pi, :n], in_=ps[zb][:, :n], func=AF.Ln), reads=[psb[zb]], writes=[ppb[pi]])
                            p.op("act", lambda e: e.activation(out=pp[:, pi, :n], in_=pp[:, pi, :n], func=AF.Exp, scale=-1.0), reads=[ppb[pi]], writes=[ppb[pi]])
                            p.op("dve", lambda e: e.tensor_tensor(out=hT[:, 12 + hm, q0:q0 + n], in0=ps[ob][:, :n], in1=pp[:, pi, :n], op=ALU.mult),
                                 reads=[psb[ob], ppb[pi]], writes=[hTb])

                if li == 0:
                    with SB("KTh", [128, 2, NKC * 128], BF16) as KTh, \
                            SB("Vh", [128, 2, NKC, 128], BF16) as Vh, \
                            SB("Tb", [128, 2, 6, 512], BF16) as Tb, \
                            SB("cst0", [128, 12, 3, NKC], F32) as cst0, \
                            SB("eT2", [128, 4, 2, 512], BF16) as eT2, \
                            SB("zacc", [128, 2, 2, 512], F32) as zacc:
                        KThb, Vhb, Tbb = p.bufs("KTh", 2), p.bufs("Vh", 2), p.bufs("Tb", 2)
                        cst0b = p.buf("cst0")
                        eT2b = p.bufs("eT2", 4)
                        zaccb = p.bufs("zacc", 2)
                        p.dma("sp", cst0[:], cst0_d, writes=[cst0b])

                        def load_head(h):
                            s = h % 2
                            p.dma("sp", KTh[:, s, :], KT_d[h], reads=[KT_db[h]], writes=[KThb[s]])
                            p.dma("sp", Vh[:, s], V_d[h], reads=[V_db], writes=[Vhb[s]])
                            p.dma("sp", Tb[:, s], bass.AP(F0_d, h * 1280, [[1, 128], [128, 6], [1, 512]]),
                                  reads=[F0_db], writes=[Tbb[s]])

                        load_head(0)
                        kcnt = 0
                        blkno = 0
                        stages = []

                        def make_post(h, q0, n, zi):
                            def stage1(r):
                                for c in (0, 1):
                                    bk = 2 * r + c
                                    p.op("pe", lambda e: e.matmul(ps[bk][:, :n], lhsT=onesf[:], rhs=zacc[:, zi, c, :n], start=True, stop=True),
                                         reads=[onesfb, zaccb[zi]], writes=[psb[bk]])
                                    p.op("act", lambda e: e.activation(out=pp[:, 2 + c, :n], in_=ps[bk][:, :n], func=AF.Ln), reads=[psb[bk]], writes=[ppb[2 + c]])
                                    p.op("act", lambda e: e.activation(out=pp[:, 2 + c, :n], in_=pp[:, 2 + c, :n], func=AF.Exp, scale=-1.0),
                                         reads=[ppb[2 + c]], writes=[ppb[2 + c]])
                                    p.op("dve", lambda e: e.tensor_tensor(out=pp[:, c, :n], in0=pp[:, c, :n], in1=pp[:, 2 + c, :n], op=ALU.mult),
                                         reads=[ppb[c], ppb[2 + c]], writes=[ppb[c]])
                                p.op("dve", lambda e: e.scalar_tensor_tensor(out=pp[:, 0, :n], in0=pp[:, 1, :n], scalar=sc[:, 0:1], in1=pp[:, 0, :n],
                                                                             op0=ALU.mult, op1=ALU.add),
                                     reads=[ppb[0], ppb[1], scb], writes=[ppb[0]])
                                p.op("dve", lambda e: e.tensor_tensor(out=sqh[:, :n], in0=pp[:, 0, :n], in1=pp[:, 0, :n], op=ALU.mult),
                                     reads=[ppb[0]], writes=[sqhb])

                            def stage2(r):
                                bk = 2 * r
                                p.op("pe", lambda e: e.matmul(ps[bk][:, :n], lhsT=cm[:, O128, :], rhs=sqh[:, :n], start=True, stop=True),
                                     reads=[cmb, sqhb], writes=[psb[bk]])
                                p.op("act", lambda e: e.activation(out=pp[:, 4, :n], in_=ps[bk][:, :n], func=AF.Ln, bias=EPS),
                                     reads=[psb[bk]], writes=[ppb[4]])
                                p.op("act", lambda e: e.activation(out=pp[:, 4, :n], in_=pp[:, 4, :n], func=AF.Exp, scale=-0.5),
                                     reads=[ppb[4]], writes=[ppb[4]])
                                p.op("dve", lambda e: e.scalar_tensor_tensor(out=hT[:, h, q0:q0 + n], in0=pp[:, 0, :n], scalar=sc[:, 1:2], in1=pp[:, 4, :n],
                                                                             op0=ALU.mult, op1=ALU.mult),
                                     reads=[ppb[0], ppb[4], scb], writes=[hTb])
                            return [stage1, stage2]

                        for h in range(12):
                            s = h % 2
                            if h + 1 < 12:
                                load_head(h + 1)
                            for qb, (q0, n) in enumerate(TOKBLKS):
                                pend = []
                                zi = blkno % 2
                                blkno += 1

                                def pv(kc, es):
                                    for c in (0, 1):
                                        p.op("pe", lambda e: e.matmul(ps[OB0 + c][:, :n], lhsT=Vh[:, s, kc, :], rhs=eT2[:, es, c, :n],
                                                                      start=(kc == 0), stop=(kc == NKC - 1)),
                                             reads=[Vhb[s], eT2b[es]], writes=[psb[OB0 + c]], inc=(c == 1))

                                for kc in range(NKC):
                                    if kc in (4, 8) and stages:
                                        stages.pop(0)(kcnt % 3)
                                        kcnt += 1
                                    r = kcnt % 3
                                    es = kcnt % 4
                                    kcnt += 1
                                    Dd = (kc - 1) * 128 - qb * 512
                                    near = (-128 <= Dd <= (512 if n == 512 else 256))
                                    di = (512 - Dd) // 128
                                    for c in (0, 1):
                                        bk = 2 * r + c
                                        p.op("pe", lambda e: e.matmul(ps[bk][:, :n], lhsT=KTh[64 * c:64 * c + 64, s, kc * 128:(kc + 1) * 128],
                                                                      rhs=QT[64 * c:64 * c + 64, h, q0:q0 + n], start=True, stop=(not near)),
                                             reads=[KThb[s], QTb], writes=[psb[bk]], inc=(c == 1 and not near))
                                    if near:
                                        for c in (0, 1):
                                            bk = 2 * r + c
                                            p.op("pe", lambda e: e.matmul(ps[bk][:, :n], lhsT=cm[:, JM, :], rhs=Tb[:, s, di, :n],
                                                                          start=False, stop=True),
                                                 reads=[cmb, Tbb[s]], writes=[psb[bk]], inc=(c == 1))
                                    p.op("act", lambda e: e.activation(out=eT2[:, es, :, :n], in_=psd[r][:, :, :n], func=AF.Exp,
                                                                       scale=S0, bias=cst0[:, h, qb, kc:kc + 1]),
                                         reads=[psb[2 * r], psb[2 * r + 1], cst0b], writes=[eT2b[es]])
                                    if kc == 0:
                                        p.op("dve", lambda e: e.tensor_copy(out=zacc[:, zi, :, :n], in_=eT2[:, es, :, :n]),
                                             reads=[eT2b[es]], writes=[zaccb[zi]])
                                    else:
                                        p.op("dve", lambda e: e.tensor_tensor(out=zacc[:, zi, :, :n], in0=zacc[:, zi, :, :n], in1=eT2[:, es, :, :n], op=ALU.add),
                                             reads=[eT2b[es]], writes=[zaccb[zi]], self_sync=False)
                                    pend.append((kc, es))
                                    if len(pend) > 2:
                                        pv(*pend.pop(0))
                                for a in pend:
                                    pv(*a)
                                assert not stages
                                for c in (0, 1):
                                    p.op("act", lambda e: e.copy(out=pp[:, c, :n], in_=ps[OB0 + c][:, :n]), reads=[psb[OB0 + c]], writes=[ppb[c]])
                                stages = make_post(h, q0, n, zi)
                        while stages:
                            stages.pop(0)(kcnt % 3)
                            kcnt += 1
                        mem_attn()
                        barrier()
                else:
                    with SB("Tb1", [128, 12, 3, 128], BF16) as Tb1:
                        Tb1b = p.buf("Tb1")
                        for h in range(12):
                            p.dma("sp", Tb1[:, h], bass.AP(F1_d, h * 512, [[1, 128], [128, 3], [1, 128]]), reads=[F1_db], writes=[Tb1b])
                        blk1 = 0
                        for t in range(1, 9):
                            for g in range(4):
                                ob, zb, pi = OB0 + blk1 % 2, ZB0 + blk1 % 2, 2 + blk1 % 2
                                blk1 += 1
                                js = (t - 1, t, t + 1)
                                for jj, j in enumerate(js):
                                    i = ecnt[0]
                                    ecnt[0] += 1
                                    bk = ring[i % 3]
                                    es = i % 4
                                    dj = 1 - (j - t)
                                    o3 = ps[bk][:, :384].rearrange("p (a b) -> p a b", a=3)
                                    p.op("pe", lambda e: e.matmul(o3, lhsT=KT1[:, g, j * 128:(j + 1) * 128],
                                                                  rhs=QT[:, 3 * g:3 * g + 3, t * 128:(t + 1) * 128], start=True, stop=False),
                                         reads=[KT1b, QTb], writes=[psb[bk]], inc=False)
                                    p.op("pe", lambda e: e.matmul(o3, lhsT=cm[:, JM, :], rhs=Tb1[:, 3 * g:3 * g + 3, dj, :], start=False, stop=True),
                                         reads=[cmb, Tb1b], writes=[psb[bk]])
                                    if t == 1 and j == 0:
                                        bias = edge[:, 0:1]
                                    elif t == 8 and j == 9:
                                        bias = edge[:, 1:2]
                                    else:
                                        bias = 0.0
                                    p.op("act", lambda e: e.activation(out=eT[:, es, :384], in_=ps[bk][:, :384], func=AF.Exp, scale=S1, bias=bias),
                                         reads=[psb[bk], edgeb], writes=[eTb[es]])
                                    p.op("pe", lambda e: e.matmul(ps[ob][:, :384], lhsT=V1[:, j, g * 128:(g + 1) * 128], rhs=eT[:, es, :384],
                                                                  start=(jj == 0), stop=(jj == 2)),
                                         reads=[V1b, eTb[es]], writes=[psb[ob]], inc=False)
                                    p.op("pe", lambda e: e.matmul(ps[zb][:, :384], lhsT=cm[:, ONES, :], rhs=eT[:, es, :384],
                                                                  start=(jj == 0), stop=(jj == 2)),
                                         reads=[cmb, eTb[es]], writes=[psb[zb]])
                                p.op("dve", lambda e: e.tensor_tensor(out=pp[:, pi, :384].rearrange("p (a b) -> p a b", a=3),
                                                                      in0=ps[zb][:, :384].rearrange("p (a b) -> p a b", a=3),
                                                                      in1=esink[:, 3 * g:3 * g + 3].unsqueeze(2).to_broadcast([128, 3, 128]), op=ALU.add),
                                     reads=[psb[zb], esinkb], writes=[ppb[pi]])
                                p.op("act", lambda e: e.activation(out=pp[:, pi, :384], in_=pp[:, pi, :384], func=AF.Ln), reads=[ppb[pi]], writes=[ppb[pi]])
                                p.op("act", lambda e: e.activation(out=pp[:, pi, :384], in_=pp[:, pi, :384], func=AF.Exp, scale=-1.0), reads=[ppb[pi]], writes=[ppb[pi]])
                                p.op("dve", lambda e: e.tensor_tensor(out=hT[:, 3 * g:3 * g + 3, t * 128:(t + 1) * 128],
                                                                      in0=ps[ob][:, :384].rearrange("p (a b) -> p a b", a=3),
                                                                      in1=pp[:, pi, :384].rearrange("p (a b) -> p a b", a=3), op=ALU.mult),
                                     reads=[psb[ob], ppb[pi]], writes=[hTb])
                        mem_attn()
                        barrier()
        if stop_after == "D%d" % li:
            d = dbg("hT", [128, KCH, NTOK], BF16)
            p.dma("sp", d, hT[:], reads=[hTb], writes=[p.buf("d")])
            return ("D",)
        t_lo, t_hi = (0, NT) if li == 0 else (1, 9)
        mblks = TOKBLKS if li == 0 else [(128, 512), (640, 512)]
        with SB("x%d" % li, [128, NT, DM], F32) as x:
            xb = p.bufs("x", NT)
            xq = [p.bufs("xq%d_" % t, 4) for t in range(NT)]
            with SB("wo", [128, 2, KCH, 512], BF16) as wo:
                wob = p.bufs("wo", 2)
                wpiece(wo[:, 0], wob[0], w_out[li, :, 0:512])
                for t in range(t_lo, t_hi):
                    p.dma("sp", x[:, t, :], xsrc[t * 128:(t + 1) * 128, :], reads=[xsrcb], writes=[xb[t]] + xq[t])
                for nb in range(4):
                    s = nb % 2
                    if nb + 1 < 4:
                        wpiece(wo[:, (nb + 1) % 2], wob[(nb + 1) % 2], w_out[li, :, (nb + 1) * 512:(nb + 2) * 512])
                    for t in range(t_lo, t_hi):
                        bk = t % 4
                        for k in range(KCH):
                            p.op("pe", lambda e: e.matmul(ps[bk][:, :], lhsT=hT[:, k, t * 128:(t + 1) * 128], rhs=wo[:, s, k, :],
                                                          start=(k == 0), stop=(k == KCH - 1)),
                                 reads=[hTb, wob[s]], writes=[psb[bk]], inc=(k == KCH - 1))
                        p.op("dve", lambda e: e.tensor_tensor(out=x[:, t, nb * 512:(nb + 1) * 512], in0=x[:, t, nb * 512:(nb + 1) * 512],
                                                              in1=ps[bk][:, :], op=ALU.add),
                             reads=[psb[bk], xq[t][nb]], writes=[xq[t][nb]])
                barrier()
            if stop_after == "E%d" % li:
                d = dbg("x", [NTOK, DM])
                p.dma("sp", d.rearrange("(t p) d -> p t d", p=128), x[:], reads=xb, writes=[p.buf("d")])
                return ("E",)
            NP = DFF // GFF
            with SB("xn1", [128, DM], BF16) as xn1, \
                    SB("wu", [128, 3, KCH, GFF], BF16) as wu, \
                    SB("wd", [128, 3, GFF // 128, DM], BF16) as wd, \
                    SB("uT", [128, 4, NTOK], BF16) as uT, \
                    SB("rl", [128, 2, 512], F32) as rl:
                xn1b = p.buf("xn1")
                wub, wdb, rlb = p.bufs("wu", 3), p.bufs("wd", 3), p.bufs("rl", 2)
                uTb = p.buf("uT")

                def load_wu(g):
                    if g < NP:
                        p.dma("pool", wu[:, g % 3], w_up[li, :, g * GFF:(g + 1) * GFF].rearrange("(k p) n -> p k n", p=128), writes=[wub[g % 3]])

                def load_wd(g):
                    if g < NP:
                        p.dma("pool", wd[:, g % 3], w_down[li, g * GFF:(g + 1) * GFF, :].rearrange("(k p) n -> p k n", p=128), writes=[wdb[g % 3]])

                for g in range(3):
                    load_wu(g)
                    load_wd(g)
                for t in range(t_lo, t_hi):
                    norm_tile(x[:, t, :], xb[t], 1 if li == 0 else 4, hT, hTb, t * 128, xn1, xn1b, (0, 1))
                for t in range(t_lo, t_hi):
                    for nb in range(4):
                        xq[t][nb].r.update(xb[t].r)
                rc = 0
                for gp in range(NP // 2):
                    for half in range(2):
                        g = 2 * gp + half
                        s = g % 3
                        for fc in range(2):
                            for (c0, n) in mblks:
                                bk = 2 + rc % 2
                                rs = rc % 2
                                rc += 1
                                for k in range(KCH):
                                    p.op("pe", lambda e: e.matmul(ps[bk][:, :n], lhsT=wu[:, s, k, fc * 128:(fc + 1) * 128], rhs=hT[:, k, c0:c0 + n],
                                                                  start=(k == 0), stop=(k == KCH - 1)),
                                         reads=[wub[s], hTb], writes=[psb[bk]], inc=(k == KCH - 1))
                                p.op("act", lambda e: e.activation(out=rl[:, rs, :n], in_=ps[bk][:, :n], func=AF.Relu), reads=[psb[bk]], writes=[rlb[rs]])
                                p.op("act", lambda e: e.activation(out=uT[:, 2 * half + fc, c0:c0 + n], in_=rl[:, rs, :n], func=AF.Square),
                                     reads=[rlb[rs]], writes=[uTb])
                        load_wu(g + 3)
                    for t in range(t_lo, t_hi):
                        for nb in range(4):
                            bk = 4 + (t * 4 + nb) % 4
                            for j in range(4):
                                g = 2 * gp + j // 2
                                p.op("pe", lambda e: e.matmul(ps[bk][:, :], lhsT=uT[:, j, t * 128:(t + 1) * 128],
                                                              rhs=wd[:, g % 3, j % 2, nb * 512:(nb + 1) * 512],
                                                              start=(j == 0), stop=(j == 3)),
                                     reads=[uTb, wdb[g % 3]], writes=[psb[bk]], inc=(j == 3))
                            p.op("dve", lambda e: e.tensor_tensor(out=x[:, t, nb * 512:(nb + 1) * 512], in0=x[:, t, nb * 512:(nb + 1) * 512],
                                                                  in1=ps[bk][:, :], op=ALU.add),
                                 reads=[psb[bk], xq[t][nb]], writes=[xq[t][nb]])
                    load_wd(2 * gp + 3)
                    load_wd(2 * gp + 4)
                if li == 0:
                    p.dma("sp", X1_d.rearrange("(t p) d -> p t d", p=128), x[:], reads=[b for t in range(NT) for b in xq[t]], writes=[X1_db])
                else:
                    p.dma("sp", y.rearrange("(t p) d -> p t d", p=128), x[:, 1:9, :], reads=[b for t in range(1, 9) for b in xq[t]], writes=[yb])
                barrier()
        return None

    for li in range(2):
        r = layer(li)
        if r is not None:
            break
    p.finish([yb])
    return nc, dbg_out


def _t5_bucket_np(rel):
    import math
    rel = np.asarray(rel, dtype=np.int32)
    half, max_exact = 16, 8
    side = np.where(rel > 0, half, 0)
    n = np.abs(rel)
    n_f = np.maximum(n, 1).astype(np.float32)
    large = max_exact + (np.log(n_f / np.float32(max_exact)) / np.float32(math.log(128 / max_exact))
                         * np.float32(half - max_exact)).astype(np.int32)
    large = np.minimum(large, half - 1)
    return (side + np.where(n < max_exact, n, large)).astype(np.int32)


def _bucket_table(us):
    return _t5_bucket_np(us)


def core_order(r):
    g0t = 8 * r
    order = []
    for kc in range(12):
        gt = g0t - 2 + kc
        order.append(gt if 0 <= gt < 32 else None)
    have = set(t for t in order if t is not None)
    for i in range(32):
        t = (g0t + 10 + i) % 32
        if t not in have:
            order.append(t)
            have.add(t)
    while len(order) < NKC:
        order.append(None)
    assert len(order) == NKC and len(have) == 32
    return order


def prep_inputs(inp):
    f32 = np.float32
    x = np.asarray(inp["x"], f32)
    mem = np.asarray(inp["mem"], f32)
    rel_bias = np.asarray(inp["rel_bias"], f32)
    B = x.shape[0]
    uu = np.arange(-2000, 2001)
    bu = _bucket_table(uu)
    bkt = np.zeros((33, 4), f32)
    for bb in range(32):
        sel = uu[bu == bb]
        if len(sel) == 0:
            lo, hi = 1e9, 1e9
        else:
            lo, hi = float(sel.min()), float(sel.max()) + 1.0
            assert len(sel) == int(hi - lo)
            if sel.min() == uu[0]:
                lo = -1e9
            if sel.max() == uu[-1]:
                hi = 1e9
        bkt[bb, 0], bkt[bb, 1] = lo, hi
        bkt[bb, 2], bkt[bb, 3] = max(lo, -128.0), min(hi, 129.0)
        if bkt[bb, 3] < bkt[bb, 2]:
            bkt[bb, 3] = bkt[bb, 2]
    bkt[32] = (-128.0, 129.0, -128.0, 129.0)
    m0 = np.arange(1280)
    ia0, ib0 = np.maximum(639 - m0, 0), np.maximum(m0 - 639, 0)
    m1 = np.minimum(np.arange(1280), 511)
    ia1, ib1 = np.maximum(255 - m1, 0), np.maximum(m1 - 255, 0)
    positions = np.asarray(inp["positions"]).astype(np.int32)
    cmat = np.zeros((128, 5, 128), f32)
    cmat[:, 0, :] = np.eye(128)
    cmat[:, 1, :] = np.eye(128)[::-1]
    cmat[:64, 2, :64] = 1.0 / 64
    cmat[64:, 2, 64:] = 1.0 / 64
    cmat[:, 3, :] = 1.0 / 128
    cmat[:, 4, :] = 1.0
    identf = np.eye(128, dtype=f32)
    relb = np.concatenate([rel_bias, np.full((1, 12), NEG, f32)], axis=0)

    def col16(v):
        return np.ascontiguousarray(np.asarray(v, f32).reshape(16, 128).T)

    gcols = np.stack([col16(inp["norm_attn"][0]), col16(inp["norm_mlp"][0]), col16(inp["norm_mem"][0]),
                      col16(inp["norm_attn"][1]), col16(inp["norm_mlp"][1]), col16(inp["norm_mem"][1])], axis=1)
    hg = np.stack([np.tile(np.asarray(inp["a_q_norm"][0], f32), 2), np.tile(np.asarray(inp["a_k_norm"][0], f32), 2),
                   np.asarray(inp["a_subln"][0], f32), np.asarray(inp["b_q_norm"][0], f32), np.asarray(inp["b_k_norm"][0], f32),
                   np.asarray(inp["m_q_norm"][0], f32), np.asarray(inp["m_k_norm"][0], f32),
                   np.asarray(inp["m_q_norm"][1], f32), np.asarray(inp["m_k_norm"][1], f32)], axis=1)
    lamv = np.broadcast_to(np.stack([np.asarray(inp[k][0], f32) for k in
                                     ("a_lambda_q1", "a_lambda_k1", "a_lambda_q2", "a_lambda_k2")])[None], (128, 4, 64))
    sinkb = np.broadcast_to(np.asarray(inp["b_sink"][0], f32)[None], (128, 12))
    shared = {
        "w_in_a": np.ascontiguousarray(inp["w_in_a"][0], f32), "w_in_b": np.ascontiguousarray(inp["w_in_b"][0], f32),
        "w_mem_kv": np.ascontiguousarray(inp["w_mem_kv"], f32), "w_out": np.ascontiguousarray(inp["w_out"], f32),
        "w_up": np.ascontiguousarray(inp["w_up"], f32), "w_down": np.ascontiguousarray(inp["w_down"], f32),
        "gcols": np.ascontiguousarray(gcols), "hg": np.ascontiguousarray(hg), "lamv": np.ascontiguousarray(lamv),
        "sinkb": np.ascontiguousarray(sinkb), "relb": relb, "bkt": bkt, "cmat": cmat, "identf": identf,
    }
    maps = []
    zt = np.zeros((128, DM), f32)
    c15, c31 = rel_bias[15], rel_bias[31]
    for b in range(B):
        xt = x[b].reshape(32, 128, DM)
        for r in range(4):
            order = core_order(r)
            g0t = 8 * r
            xfull = np.concatenate([xt[t] if t is not None else zt for t in order], axis=0)
            win = [g0t - 1 + i for i in range(NT)]
            xown = np.concatenate([xt[t] if 0 <= t < 32 else zt for t in win], axis=0)
            cst = np.zeros((12, 3, NKC), f32)
            for qb in range(3):
                n = 512 if qb < 2 else 256
                qt = [win[i] for i in range(4 * qb, min(4 * qb + 4, NT)) if 0 <= win[i] < 32]
                for kc in range(NKC):
                    Dd = (kc - 1) * 128 - qb * 512
                    near = -128 <= Dd <= (512 if n == 512 else 256)
                    gt = order[kc]
                    if gt is None:
                        cst[:, qb, kc] = NEG
                    elif near:
                        cst[:, qb, kc] = 0.0
                    elif gt > max(qt):
                        cst[:, qb, kc] = c31
                    else:
                        assert gt < min(qt)
                        cst[:, qb, kc] = c15
            edge = np.zeros((128, 2), f32)
            if win[0] < 0:
                edge[:, 0] = NEG
            if win[-1] > 31:
                edge[:, 1] = NEG
            pown = positions[b, g0t * 128:(g0t + 8) * 128]
            ppos = np.stack([pown[ia0], pown[ib0], pown[ia1], pown[ib1]], axis=0)
            m = dict(shared)
            m["ppos"] = np.ascontiguousarray(np.broadcast_to(ppos[None], (33, 4, 1280))).astype(np.int32)
            m.update({"xfull": xfull, "xown": xown, "mem": np.ascontiguousarray(mem[b]),
                      "cst0": np.ascontiguousarray(np.broadcast_to(cst[None], (128, 12, 3, NKC))), "edge": edge})
            maps.append(m)
    return maps


_CACHE = {}


def kernel(**inputs):
    maps = prep_inputs(inputs)
    if "nc" not in _CACHE:
        _CACHE["nc"] = build_program()[0]
    nc = _CACHE["nc"]
    res = run_bass_kernel_spmd(nc, maps, core_ids=list(range(8)))
    B = 2
    out = np.zeros((B, 4096, DM), np.float32)
    for b in range(B):
        for r in range(4):
            out[b, r * 1024:(r + 1) * 1024, :] = res.results[b * 4 + r]["y"]
    return out
```

```python
import numpy as np
import concourse.bass as bass
import concourse.mybir as mybir
from concourse.bass_utils import run_bass_kernel_spmd

F32 = mybir.dt.float32
BF16 = mybir.dt.bfloat16
I32 = mybir.dt.int32
AF = mybir.ActivationFunctionType
ALU = mybir.AluOpType
AX = mybir.AxisListType


class Buf:
    __slots__ = ("name", "w", "r")

    def __init__(self, name):
        self.name = name
        self.w = None
        self.r = {}


class Chan:
    __slots__ = ("sem", "key", "total")

    def __init__(self, sem, key):
        self.sem = sem
        self.key = key
        self.total = 0


class Prog:
    ENGS = ("pe", "act", "dve", "pool", "sp")
    NCH = 12

    def __init__(self, nc):
        self.nc = nc
        self.e = {"pe": nc.tensor, "act": nc.scalar, "dve": nc.vector, "pool": nc.gpsimd, "sp": nc.sync}
        self.semobj = {}
        self.cnt = {}
        self.seen = {k: {} for k in self.ENGS}
        for k in self.ENGS:
            self.semobj[k] = nc.alloc_semaphore("s_" + k)
            self.cnt[k] = 0
        self.chans = {}
        self.rr = {}
        for q in ("sp", "pool", "act"):
            n = self.NCH if q != "act" else 4
            self.chans[q] = []
            for i in range(n):
                key = "c_%s%d" % (q, i)
                self.semobj[key] = nc.alloc_semaphore(key)
                self.chans[q].append(Chan(self.semobj[key], key))
            self.rr[q] = 0
        self.nwaits = 0
        self.nins = {k: 0 for k in self.ENGS}

    def buf(self, name):
        return Buf(name)

    def bufs(self, name, n):
        return [Buf("%s%d" % (name, i)) for i in range(n)]

    def _wait(self, eng, tok):
        key, val = tok
        if key == "pe" and eng == "pe":
            return
        if key == "pe":
            assert val <= self.cnt["pe"], "waiting on a pending (un-incremented) PE instruction"
        if self.seen[eng].get(key, 0) >= val:
            return
        self.e[eng].wait_ge(self.semobj[key], val)
        self.seen[eng][key] = val
        self.nwaits += 1

    def _deps(self, eng, reads, writes):
        toks = {}
        for b in reads:
            if b.w is not None:
                k, v = b.w
                toks[k] = max(toks.get(k, 0), v)
        for b in writes:
            if b.w is not None:
                k, v = b.w
                toks[k] = max(toks.get(k, 0), v)
            for k, v in b.r.items():
                toks[k] = max(toks.get(k, 0), v)
        for k, v in toks.items():
            self._wait(eng, (k, v))

    def _mark(self, tok, reads, writes):
        k, v = tok
        for b in reads:
            b.r[k] = max(b.r.get(k, 0), v)
        for b in writes:
            b.w = tok
            b.r = {}

    def op(self, eng, fn, reads=(), writes=(), inc=True, self_sync=True):
        if not self_sync:
            saved = self.seen[eng].get(eng, 0)
            self.seen[eng][eng] = 1 << 60
            self._deps(eng, reads, writes)
            self.seen[eng][eng] = saved
        else:
            self._deps(eng, reads, writes)
        ins = fn(self.e[eng])
        self.nins[eng] += 1
        if inc:
            self.cnt[eng] += 1
            ins.then_inc(self.semobj[eng], 1)
            tok = (eng, self.cnt[eng])
        else:
            assert eng == "pe"
            tok = (eng, self.cnt[eng] + 1)
        self._mark(tok, reads, writes)
        return ins

    def dma(self, q, out, in_, reads=(), writes=(), **kw):
        chs = self.chans[q]
        ch = chs[self.rr[q] % len(chs)]
        self.rr[q] += 1
        if ch.total > 0:
            self._wait(q, (ch.key, ch.total))
        self._deps(q, reads, writes)
        ins = self.e[q].dma_start(out=out, in_=in_, **kw)
        self.nins[q] += 1
        ch.total += 16
        ins.then_inc(ch.sem, 16)
        tok = (ch.key, ch.total)
        self._mark(tok, reads, writes)
        return tok

    def finish(self, out_bufs):
        for b in out_bufs:
            if b.w is not None:
                self._wait("sp", b.w)
        for k in self.ENGS:
            if k != "sp" and self.cnt[k] > 0:
                self._wait("sp", (k, self.cnt[k]))
        for q in self.chans:
            for ch in self.chans[q]:
                if ch.total > 0:
                    self._wait("sp", (ch.key, ch.total))


DM = 2048
KCH = 16
NT = 10
NTOK = NT * 128
NKC = 34
DFF = 8192
GFF = 256
EPS = 1e-6
NEG = -30000.0
S0 = 0.125
S1 = 128.0 ** -0.5
TOKBLKS = [(0, 512), (512, 512), (1024, 256)]


def build_program(stop_after=None, debug=()):
    nc = bass.Bass("TRN2", target_bir_lowering=False)
    p = Prog(nc)
    _uid = [0]

    def _un(name):
        _uid[0] += 1
        return "sb%d_%s" % (_uid[0], name)

    def A(name, shape, dt):
        return nc.alloc_sbuf_tensor(_un(name), shape, dt)

    def SB(name, shape, dt):
        return nc.sbuf_tensor(_un(name), shape, dt)

    def din(name, shape, dt=F32):
        return nc.dram_tensor(name, list(shape), dt, kind="ExternalInput").ap()

    xfull = din("xfull", [NKC * 128, DM])
    xown = din("xown", [NTOK, DM])
    memd = din("mem", [256, DM])
    w_in_a = din("w_in_a", [DM, 5120])
    w_in_b = din("w_in_b", [DM, 3072])
    w_mem_kv = din("w_mem_kv", [2, DM, 1024])
    w_out = din("w_out", [2, DM, DM])
    w_up = din("w_up", [2, DM, DFF])
    w_down = din("w_down", [2, DFF, DM])
    gcols_d = din("gcols", [128, 6, 16])
    hg_d = din("hg", [128, 9])
    lamv_d = din("lamv", [128, 4, 64])
    cst0_d = din("cst0", [128, 12, 3, NKC])
    edge_d = din("edge", [128, 2])
    sink_d = din("sinkb", [128, 12])
    relb_d = din("relb", [33, 12])
    ppos_d = din("ppos", [33, 4, 1280], I32)
    bkt_d = din("bkt", [33, 4])
    cmat_d = din("cmat", [128, 5, 128])
    identf_d = din("identf", [128, 128])
    y = nc.dram_tensor("y", [8 * 128, DM], F32, kind="ExternalOutput").ap()

    kvkind = "ExternalOutput" if "KV" in debug else "Internal"
    KT_d = nc.dram_tensor("KT_d", [12, 128, NKC * 128], BF16, kind=kvkind).ap()
    V_d = nc.dram_tensor("V_d", [12, 128, NKC, 128], BF16, kind=kvkind).ap()
    F0_d = nc.dram_tensor("F0_d", [12, 1280], BF16)
    F1_d = nc.dram_tensor("F1_d", [12, 512], BF16)
    KT_db = p.bufs("KT_d", 12)
    V_db = p.buf("V_d")
    F0_db = p.buf("F0_d")
    F1_db = p.buf("F1_d")
    yb = p.buf("y")
    dbg_out = {}

    def dbg(name, shape, dt=F32):
        t = nc.dram_tensor("dbg_" + name, list(shape), dt, kind="ExternalOutput").ap()
        dbg_out[name] = t
        return t

    gcols = A("gcols", [128, 6, 16], F32); gcolsb = p.buf("gcols")
    hg = A("hg", [128, 9], F32); hgb = p.buf("hg")
    cm = A("cm", [128, 5, 128], BF16); cmb = p.buf("cm")
    identf = A("identf", [128, 128], F32); identfb = p.buf("identf")
    sc = A("sc", [128, 16], F32); scb = p.buf("sc")
    edge = A("edge", [128, 2], F32); edgeb = p.buf("edge")
    esink = A("esink", [128, 12], F32); esinkb = p.buf("esink")
    ssq = A("ssq", [128, 8], F32); ssqb = p.bufs("ssq", 4)
    junk = A("junk", [128, DM], BF16); junkb = p.buf("junk")
    IDB, JM, OD64, O128, ONES = 0, 1, 2, 3, 4

    psd = [nc.alloc_psum_tensor("psd%d" % i, [128, 2, 512], F32) for i in range(4)]
    ps = [psd[i // 2][:, i % 2, :] for i in range(8)]
    psb = p.bufs("ps", 8)
    onesf = A("onesf", [128, 128], F32); onesfb = p.buf("onesf")
    p.op("dve", lambda e: e.memset(onesf[:], 1.0), writes=[onesfb])

    p.dma("sp", gcols[:], gcols_d, writes=[gcolsb])
    p.dma("sp", hg[:], hg_d, writes=[hgb])
    p.dma("sp", identf[:], identf_d, writes=[identfb])
    p.dma("sp", edge[:], edge_d, writes=[edgeb])
    p.dma("sp", esink[:], sink_d, writes=[esinkb])
    p.dma("pool", cm[:], cmat_d, writes=[cmb])
    p.op("act", lambda e: e.activation(out=esink[:], in_=esink[:], func=AF.Exp), reads=[esinkb], writes=[esinkb])

    state = {"ss": 0, "tp": 0}

    def barrier():
        for e in Prog.ENGS:
            for k in Prog.ENGS:
                if k != e and p.cnt[k] > 0:
                    p._wait(e, (k, p.cnt[k]))
            for q in p.chans:
                for ch in p.chans[q]:
                    if ch.total > 0:
                        p._wait(e, (ch.key, ch.total))

    def norm_tile(x_ap, x_buf, gi, dst, dst_buf, col0, xn, xnb, tpbanks):
        s = state["ss"] % 4
        state["ss"] += 1
        ssa = ssq[:, 2 * s:2 * s + 1]
        rsa = ssq[:, 2 * s + 1:2 * s + 2]
        p.op("act", lambda e: e.activation(out=junk[:], in_=x_ap, func=AF.Square, accum_out=ssa),
             reads=[x_buf], writes=[junkb, ssqb[s]])
        p.op("act", lambda e: e.activation(out=rsa, in_=ssa, func=AF.Ln, scale=1.0 / DM, bias=EPS),
             reads=[ssqb[s]], writes=[ssqb[s]])
        p.op("act", lambda e: e.activation(out=rsa, in_=rsa, func=AF.Exp, scale=-0.5),
             reads=[ssqb[s]], writes=[ssqb[s]])
        p.op("dve", lambda e: e.tensor_scalar(out=xn[:], in0=x_ap, scalar1=rsa, scalar2=None, op0=ALU.mult),
             reads=[x_buf, ssqb[s]], writes=[xnb])
        for kb in range(4):
            bi = tpbanks[state["tp"] % len(tpbanks)]
            state["tp"] += 1
            for j in range(4):
                k = kb * 4 + j
                p.op("pe", lambda e: e.transpose(ps[bi].bitcast(BF16)[:, j * 128:(j + 1) * 128], xn[:, k * 128:(k + 1) * 128], cm[:, IDB, :]),
                     reads=[xnb, cmb], writes=[psb[bi]], inc=(j == 3))
            p.op("dve", lambda e: e.tensor_tensor(
                out=dst[:, kb * 4:(kb + 1) * 4, col0:col0 + 128],
                in0=ps[bi].bitcast(BF16)[:, 0:512].rearrange("p (a b) -> p a b", a=4),
                in1=gcols[:, gi, kb * 4:(kb + 1) * 4].unsqueeze(2).to_broadcast([128, 4, 128]),
                op=ALU.mult), reads=[psb[bi], gcolsb], writes=[dst_buf])

    def proj_fm(w, wbuf, j, src, src_buf, c0, n, nm, gcol, dst_ap, dst_buf, tmp, banks, post=None):
        pj = (2, 3, 7)[state.get("pj", 0) % 3]
        state["pj"] = state.get("pj", 0) + 1
        msb = 4
        for k in range(KCH):
            p.op("pe", lambda e: e.matmul(ps[pj][:, :n], lhsT=w[:, k, j * 128:(j + 1) * 128], rhs=src[:, k, c0:c0 + n],
                                          start=(k == 0), stop=(k == KCH - 1)),
                 reads=[wbuf, src_buf], writes=[psb[pj]], inc=(k == KCH - 1))
        sq, sqb, ln, lnb = tmp
        p.op("act", lambda e: e.activation(out=sq[:, :n], in_=ps[pj][:, :n], func=AF.Square), reads=[psb[pj]], writes=[sqb])

        def tail():
            p.op("pe", lambda e: e.matmul(ps[msb][:, :n], lhsT=cm[:, nm, :], rhs=sq[:, :n], start=True, stop=True),
                 reads=[sqb, cmb], writes=[psb[msb]])
            p.op("act", lambda e: e.activation(out=ln[:, :n], in_=ps[msb][:, :n], func=AF.Ln, bias=EPS), reads=[psb[msb]], writes=[lnb])
            p.op("act", lambda e: e.activation(out=ln[:, :n], in_=ln[:, :n], func=AF.Exp, scale=-0.5), reads=[lnb], writes=[lnb])
            p.op("dve", lambda e: e.scalar_tensor_tensor(out=dst_ap, in0=ps[pj][:, :n], scalar=gcol, in1=ln[:, :n],
                                                         op0=ALU.mult, op1=ALU.mult),
                 reads=[psb[pj], lnb, hgb], writes=[dst_buf])
            if post is not None:
                post()

        proj_flush()
        state["tail"] = tail

    def proj_flush():
        t = state.pop("tail", None)
        if t is not None:
            t()

    def wpiece(dst, dst_buf, src2d):
        p.dma("pool", dst, src2d.rearrange("(k p) n -> p k n", p=128), writes=[dst_buf])

    with SB("lamv", [128, 4, 64], F32) as lamv, \
            SB("relb", [33, 12], F32) as relb, \
            SB("relb8", [33, 12], F32) as relb8, \
            SB("oh0", [32, 1280], F32) as oh0, \
            SB("oh1", [33, 512], F32) as oh1, \
            SB("fst", [12, 1280], BF16) as fst:
        lamvb, relbb, relb8b, oh0b, oh1b, fstb = [p.buf(n) for n in ("lamv", "relb", "relb8", "oh0", "oh1", "fst")]
        p.dma("sp", lamv[:], lamv_d, writes=[lamvb])
        p.dma("sp", relb[:], relb_d, writes=[relbb])
        with SB("ppos", [33, 4, 1280], I32) as ppos, SB("bkt", [33, 4], F32) as bkt, SB("urel", [33, 1280], F32) as urel:
            pposb, bktb, urelb = p.buf("ppos"), p.buf("bkt"), p.buf("urel")
            p.dma("sp", ppos[:], ppos_d, writes=[pposb])
            p.dma("sp", bkt[:], bkt_d, writes=[bktb])
            for (oh, ohb, kk, n, ia, c0) in ((oh0, oh0b, 32, 1280, 0, 0), (oh1, oh1b, 33, 512, 2, 2)):
                p.op("dve", lambda e: e.tensor_tensor(out=urel[:kk, :n], in0=ppos[:kk, ia, :n], in1=ppos[:kk, ia + 1, :n], op=ALU.subtract),
                     reads=[pposb], writes=[urelb])
                p.op("dve", lambda e: e.tensor_scalar(out=oh[:kk, :n], in0=urel[:kk, :n], scalar1=bkt[:kk, c0:c0 + 1], scalar2=None, op0=ALU.is_ge),
                     reads=[urelb, bktb], writes=[ohb])
                p.op("dve", lambda e: e.scalar_tensor_tensor(out=oh[:kk, :n], in0=urel[:kk, :n], scalar=bkt[:kk, c0 + 1:c0 + 2], in1=oh[:kk, :n],
                                                             op0=ALU.is_lt, op1=ALU.mult),
                     reads=[urelb, bktb, ohb], writes=[ohb])
            p.op("dve", lambda e: e.tensor_scalar(out=oh1[32:33, :], in0=oh1[32:33, :], scalar1=-1.0, scalar2=1.0, op0=ALU.mult, op1=ALU.add),
                 reads=[oh1b], writes=[oh1b])
            barrier()
        p.op("dve", lambda e: e.tensor_tensor(out=lamv[:, 0, :], in0=lamv[:, 0, :], in1=lamv[:, 1, :], op=ALU.mult),
             reads=[lamvb], writes=[lamvb])
        p.op("dve", lambda e: e.tensor_tensor(out=lamv[:, 2, :], in0=lamv[:, 2, :], in1=lamv[:, 3, :], op=ALU.mult),
             reads=[lamvb], writes=[lamvb])
        p.op("dve", lambda e: e.reduce_sum(out=sc[:, 2:3], in_=lamv[:, 0, :], axis=AX.X), reads=[lamvb], writes=[scb])
        p.op("dve", lambda e: e.reduce_sum(out=sc[:, 3:4], in_=lamv[:, 2, :], axis=AX.X), reads=[lamvb], writes=[scb])
        p.op("act", lambda e: e.activation(out=sc[:, 2:4], in_=sc[:, 2:4], func=AF.Exp), reads=[scb], writes=[scb])
        p.op("dve", lambda e: e.tensor_tensor(out=sc[:, 0:1], in0=sc[:, 3:4], in1=sc[:, 2:3], op=ALU.subtract),
             reads=[scb], writes=[scb])
        p.op("dve", lambda e: e.tensor_scalar(out=sc[:, 0:1], in0=sc[:, 0:1], scalar1=-0.2, scalar2=None, op0=ALU.add),
             reads=[scb], writes=[scb])
        p.op("dve", lambda e: e.tensor_scalar(out=sc[:, 1:2], in0=hg[:, 2:3], scalar1=0.8, scalar2=None, op0=ALU.mult),
             reads=[scb, hgb], writes=[scb])
        for (scale, oh, ohb, n, Fd, Fdb, kk) in ((1.0 / S0, oh0, oh0b, 1280, F0_d, F0_db, 32), (1.0 / S1, oh1, oh1b, 512, F1_d, F1_db, 33)):
            p.op("act", lambda e: e.mul(out=relb8[:], in_=relb[:], mul=scale), reads=[relbb], writes=[relb8b])
            for c0 in range(0, n, 512):
                nn = min(512, n - c0)
                p.op("pe", lambda e: e.matmul(ps[0][:12, :nn], lhsT=relb8[:kk, :], rhs=oh[:kk, c0:c0 + nn], start=True, stop=True),
                     reads=[relb8b, ohb], writes=[psb[0]])
                p.op("act", lambda e: e.copy(out=fst[:, c0:c0 + nn], in_=ps[0][:12, :nn]), reads=[psb[0]], writes=[fstb])
            p.dma("sp", Fd.ap()[:, :n], fst[:, :n], reads=[fstb], writes=[Fdb])
        barrier()
    if "F0" in debug:
        pass

    def phase_a():
        with SB("wk", [128, KCH, 1536], BF16) as wk, \
                SB("wv", [128, KCH, 1536], BF16) as wv, \
                SB("xt", [128, 2, DM], F32) as xt, \
                SB("xn", [128, 2, DM], BF16) as xn_, \
                SB("hTc", [128, 2, KCH, 512], BF16) as hTc, \
                SB("kst", [128, 2, 512], BF16) as kst, \
                SB("vst", [128, 2, 12, 4, 128], BF16) as vst, \
                SB("sq", [128, 2, 512], BF16) as sq, \
                SB("ln", [128, 2, 512], F32) as ln:
            wkb, wvb = p.bufs("wk", 3), p.bufs("wv", 3)
            xtb, xnb, hTcb, kstb, vstb, sqb, lnb = (p.bufs("xt", 2), p.bufs("xn", 2), p.bufs("hTc", 2), p.bufs("kst", 2),
                                                    p.bufs("vst", 2), p.bufs("sq", 2), p.bufs("ln", 2))
            for i in range(3):
                wpiece(wk[:, :, i * 512:(i + 1) * 512], wkb[i], w_in_a[:, 1536 + i * 512:1536 + (i + 1) * 512])
            for i in range(3):
                wpiece(wv[:, :, i * 512:(i + 1) * 512], wvb[i], w_in_a[:, 3072 + i * 512:3072 + (i + 1) * 512])
            blocks = [(t0, min(4, NKC - t0)) for t0 in range(0, NKC, 4)]
            ti = 0
            hk = 0
            for bi_, (t0, nt) in enumerate(blocks):
                hs = bi_ % 2
                n = nt * 128
                for i in range(nt):
                    s = ti % 2
                    p.dma("sp", xt[:, s, :], xfull[(t0 + i) * 128:(t0 + i + 1) * 128, :], writes=[xtb[s]])
                    norm_tile(xt[:, s, :], xtb[s], 0, hTc[:, hs], hTcb[hs], i * 128, xn_[:, ti % 2, :], xnb[ti % 2], (0, 1))
                    ti += 1
                for h in range(12):
                    s2 = hk % 2
                    hk += 1
                    def store(h=h, s2=s2, t0=t0, n=n):
                        p.dma("pool", KT_d[h, :, t0 * 128:t0 * 128 + n], kst[:, s2, :n], reads=[kstb[s2]], writes=[KT_db[h]])
                    proj_fm(wk, wkb[h // 4], h, hTc[:, hs], hTcb[hs], 0, n, OD64, hg[:, 1:2],
                            kst[:, s2, :n], kstb[s2], (sq[:, s2, :], sqb[s2], ln[:, s2, :], lnb[s2]), (2 + s2, 4 if s2 == 0 else 7), post=store)
                for i in range(nt):
                    for nb in range(3):
                        if i == 0 and nb == 1:
                            proj_flush()
                        bk = 5 + (i * 3 + nb) % 2
                        for k in range(KCH):
                            p.op("pe", lambda e: e.matmul(ps[bk][:, :], lhsT=hTc[:, hs, k, i * 128:(i + 1) * 128],
                                                          rhs=wv[:, k, nb * 512:(nb + 1) * 512], start=(k == 0), stop=(k == KCH - 1)),
                                 reads=[hTcb[hs], wvb[nb]], writes=[psb[bk]], inc=(k == KCH - 1))
                        p.op("act", lambda e: e.copy(out=vst[:, hs, nb * 4:(nb + 1) * 4, i, :],
                                                     in_=ps[bk][:, :].rearrange("p (a b) -> p a b", a=4)),
                             reads=[psb[bk]], writes=[vstb[hs]])
                p.dma("pool", V_d.rearrange("h p c e -> p h c e")[:, :, t0:t0 + nt, :], vst[:, hs, :, :nt, :],
                      reads=[vstb[hs]], writes=[V_db])
            barrier()

    phase_a()
    if stop_after == "A":
        p.finish([])
        return nc, dbg_out

    hT = A("hT", [128, KCH, NTOK], BF16)
    hTb = p.buf("hT")
    X1_d = nc.dram_tensor("X1_d", [NTOK, DM], F32).ap()
    X1_db = p.buf("X1_d")

    def layer(li):
        xsrc = xown if li == 0 else X1_d
        xsrcb = p.buf("xsrc") if li == 0 else X1_db
        w_in = w_in_a if li == 0 else w_in_b
        with SB("QT%d" % li, [128, 16, NTOK], BF16) as QT, \
                SB("KmT%d" % li, [128, 4, 256], BF16) as KmT, \
                SB("Vm%d" % li, [128, 2, 512], BF16) as Vm, \
                SB("KT1%d" % li, [128, 4, NTOK if li == 1 else 2], BF16) as KT1, \
                SB("V1%d" % li, [128, NT, 512 if li == 1 else 2], BF16) as V1:
            QTb, KmTb, Vmb, KT1b, V1b = p.buf("QT"), p.buf("KmT"), p.buf("Vm"), p.buf("KT1"), p.buf("V1")
            with SB("xt", [128, 2, DM], F32) as xt, \
                    SB("xn", [128, 2, DM], BF16) as xn_, \
                    SB("wq", [128, 2, KCH, 512], BF16) as wq, \
                    SB("mnT", [128, KCH, 256], BF16) as mnT, \
                    SB("sq", [128, 2, 512], BF16) as sq, \
                    SB("ln", [128, 2, 512], F32) as ln:
                xtb, xnb, wqb, sqb, lnb = p.bufs("xt", 2), p.bufs("xn", 2), p.bufs("wq", 2), p.bufs("sq", 2), p.bufs("ln", 2)
                mnTb = p.buf("mnT")
                wi = [0]

                def next_w(src2d):
                    s = wi[0] % 2
                    wi[0] += 1
                    wpiece(wq[:, s], wqb[s], src2d)
                    return wq[:, s], wqb[s]

                pieces = []
                if li == 0:
                    for i in range(3):
                        pieces.append(("q", i, w_in[:, i * 512:(i + 1) * 512]))
                    pieces.append(("qm", 0, w_in[:, 4608:5120]))
                else:
                    for i in range(3):
                        pieces.append(("q", i, w_in[:, i * 512:(i + 1) * 512]))
                    pieces.append(("k", 0, w_in[:, 1536:2048]))
                    pieces.append(("v", 0, w_in[:, 2048:2560]))
                    pieces.append(("qm", 0, w_in[:, 2560:3072]))
                pieces.append(("mk", 0, w_mem_kv[li, :, 0:512]))
                pieces.append(("mv", 0, w_mem_kv[li, :, 512:1024]))
                loaded = [next_w(pieces[0][2]), None]
                for t in range(NT):
                    s = t % 2
                    p.dma("sp", xt[:, s, :], xsrc[t * 128:(t + 1) * 128, :], reads=[xsrcb], writes=[xtb[s]])
                    norm_tile(xt[:, s, :], xtb[s], 0 if li == 0 else 3, hT, hTb, t * 128, xn_[:, s, :], xnb[s], (0, 1))
                for t in range(2):
                    s = t % 2
                    p.dma("sp", xt[:, s, :], memd[t * 128:(t + 1) * 128, :], writes=[xtb[s]])
                    norm_tile(xt[:, s, :], xtb[s], 2 if li == 0 else 5, mnT, mnTb, t * 128, xn_[:, s, :], xnb[s], (0, 1))
                cnt = 0
                for pi, (kind, idx, src2d) in enumerate(pieces):
                    w, wb = loaded[pi % 2]
                    if pi + 1 < len(pieces):
                        loaded[(pi + 1) % 2] = next_w(pieces[pi + 1][2])
                    if kind in ("q", "qm", "k", "mk"):
                        for j in range(4):
                            if kind == "q":
                                fc = idx * 4 + j
                                nm, gc = (OD64, hg[:, 0:1]) if li == 0 else (O128, hg[:, 3:4])
                                dst, dstb, src, srcb, blks = QT, QTb, hT, hTb, TOKBLKS
                            elif kind == "qm":
                                fc = 12 + j
                                nm, gc = O128, (hg[:, 5:6] if li == 0 else hg[:, 7:8])
                                dst, dstb, src, srcb, blks = QT, QTb, hT, hTb, TOKBLKS
                            elif kind == "k":
                                fc = j
                                nm, gc = O128, hg[:, 4:5]
                                dst, dstb, src, srcb, blks = KT1, KT1b, hT, hTb, TOKBLKS
                            else:
                                fc = j
                                nm, gc = O128, (hg[:, 6:7] if li == 0 else hg[:, 8:9])
                                dst, dstb, src, srcb, blks = KmT, KmTb, mnT, mnTb, [(0, 256)]
                            for (c0, n) in blks:
                                s2 = cnt % 2
                                cnt += 1
                                proj_fm(w, wb, j, src, srcb, c0, n, nm, gc, dst[:, fc, c0:c0 + n], dstb,
                                        (sq[:, s2, :], sqb[s2], ln[:, s2, :], lnb[s2]), (2 + s2, 4 if s2 == 0 else 7))
                    else:
                        if kind == "v":
                            src, srcb, ntl, dst, dstb = hT, hTb, NT, V1, V1b
                        else:
                            src, srcb, ntl, dst, dstb = mnT, mnTb, 2, Vm, Vmb
                        for i in range(ntl):
                            if i == 1:
                                proj_flush()
                            bk = 5 + i % 2
                            for k in range(KCH):
                                p.op("pe", lambda e: e.matmul(ps[bk][:, :], lhsT=src[:, k, i * 128:(i + 1) * 128],
                                                              rhs=w[:, k, :], start=(k == 0), stop=(k == KCH - 1)),
                                     reads=[srcb, wb], writes=[psb[bk]], inc=(k == KCH - 1))
                            p.op("act", lambda e: e.copy(out=dst[:, i, :], in_=ps[bk][:, :]), reads=[psb[bk]], writes=[dstb])
                proj_flush()
                barrier()
            if stop_after == "B%d" % li:
                return ("B", QT, KmT, Vm, KT1, V1)
            with SB("eT", [128, 4, 512], BF16) as eT, \
                    SB("pp", [128, 6, 512], F32) as pp, \
                    SB("sqb", [128, 512], BF16) as sqh:
                eTb = p.bufs("eT", 4)
                ppb = p.bufs("pp", 6)
                sqhb = p.buf("sqh")
                ring = (0, 1, 2)
                OB0, OB1, ZB0, ZB1, MSB = 6, 7, 4, 5, 3
                ecnt = [0]

                def mem_attn():
                    blk = 0
                    for hm in range(4):
                        for (q0, n) in TOKBLKS:
                            ob, zb, pi = OB0 + blk % 2, ZB0 + blk % 2, 2 + blk % 2
                            blk += 1
                            for mc in range(2):
                                i = ecnt[0]
                                ecnt[0] += 1
                                bk = ring[i % 3]
                                es = i % 4
                                p.op("pe", lambda e: e.matmul(ps[bk][:, :n], lhsT=KmT[:, hm, mc * 128:(mc + 1) * 128],
                                                              rhs=QT[:, 12 + hm, q0:q0 + n], start=True, stop=True),
                                     reads=[KmTb, QTb], writes=[psb[bk]])
                                p.op("act", lambda e: e.activation(out=eT[:, es, :n], in_=ps[bk][:, :n], func=AF.Exp, scale=S1),
                                     reads=[psb[bk]], writes=[eTb[es]])
                                p.op("pe", lambda e: e.matmul(ps[ob][:, :n], lhsT=Vm[:, mc, hm * 128:(hm + 1) * 128], rhs=eT[:, es, :n],
                                                              start=(mc == 0), stop=(mc == 1)),
                                     reads=[Vmb, eTb[es]], writes=[psb[ob]], inc=False)
                                p.op("pe", lambda e: e.matmul(ps[zb][:, :n], lhsT=cm[:, ONES, :], rhs=eT[:, es, :n],
                                                              start=(mc == 0), stop=(mc == 1)),
                                     reads=[cmb, eTb[es]], writes=[psb[zb]])
                            p.op("act", lambda e: e.activation(out=pp[:, pi, :n], in_=ps[zb][:, :n], func=AF.Ln), reads=[psb[zb]], writes=[ppb[pi]])
                            p.op("act", lambda e: e.activation(out=pp[:, pi, :n], in_=pp[:, pi, :n], func=AF.Exp, scale=-1.0), reads=[ppb[pi]], writes=[ppb[pi]])
                            p.op("dve", lambda e: e.tensor_tensor(out=hT[:, 12 + hm, q0:q0 + n], in0=ps[ob][:, :n], in1=pp[:, pi, :n], op=ALU.mult),
                                 reads=[psb[ob], ppb[pi]], writes=[hTb])

                if li == 0:
                    with SB("KTh", [128, 2, NKC * 128], BF16) as KTh, \
                            SB("Vh", [128, 2, NKC, 128], BF16) as Vh, \
                            SB("Tb", [128, 2, 6, 512], BF16) as Tb, \
                            SB("cst0", [128, 12, 3, NKC], F32) as cst0, \
                            SB("eT2", [128, 4, 2, 512], BF16) as eT2, \
                            SB("zacc", [128, 2, 2, 512], F32) as zacc:
                        KThb, Vhb, Tbb = p.bufs("KTh", 2), p.bufs("Vh", 2), p.bufs("Tb", 2)
                        cst0b = p.buf("cst0")
                        eT2b = p.bufs("eT2", 4)
                        zaccb = p.bufs("zacc", 2)
                        p.dma("sp", cst0[:], cst0_d, writes=[cst0b])

                        def load_head(h):
                            s = h % 2
                            p.dma("sp", KTh[:, s, :], KT_d[h], reads=[KT_db[h]], writes=[KThb[s]])
                            p.dma("sp", Vh[:, s], V_d[h], reads=[V_db], writes=[Vhb[s]])
                            p.dma("sp", Tb[:, s], bass.AP(F0_d, h * 1280, [[1, 128], [128, 6], [1, 512]]),
                                  reads=[F0_db], writes=[Tbb[s]])

                        load_head(0)
                        kcnt = 0
                        blkno = 0
                        stages = []

                        def make_post(h, q0, n, zi):
                            def stage1(r):
                                for c in (0, 1):
                                    bk = 2 * r + c
                                    p.op("pe", lambda e: e.matmul(ps[bk][:, :n], lhsT=onesf[:], rhs=zacc[:, zi, c, :n], start=True, stop=True),
                                         reads=[onesfb, zaccb[zi]], writes=[psb[bk]])
                                    p.op("act", lambda e: e.activation(out=pp[:, 2 + c, :n], in_=ps[bk][:, :n], func=AF.Ln), reads=[psb[bk]], writes=[ppb[2 + c]])
                                    p.op("act", lambda e: e.activation(out=pp[:, 2 + c, :n], in_=pp[:, 2 + c, :n], func=AF.Exp, scale=-1.0),
                                         reads=[ppb[2 + c]], writes=[ppb[2 + c]])
                                    p.op("dve", lambda e: e.tensor_tensor(out=pp[:, c, :n], in0=pp[:, c, :n], in1=pp[:, 2 + c, :n], op=ALU.mult),
                                         reads=[ppb[c], ppb[2 + c]], writes=[ppb[c]])
                                p.op("dve", lambda e: e.scalar_tensor_tensor(out=pp[:, 0, :n], in0=pp[:, 1, :n], scalar=sc[:, 0:1], in1=pp[:, 0, :n],
                                                                             op0=ALU.mult, op1=ALU.add),
                                     reads=[ppb[0], ppb[1], scb], writes=[ppb[0]])
                                p.op("dve", lambda e: e.tensor_tensor(out=sqh[:, :n], in0=pp[:, 0, :n], in1=pp[:, 0, :n], op=ALU.mult),
                                     reads=[ppb[0]], writes=[sqhb])

                            def stage2(r):
                                bk = 2 * r
                                p.op("pe", lambda e: e.matmul(ps[bk][:, :n], lhsT=cm[:, O128, :], rhs=sqh[:, :n], start=True, stop=True),
                                     reads=[cmb, sqhb], writes=[psb[bk]])
                                p.op("act", lambda e: e.activation(out=pp[:, 4, :n], in_=ps[bk][:, :n], func=AF.Ln, bias=EPS),
                                     reads=[psb[bk]], writes=[ppb[4]])
                                p.op("act", lambda e: e.activation(out=pp[:, 4, :n], in_=pp[:, 4, :n], func=AF.Exp, scale=-0.5),
                                     reads=[ppb[4]], writes=[ppb[4]])
                                p.op("dve", lambda e: e.scalar_tensor_tensor(out=hT[:, h, q0:q0 + n], in0=pp[:, 0, :n], scalar=sc[:, 1:2], in1=pp[:, 4, :n],
                                                                             op0=ALU.mult, op1=ALU.mult),
                                     reads=[ppb[0], ppb[4], scb], writes=[hTb])
                            return [stage1, stage2]

                        for h in range(12):
                            s = h % 2
                            if h + 1 < 12:
                                load_head(h + 1)
                            for qb, (q0, n) in enumerate(TOKBLKS):
                                pend = []
                                zi = blkno % 2
                                blkno += 1

                                def pv(kc, es):
                                    for c in (0, 1):
                                        p.op("pe", lambda e: e.matmul(ps[OB0 + c][:, :n], lhsT=Vh[:, s, kc, :], rhs=eT2[:, es, c, :n],
                                                                      start=(kc == 0), stop=(kc == NKC - 1)),
                                             reads=[Vhb[s], eT2b[es]], writes=[psb[OB0 + c]], inc=(c == 1))

                                for kc in range(NKC):
                                    if kc in (4, 8) and stages:
                                        stages.pop(0)(kcnt % 3)
                                        kcnt += 1
                                    r = kcnt % 3
                                    es = kcnt % 4
                                    kcnt += 1
                                    Dd = (kc - 1) * 128 - qb * 512
                                    near = (-128 <= Dd <= (512 if n == 512 else 256))
                                    di = (512 - Dd) // 128
                                    for c in (0, 1):
                                        bk = 2 * r + c
                                        p.op("pe", lambda e: e.matmul(ps[bk][:, :n], lhsT=KTh[64 * c:64 * c + 64, s, kc * 128:(kc + 1) * 128],
                                                                      rhs=QT[64 * c:64 * c + 64, h, q0:q0 + n], start=True, stop=(not near)),
                                             reads=[KThb[s], QTb], writes=[psb[bk]], inc=(c == 1 and not near))
                                    if near:
                                        for c in (0, 1):
                                            bk = 2 * r + c
                                            p.op("pe", lambda e: e.matmul(ps[bk][:, :n], lhsT=cm[:, JM, :], rhs=Tb[:, s, di, :n],
                                                                          start=False, stop=True),
                                                 reads=[cmb, Tbb[s]], writes=[psb[bk]], inc=(c == 1))
                                    p.op("act", lambda e: e.activation(out=eT2[:, es, :, :n], in_=psd[r][:, :, :n], func=AF.Exp,
                                                                       scale=S0, bias=cst0[:, h, qb, kc:kc + 1]),
                                         reads=[psb[2 * r], psb[2 * r + 1], cst0b], writes=[eT2b[es]])
                                    if kc == 0:
                                        p.op("dve", lambda e: e.tensor_copy(out=zacc[:, zi, :, :n], in_=eT2[:, es, :, :n]),
                                             reads=[eT2b[es]], writes=[zaccb[zi]])
                                    else:
                                        p.op("dve", lambda e: e.tensor_tensor(out=zacc[:, zi, :, :n], in0=zacc[:, zi, :, :n], in1=eT2[:, es, :, :n], op=ALU.add),
                                             reads=[eT2b[es]], writes=[zaccb[zi]], self_sync=False)
                                    pend.append((kc, es))
                                    if len(pend) > 2:
                                        pv(*pend.pop(0))
                                for a in pend:
                                    pv(*a)
                                assert not stages
                                for c in (0, 1):
                                    p.op("act", lambda e: e.copy(out=pp[:, c, :n], in_=ps[OB0 + c][:, :n]), reads=[psb[OB0 + c]], writes=[ppb[c]])
                                stages = make_post(h, q0, n, zi)
                        while stages:
                            stages.pop(0)(kcnt % 3)
                            kcnt += 1
                        mem_attn()
                        barrier()
                else:
                    with SB("Tb1", [128, 12, 3, 128], BF16) as Tb1:
                        Tb1b = p.buf("Tb1")
                        for h in range(12):
                            p.dma("sp", Tb1[:, h], bass.AP(F1_d, h * 512, [[1, 128], [128, 3], [1, 128]]), reads=[F1_db], writes=[Tb1b])
                        blk1 = 0
                        for t in range(1, 9):
                            for g in range(4):
                                ob, zb, pi = OB0 + blk1 % 2, ZB0 + blk1 % 2, 2 + blk1 % 2
                                blk1 += 1
                                js = (t - 1, t, t + 1)
                                for jj, j in enumerate(js):
                                    i = ecnt[0]
                                    ecnt[0] += 1
                                    bk = ring[i % 3]
                                    es = i % 4
                                    dj = 1 - (j - t)
                                    o3 = ps[bk][:, :384].rearrange("p (a b) -> p a b", a=3)
                                    p.op("pe", lambda e: e.matmul(o3, lhsT=KT1[:, g, j * 128:(j + 1) * 128],
                                                                  rhs=QT[:, 3 * g:3 * g + 3, t * 128:(t + 1) * 128], start=True, stop=False),
                                         reads=[KT1b, QTb], writes=[psb[bk]], inc=False)
                                    p.op("pe", lambda e: e.matmul(o3, lhsT=cm[:, JM, :], rhs=Tb1[:, 3 * g:3 * g + 3, dj, :], start=False, stop=True),
                                         reads=[cmb, Tb1b], writes=[psb[bk]])
                                    if t == 1 and j == 0:
                                        bias = edge[:, 0:1]
                                    elif t == 8 and j == 9:
                                        bias = edge[:, 1:2]
                                    else:
                                        bias = 0.0
                                    p.op("act", lambda e: e.activation(out=eT[:, es, :384], in_=ps[bk][:, :384], func=AF.Exp, scale=S1, bias=bias),
                                         reads=[psb[bk], edgeb], writes=[eTb[es]])
                                    p.op("pe", lambda e: e.matmul(ps[ob][:, :384], lhsT=V1[:, j, g * 128:(g + 1) * 128], rhs=eT[:, es, :384],
                                                                  start=(jj == 0), stop=(jj == 2)),
                                         reads=[V1b, eTb[es]], writes=[psb[ob]], inc=False)
                                    p.op("pe", lambda e: e.matmul(ps[zb][:, :384], lhsT=cm[:, ONES, :], rhs=eT[:, es, :384],
                                                                  start=(jj == 0), stop=(jj == 2)),
                                         reads=[cmb, eTb[es]], writes=[psb[zb]])
                                p.op("dve", lambda e: e.tensor_tensor(out=pp[:, pi, :384].rearrange("p (a b) -> p a b", a=3),
                                                                      in0=ps[zb][:, :384].rearrange("p (a b) -> p a b", a=3),
                                                                      in1=esink[:, 3 * g:3 * g + 3].unsqueeze(2).to_broadcast([128, 3, 128]), op=ALU.add),
                                     reads=[psb[zb], esinkb], writes=[ppb[pi]])
                                p.op("act", lambda e: e.activation(out=pp[:, pi, :384], in_=pp[:, pi, :384], func=AF.Ln), reads=[ppb[pi]], writes=[ppb[pi]])
                                p.op("act", lambda e: e.activation(out=pp[:, pi, :384], in_=pp[:, pi, :384], func=AF.Exp, scale=-1.0), reads=[ppb[pi]], writes=[ppb[pi]])
                                p.op("dve", lambda e: e.tensor_tensor(out=hT[:, 3 * g:3 * g + 3, t * 128:(t + 1) * 128],
                                                                      in0=ps[ob][:, :384].rearrange("p (a b) -> p a b", a=3),
                                                                      in1=pp[:, pi, :384].rearrange("p (a b) -> p a b", a=3), op=ALU.mult),
                                     reads=[psb[ob], ppb[pi]], writes=[hTb])
                        mem_attn()
                        barrier()
        if stop_after == "D%d" % li:
            d = dbg("hT", [128, KCH, NTOK], BF16)
            p.dma("sp", d, hT[:], reads=[hTb], writes=[p.buf("d")])
            return ("D",)
        t_lo, t_hi = (0, NT) if li == 0 else (1, 9)
        mblks = TOKBLKS if li == 0 else [(128, 512), (640, 512)]
        with SB("x%d" % li, [128, NT, DM], F32) as x:
            xb = p.bufs("x", NT)
            xq = [p.bufs("xq%d_" % t, 4) for t in range(NT)]
            with SB("wo", [128, 2, KCH, 512], BF16) as wo:
                wob = p.bufs("wo", 2)
                wpiece(wo[:, 0], wob[0], w_out[li, :, 0:512])
                for t in range(t_lo, t_hi):
                    p.dma("sp", x[:, t, :], xsrc[t * 128:(t + 1) * 128, :], reads=[xsrcb], writes=[xb[t]] + xq[t])
                for nb in range(4):
                    s = nb % 2
                    if nb + 1 < 4:
                        wpiece(wo[:, (nb + 1) % 2], wob[(nb + 1) % 2], w_out[li, :, (nb + 1) * 512:(nb + 2) * 512])
                    for t in range(t_lo, t_hi):
                        bk = t % 4
                        for k in range(KCH):
                            p.op("pe", lambda e: e.matmul(ps[bk][:, :], lhsT=hT[:, k, t * 128:(t + 1) * 128], rhs=wo[:, s, k, :],
                                                          start=(k == 0), stop=(k == KCH - 1)),
                                 reads=[hTb, wob[s]], writes=[psb[bk]], inc=(k == KCH - 1))
                        p.op("dve", lambda e: e.tensor_tensor(out=x[:, t, nb * 512:(nb + 1) * 512], in0=x[:, t, nb * 512:(nb + 1) * 512],
                                                              in1=ps[bk][:, :], op=ALU.add),
                             reads=[psb[bk], xq[t][nb]], writes=[xq[t][nb]])
                barrier()
            if stop_after == "E%d" % li:
                d = dbg("x", [NTOK, DM])
                p.dma("sp", d.rearrange("(t p) d -> p t d", p=128), x[:], reads=xb, writes=[p.buf("d")])
                return ("E",)
            NP = DFF // GFF
            with SB("xn1", [128, DM], BF16) as xn1, \
                    SB("wu", [128, 3, KCH, GFF], BF16) as wu, \
                    SB("wd", [128, 3, GFF // 128, DM], BF16) as wd, \
                    SB("uT", [128, 4, NTOK], BF16) as uT, \
                    SB("rl", [128, 2, 512], F32) as rl:
                xn1b = p.buf("xn1")
                wub, wdb, rlb = p.bufs("wu", 3), p.bufs("wd", 3), p.bufs("rl", 2)
                uTb = p.buf("uT")

                def load_wu(g):
                    if g < NP:
                        p.dma("pool", wu[:, g % 3], w_up[li, :, g * GFF:(g + 1) * GFF].rearrange("(k p) n -> p k n", p=128), writes=[wub[g % 3]])

                def load_wd(g):
                    if g < NP:
                        p.dma("pool", wd[:, g % 3], w_down[li, g * GFF:(g + 1) * GFF, :].rearrange("(k p) n -> p k n", p=128), writes=[wdb[g % 3]])

                hTt = p.bufs("hTt", NT)
                for g in range(3):
                    load_wu(g)
                    load_wd(g)
                for t in range(t_lo, t_hi):
                    norm_tile(x[:, t, :], xb[t], 1 if li == 0 else 4, hT, hTt[t], t * 128, xn1, xn1b, (0, 1))
                for t in range(t_lo, t_hi):
                    for nb in range(4):
                        xq[t][nb].r.update(xb[t].r)
                rc = 0
                for gp in range(NP // 2):
                    for half in range(2):
                        g = 2 * gp + half
                        s = g % 3
                        for fc in range(2):
                            for (c0, n) in mblks:
                                bk = 2 + rc % 2
                                rs = rc % 2
                                rc += 1
                                for k in range(KCH):
                                    p.op("pe", lambda e: e.matmul(ps[bk][:, :n], lhsT=wu[:, s, k, fc * 128:(fc + 1) * 128], rhs=hT[:, k, c0:c0 + n],
                                                                  start=(k == 0), stop=(k == KCH - 1)),
                                         reads=[wub[s]] + hTt[c0 // 128:(c0 + n) // 128], writes=[psb[bk]], inc=(k == KCH - 1))
                                p.op("act", lambda e: e.activation(out=rl[:, rs, :n], in_=ps[bk][:, :n], func=AF.Relu), reads=[psb[bk]], writes=[rlb[rs]])
                                p.op("act", lambda e: e.activation(out=uT[:, 2 * half + fc, c0:c0 + n], in_=rl[:, rs, :n], func=AF.Square),
                                     reads=[rlb[rs]], writes=[uTb])
                        load_wu(g + 3)
                    for t in range(t_lo, t_hi):
                        for nb in range(4):
                            bk = 4 + (t * 4 + nb) % 4
                            for j in range(4):
                                g = 2 * gp + j // 2
                                p.op("pe", lambda e: e.matmul(ps[bk][:, :], lhsT=uT[:, j, t * 128:(t + 1) * 128],
                                                              rhs=wd[:, g % 3, j % 2, nb * 512:(nb + 1) * 512],
                                                              start=(j == 0), stop=(j == 3)),
                                     reads=[uTb, wdb[g % 3]], writes=[psb[bk]], inc=(j == 3))
                            p.op("dve", lambda e: e.tensor_tensor(out=x[:, t, nb * 512:(nb + 1) * 512], in0=x[:, t, nb * 512:(nb + 1) * 512],
                                                                  in1=ps[bk][:, :], op=ALU.add),
                                 reads=[psb[bk], xq[t][nb]], writes=[xq[t][nb]])
                    load_wd(2 * gp + 3)
                    load_wd(2 * gp + 4)
                if li == 0:
                    p.dma("sp", X1_d.rearrange("(t p) d -> p t d", p=128), x[:], reads=[b for t in range(NT) for b in xq[t]], writes=[X1_db])
                else:
                    p.dma("sp", y.rearrange("(t p) d -> p t d", p=128), x[:, 1:9, :], reads=[b for t in range(1, 9) for b in xq[t]], writes=[yb])
                barrier()
        return None

    for li in range(2):
        r = layer(li)
        if r is not None:
            break
    p.finish([yb])
    return nc, dbg_out


def _t5_bucket_np(rel):
    import math
    rel = np.asarray(rel, dtype=np.int32)
    half, max_exact = 16, 8
    side = np.where(rel > 0, half, 0)
    n = np.abs(rel)
    n_f = np.maximum(n, 1).astype(np.float32)
    large = max_exact + (np.log(n_f / np.float32(max_exact)) / np.float32(math.log(128 / max_exact))
                         * np.float32(half - max_exact)).astype(np.int32)
    large = np.minimum(large, half - 1)
    return (side + np.where(n < max_exact, n, large)).astype(np.int32)


def _bucket_table(us):
    return _t5_bucket_np(us)


def core_order(r):
    g0t = 8 * r
    order = []
    for kc in range(12):
        gt = g0t - 2 + kc
        order.append(gt if 0 <= gt < 32 else None)
    have = set(t for t in order if t is not None)
    for i in range(32):
        t = (g0t + 10 + i) % 32
        if t not in have:
            order.append(t)
            have.add(t)
    while len(order) < NKC:
        order.append(None)
    assert len(order) == NKC and len(have) == 32
    return order


def prep_inputs(inp):
    f32 = np.float32
    x = np.asarray(inp["x"], f32)
    mem = np.asarray(inp["mem"], f32)
    rel_bias = np.asarray(inp["rel_bias"], f32)
    B = x.shape[0]
    uu = np.arange(-2000, 2001)
    bu = _bucket_table(uu)
    bkt = np.zeros((33, 4), f32)
    for bb in range(32):
        sel = uu[bu == bb]
        if len(sel) == 0:
            lo, hi = 1e9, 1e9
        else:
            lo, hi = float(sel.min()), float(sel.max()) + 1.0
            assert len(sel) == int(hi - lo)
            if sel.min() == uu[0]:
                lo = -1e9
            if sel.max() == uu[-1]:
                hi = 1e9
        bkt[bb, 0], bkt[bb, 1] = lo, hi
        bkt[bb, 2], bkt[bb, 3] = max(lo, -128.0), min(hi, 129.0)
        if bkt[bb, 3] < bkt[bb, 2]:
            bkt[bb, 3] = bkt[bb, 2]
    bkt[32] = (-128.0, 129.0, -128.0, 129.0)
    m0 = np.arange(1280)
    ia0, ib0 = np.maximum(639 - m0, 0), np.maximum(m0 - 639, 0)
    m1 = np.minimum(np.arange(1280), 511)
    ia1, ib1 = np.maximum(255 - m1, 0), np.maximum(m1 - 255, 0)
    positions = np.asarray(inp["positions"]).astype(np.int32)
    cmat = np.zeros((128, 5, 128), f32)
    cmat[:, 0, :] = np.eye(128)
    cmat[:, 1, :] = np.eye(128)[::-1]
    cmat[:64, 2, :64] = 1.0 / 64
    cmat[64:, 2, 64:] = 1.0 / 64
    cmat[:, 3, :] = 1.0 / 128
    cmat[:, 4, :] = 1.0
    identf = np.eye(128, dtype=f32)
    relb = np.concatenate([rel_bias, np.full((1, 12), NEG, f32)], axis=0)

    def col16(v):
        return np.ascontiguousarray(np.asarray(v, f32).reshape(16, 128).T)

    gcols = np.stack([col16(inp["norm_attn"][0]), col16(inp["norm_mlp"][0]), col16(inp["norm_mem"][0]),
                      col16(inp["norm_attn"][1]), col16(inp["norm_mlp"][1]), col16(inp["norm_mem"][1])], axis=1)
    hg = np.stack([np.tile(np.asarray(inp["a_q_norm"][0], f32), 2), np.tile(np.asarray(inp["a_k_norm"][0], f32), 2),
                   np.asarray(inp["a_subln"][0], f32), np.asarray(inp["b_q_norm"][0], f32), np.asarray(inp["b_k_norm"][0], f32),
                   np.asarray(inp["m_q_norm"][0], f32), np.asarray(inp["m_k_norm"][0], f32),
                   np.asarray(inp["m_q_norm"][1], f32), np.asarray(inp["m_k_norm"][1], f32)], axis=1)
    lamv = np.broadcast_to(np.stack([np.asarray(inp[k][0], f32) for k in
                                     ("a_lambda_q1", "a_lambda_k1", "a_lambda_q2", "a_lambda_k2")])[None], (128, 4, 64))
    sinkb = np.broadcast_to(np.asarray(inp["b_sink"][0], f32)[None], (128, 12))
    shared = {
        "w_in_a": np.ascontiguousarray(inp["w_in_a"][0], f32), "w_in_b": np.ascontiguousarray(inp["w_in_b"][0], f32),
        "w_mem_kv": np.ascontiguousarray(inp["w_mem_kv"], f32), "w_out": np.ascontiguousarray(inp["w_out"], f32),
        "w_up": np.ascontiguousarray(inp["w_up"], f32), "w_down": np.ascontiguousarray(inp["w_down"], f32),
        "gcols": np.ascontiguousarray(gcols), "hg": np.ascontiguousarray(hg), "lamv": np.ascontiguousarray(lamv),
        "sinkb": np.ascontiguousarray(sinkb), "relb": relb, "bkt": bkt, "cmat": cmat, "identf": identf,
    }
    maps = []
    zt = np.zeros((128, DM), f32)
    c15, c31 = rel_bias[15], rel_bias[31]
    for b in range(B):
        xt = x[b].reshape(32, 128, DM)
        for r in range(4):
            order = core_order(r)
            g0t = 8 * r
            xfull = np.concatenate([xt[t] if t is not None else zt for t in order], axis=0)
            win = [g0t - 1 + i for i in range(NT)]
            xown = np.concatenate([xt[t] if 0 <= t < 32 else zt for t in win], axis=0)
            cst = np.zeros((12, 3, NKC), f32)
            for qb in range(3):
                n = 512 if qb < 2 else 256
                qt = [win[i] for i in range(4 * qb, min(4 * qb + 4, NT)) if 0 <= win[i] < 32]
                for kc in range(NKC):
                    Dd = (kc - 1) * 128 - qb * 512
                    near = -128 <= Dd <= (512 if n == 512 else 256)
                    gt = order[kc]
                    if gt is None:
                        cst[:, qb, kc] = NEG
                    elif near:
                        cst[:, qb, kc] = 0.0
                    elif gt > max(qt):
                        cst[:, qb, kc] = c31
                    else:
                        assert gt < min(qt)
                        cst[:, qb, kc] = c15
            edge = np.zeros((128, 2), f32)
            if win[0] < 0:
                edge[:, 0] = NEG
            if win[-1] > 31:
                edge[:, 1] = NEG
            pown = positions[b, g0t * 128:(g0t + 8) * 128]
            ppos = np.stack([pown[ia0], pown[ib0], pown[ia1], pown[ib1]], axis=0)
            m = dict(shared)
            m["ppos"] = np.ascontiguousarray(np.broadcast_to(ppos[None], (33, 4, 1280))).astype(np.int32)
            m.update({"xfull": xfull, "xown": xown, "mem": np.ascontiguousarray(mem[b]),
                      "cst0": np.ascontiguousarray(np.broadcast_to(cst[None], (128, 12, 3, NKC))), "edge": edge})
            maps.append(m)
    return maps


_CACHE = {}


def kernel(**inputs):
    maps = prep_inputs(inputs)
    if "nc" not in _CACHE:
        _CACHE["nc"] = build_program()[0]
    nc = _CACHE["nc"]
    res = run_bass_kernel_spmd(nc, maps, core_ids=list(range(8)))
    B = 2
    out = np.zeros((B, 4096, DM), np.float32)
    for b in range(B):
        for r in range(4):
            out[b, r * 1024:(r + 1) * 1024, :] = res.results[b * 4 + r]["y"]
    return out
```
